# Optimizing a Trainium2 kernel written in Bass

```python
import jax, jax.numpy as jnp
from jax import lax
import numpy as np

D_MODEL = 1024
BATCH = 8
SEQ = 2048
DEPTH = 4

N_MEM = 256
N_MIXERS = 2
N_POOL_LAYERS = (DEPTH + 1) // 2
N_MOBA_LAYERS = DEPTH // 2
POOL_GROUPS = 4
POOL_WINDOWS = (2, 4, 8, 16)
POOL_GW = D_MODEL // POOL_GROUPS
MOBA_HEADS = 8
HEAD_DIM = D_MODEL // MOBA_HEADS
ROT_DIM = HEAD_DIM // 4
ROPE_THETA = 500000.0
MOBA_BLOCK = 256
MOBA_TOPK = 3
Q_CHUNK = 16
MEM_HEADS = 4
MEM_HEAD_DIM = D_MODEL // MEM_HEADS
D_FF = 4 * D_MODEL
RMS_EPS = 1e-6
N_NORMS = 6

kernel_name = "pool_moba_memory_hybrid_trunk"


def rms_norm(x, g):
    xf = x.astype(jnp.float32)
    y = xf * lax.rsqrt(jnp.mean(xf * xf, axis=-1, keepdims=True) + RMS_EPS)
    return (y * g.astype(jnp.float32)).astype(x.dtype)


def apply_partial_rope(x, cos, sin):
    half = ROT_DIM // 2
    xr = x[..., :ROT_DIM].astype(jnp.float32)
    x1, x2 = xr[..., :half], xr[..., half:]
    rot = jnp.concatenate([x1 * cos - x2 * sin, x2 * cos + x1 * sin], axis=-1)
    return jnp.concatenate([rot.astype(x.dtype), x[..., ROT_DIM:]], axis=-1)


def pool_mixer(h, w_in, w_group, scale):
    b, s, _ = h.shape
    u = (h @ w_in).reshape(b, s, POOL_GROUPS, POOL_GW)
    uf = u.astype(jnp.float32)
    cs = jnp.concatenate([jnp.zeros((b, 1, POOL_GROUPS, POOL_GW), jnp.float32),
                          jnp.cumsum(uf, axis=1)], axis=1)
    t = jnp.arange(s)
    outs = []
    for g, w in enumerate(POOL_WINDOWS):
        csg = cs[:, :, g]
        cs_lo = jnp.concatenate([jnp.zeros((b, w - 1, POOL_GW), jnp.float32), csg],
                                axis=1)[:, :s]
        win_sum = csg[:, 1:] - cs_lo
        cnt = jnp.minimum(t + 1, w).astype(jnp.float32)[None, :, None]
        outs.append(win_sum / cnt - uf[:, :, g])
    pooled = jnp.stack(outs, axis=2).astype(h.dtype)
    y = jnp.einsum('bsgc,gcd->bsgd', pooled, w_group).reshape(b, s, D_MODEL)
    return y * scale


def moba_attention(h, w_qkv, w_o, cos, sin):
    b, s, _ = h.shape
    qkv = (h @ w_qkv).reshape(b, s, 3, MOBA_HEADS, HEAD_DIM)
    q = jnp.transpose(qkv[:, :, 0], (0, 2, 1, 3))
    k = jnp.transpose(qkv[:, :, 1], (0, 2, 1, 3))
    v = jnp.transpose(qkv[:, :, 2], (0, 2, 1, 3))
    q = apply_partial_rope(q, cos, sin) * (HEAD_DIM ** -0.5)
    k = apply_partial_rope(k, cos, sin)
    n_blk = -(-s // MOBA_BLOCK)
    pad = n_blk * MOBA_BLOCK - s
    kb = jnp.pad(k, ((0, 0), (0, 0), (0, pad), (0, 0))).reshape(
        b, MOBA_HEADS, n_blk, MOBA_BLOCK, HEAD_DIM)
    vb = jnp.pad(v, ((0, 0), (0, 0), (0, pad), (0, 0))).reshape(
        b, MOBA_HEADS, n_blk, MOBA_BLOCK, HEAD_DIM)
    k_mean = jnp.mean(kb.astype(jnp.float32), axis=3)
    k_sel = min(MOBA_TOPK, n_blk)
    bi = jnp.arange(b)[:, None, None, None]
    hi = jnp.arange(MOBA_HEADS)[None, :, None, None]
    blk_ids = jnp.arange(n_blk)
    key_off = jnp.arange(MOBA_BLOCK)

    def chunk(c):
        start = c * Q_CHUNK
        qc = lax.dynamic_slice_in_dim(q, start, Q_CHUNK, axis=2)
        qpos = start + jnp.arange(Q_CHUNK)
        qblk = start // MOBA_BLOCK
        gate = jnp.einsum('bhqd,bhnd->bhqn', qc.astype(jnp.float32), k_mean)
        gate = jnp.where(blk_ids < qblk, gate, -jnp.inf)
        _, sel = lax.top_k(gate, k_sel)
        valid = sel < qblk
        ks = kb[bi, hi, sel]
        vs = vb[bi, hi, sel]
        s_sel = jnp.einsum('bhqd,bhqrkd->bhqrk', qc, ks).astype(jnp.float32)
        s_sel = jnp.where(valid[..., None], s_sel, -jnp.inf).reshape(
            b, MOBA_HEADS, Q_CHUNK, k_sel * MOBA_BLOCK)
        k_own = lax.dynamic_index_in_dim(kb, qblk, axis=2, keepdims=False)
        v_own = lax.dynamic_index_in_dim(vb, qblk, axis=2, keepdims=False)
        s_own = jnp.einsum('bhqd,bhkd->bhqk', qc, k_own).astype(jnp.float32)
        own_pos = qblk * MOBA_BLOCK + key_off
        s_own = jnp.where(own_pos[None, :] <= qpos[:, None], s_own, -jnp.inf)
        p = jax.nn.softmax(jnp.concatenate([s_sel, s_own], axis=-1), axis=-1).astype(v.dtype)
        p_sel = p[..., :k_sel * MOBA_BLOCK].reshape(b, MOBA_HEADS, Q_CHUNK, k_sel, MOBA_BLOCK)
        p_own = p[..., k_sel * MOBA_BLOCK:]
        return (jnp.einsum('bhqrk,bhqrkd->bhqd', p_sel, vs)
                + jnp.einsum('bhqk,bhkd->bhqd', p_own, v_own))

    o = lax.map(chunk, jnp.arange(s // Q_CHUNK))
    o = jnp.transpose(o, (1, 0, 3, 2, 4)).reshape(b, s, D_MODEL)
    return o @ w_o


def memory_cross_attention(h, mem_n, w_q, w_kv, w_o):
    b, s, _ = h.shape
    m = mem_n.shape[1]
    q = (h @ w_q).reshape(b, s, MEM_HEADS, MEM_HEAD_DIM)
    kv = (mem_n @ w_kv).reshape(b, m, 2, MEM_HEADS, MEM_HEAD_DIM)
    k, v = kv[:, :, 0], kv[:, :, 1]
    sc = jnp.einsum('bshd,bmhd->bhsm', q, k).astype(jnp.float32) * (MEM_HEAD_DIM ** -0.5)
    p = jax.nn.softmax(sc, axis=-1).astype(v.dtype)
    o = jnp.einsum('bhsm,bmhd->bshd', p, v).reshape(b, s, D_MODEL)
    return o @ w_o


def sq_relu_mlp(h, w1, w2):
    a = jax.nn.relu(h @ w1)
    return (a * a) @ w2


def setup_inputs(seed: int = 0) -> dict:
    key = jax.random.key(seed)
    ks = jax.random.split(key, 16)
    f32 = jnp.float32

    def w(k, shape, fan_in):
        return jax.random.normal(k, shape, f32) * (fan_in ** -0.5)

    return {
        "x": jax.random.normal(ks[0], (BATCH, SEQ, D_MODEL), f32),
        "mem": jax.random.normal(ks[1], (BATCH, N_MEM, D_MODEL), f32),
        "norm_gains": 1.0 + 0.02 * jax.random.normal(ks[2], (DEPTH, N_NORMS, D_MODEL), f32),
        "mem_norm": 1.0 + 0.02 * jax.random.normal(ks[3], (DEPTH, D_MODEL), f32),
        "pool_w_in": w(ks[4], (N_POOL_LAYERS, D_MODEL, D_MODEL), D_MODEL),
        "pool_w_group": w(ks[5], (N_POOL_LAYERS, POOL_GROUPS, POOL_GW, POOL_GW), POOL_GW),
        "pool_scale": 1.0 + 0.1 * jax.random.normal(ks[6], (N_POOL_LAYERS, D_MODEL), f32),
        "moba_w_qkv": w(ks[7], (N_MOBA_LAYERS, D_MODEL, 3 * D_MODEL), D_MODEL),
        "moba_w_o": w(ks[8], (N_MOBA_LAYERS, D_MODEL, D_MODEL), D_MODEL),
        "xa_w_q": w(ks[9], (DEPTH, D_MODEL, D_MODEL), D_MODEL),
        "xa_w_kv": w(ks[10], (DEPTH, D_MODEL, 2 * D_MODEL), D_MODEL),
        "xa_w_o": w(ks[11], (DEPTH, D_MODEL, D_MODEL), D_MODEL),
        "mlp_w1": w(ks[12], (DEPTH, D_MODEL, D_FF), D_MODEL),
        "mlp_w2": w(ks[13], (DEPTH, D_FF, D_MODEL), D_FF),
    }


def reference(x, mem, norm_gains, mem_norm, pool_w_in, pool_w_group, pool_scale,
              moba_w_qkv, moba_w_o, xa_w_q, xa_w_kv, xa_w_o, mlp_w1, mlp_w2):
    s = x.shape[1]
    pos = jnp.arange(s, dtype=jnp.float32)
    inv_freq = ROPE_THETA ** (-jnp.arange(0, ROT_DIM, 2, dtype=jnp.float32) / ROT_DIM)
    ang = pos[:, None] * inv_freq[None, :]
    cos, sin = jnp.cos(ang), jnp.sin(ang)
    for i in range(DEPTH):
        j = i // N_MIXERS
        hn = rms_norm(x, norm_gains[i, 0])
        if i % N_MIXERS == 0:
            y = pool_mixer(hn, pool_w_in[j], pool_w_group[j], pool_scale[j])
        else:
            y = moba_attention(hn, moba_w_qkv[j], moba_w_o[j], cos, sin)
        x = x + rms_norm(y, norm_gains[i, 1])
        mem_n = rms_norm(mem, mem_norm[i])
        y = memory_cross_attention(rms_norm(x, norm_gains[i, 2]), mem_n,
                                   xa_w_q[i], xa_w_kv[i], xa_w_o[i])
        x = x + rms_norm(y, norm_gains[i, 3])
        y = sq_relu_mlp(rms_norm(x, norm_gains[i, 4]), mlp_w1[i], mlp_w2[i])
        x = x + rms_norm(y, norm_gains[i, 5])
    return x
```

```python
import numpy as np
from contextlib import ExitStack
import concourse.bass as bass
import concourse.mybir as mybir
from concourse.bass_utils import run_bass_kernel_spmd

F32 = mybir.dt.float32
BF16 = mybir.dt.bfloat16
ALU = mybir.AluOpType
AF = mybir.ActivationFunctionType
AX = mybir.AxisListType

ENGS = ("pe", "act", "dve", "pool", "sp")


def I(method, **kw):
    return (method, kw)

D = 1024
KC = 8
SEQ = 2048
NMEM = 256
DEPTH = 4
DFF = 4096
NEG = -30000.0
WINDOWS = (2, 4, 8, 16)
EPS = 1e-6
DBG = set()


class Op:
    __slots__ = ("eng", "fn", "deps", "slot", "seq", "epoch", "ndep", "gid")

    def __init__(self, eng, fn, slot, epoch, gid):
        self.eng = eng
        self.fn = fn
        self.deps = []
        self.slot = slot
        self.seq = None
        self.epoch = epoch
        self.ndep = 0
        self.gid = gid


class Sched:
    def __init__(self):
        self.ops = {e: [] for e in ENGS}
        self.lw = {}
        self.rd = {}
        self.epoch = 0
        self.gid = 0
        self.last = {e: None for e in ENGS}
        self.barrier_deps = []
        self.dma_ops = []
        self.lastslot = {}

    def add(self, eng, fn, reads=(), writes=(), slot=None, nobarrier=False):
        o = Op(eng, fn, slot, self.epoch, self.gid)
        self.gid += 1
        deps = {}
        for k in reads:
            w = self.lw.get(k)
            if w is not None:
                deps[w.gid] = w
        for k in writes:
            w = self.lw.get(k)
            if w is not None:
                deps[w.gid] = w
            for r in self.rd.get(k, {}).values():
                deps[r.gid] = r
        if not nobarrier:
            for b in self.barrier_deps:
                deps[b.gid] = b
        if eng == "pe" and slot is None:
            deps = {g: d for g, d in deps.items() if not (d.eng == "pe" and d.slot is None)}
        o.deps = list(deps.values())
        for d in o.deps:
            d.ndep += 1
        rk = eng if slot is None else ("dma", o.gid)
        for k in reads:
            self.rd.setdefault(k, {})[rk] = o
        for k in writes:
            self.lw[k] = o
            self.rd[k] = {}
        self.ops[eng].append(o)
        if slot is not None:
            self.dma_ops.append(o)
            self.lastslot[slot] = o
        else:
            self.last[eng] = o
        return o

    def barrier(self, keep=()):
        deps = {}
        for e in ENGS:
            if self.last[e] is not None:
                deps[self.last[e].gid] = self.last[e]
        for o in self.lastslot.values():
            deps[o.gid] = o
        self.barrier_deps = list(deps.values())
        keep = set(keep) | {"WA", "WB"}
        self.lw = {k: v for k, v in self.lw.items() if k in keep}
        self.rd = {k: v for k, v in self.rd.items() if k in keep}

    def emit(self, nc, stack, final_wait_slots=()):
        n_epochs = self.epoch + 1
        sems = {}
        for e in ("pe", "act", "dve", "pool"):
            for ep in range(n_epochs):
                sems[(e, ep)] = stack.enter_context(nc.semaphore(f"s_{e}_{ep}"))
        slot_names = []
        for o in self.dma_ops:
            if o.slot not in slot_names:
                slot_names.append(o.slot)
        for s in slot_names:
            sems[("dma", s)] = stack.enter_context(nc.semaphore(f"d_{s}"))
        cnt = {}
        for e in ENGS:
            for o in self.ops[e]:
                if o.slot is not None:
                    k = ("dma", o.slot)
                    cnt[k] = cnt.get(k, 0) + 16
                    o.seq = cnt[k]
                elif o.ndep > 0:
                    k = (e, o.epoch)
                    cnt[k] = cnt.get(k, 0) + 1
                    o.seq = cnt[k]
        self.sem_counts = cnt

        def semkey(d):
            return ("dma", d.slot) if d.slot is not None else (d.eng, d.epoch)

        block = stack.enter_context(nc.Block())
        engobj = {"pe": "tensor", "act": "scalar", "dve": "vector", "pool": "gpsimd", "sp": "sync"}

        def run(e, eng):
            waited = {}
            for o in self.ops[e]:
                for d in o.deps:
                    if d.slot is None and d.eng == e and e == "pe":
                        continue
                    k = semkey(d)
                    if waited.get(k, 0) >= d.seq:
                        continue
                    eng.wait_ge(sems[k], d.seq)
                    waited[k] = d.seq
                ins = getattr(eng, o.fn[0])(**o.fn[1])
                if o.slot is not None:
                    ins.then_inc(sems[("dma", o.slot)], 16)
                elif o.ndep > 0:
                    ins.then_inc(sems[(e, o.epoch)], 1)
            if e == "sp":
                for s in final_wait_slots:
                    k = ("dma", s)
                    eng.wait_ge(sems[k], cnt[k])

        for e in ENGS:
            if not self.ops[e] and e != "sp":
                continue
            deco = getattr(block, engobj[e])

            def _f(eng, e=e):
                run(e, eng)
            deco(_f)


class Arena:
    def __init__(self, tensor, nwords):
        self.t = tensor
        self.n = nwords
        self.off = 0

    def f32(self, *shape):
        n = int(np.prod(shape))
        n_al = (n + 7) // 8 * 8
        assert self.off + n_al <= self.n, f"arena overflow {self.off}+{n_al}>{self.n}"
        a = self.t[:, self.off:self.off + n]
        self.off += n_al
        return self._shape(a, shape)

    def bf16(self, *shape):
        n = int(np.prod(shape))
        assert n % 2 == 0
        nw = n // 2
        n_al = (nw + 7) // 8 * 8
        assert self.off + n_al <= self.n, f"arena overflow {self.off}+{n_al}>{self.n}"
        a = self.t[:, self.off:self.off + nw].bitcast(BF16)
        self.off += n_al
        return self._shape(a, shape)

    @staticmethod
    def _shape(a, shape):
        if len(shape) == 1:
            return a
        if len(shape) == 2:
            return a.rearrange("p (a b) -> p a b", a=shape[0])
        if len(shape) == 3:
            return a.rearrange("p (a b c) -> p a b c", a=shape[0], b=shape[1])
        raise ValueError

    def mark(self):
        return self.off

    def release(self, m):
        self.off = m


def build_program(layers, first=True, last=True):
    nc = bass.Bass("TRN2", target_bir_lowering=False)

    def din(name, shape):
        return nc.dram_tensor(name, list(shape), F32, kind="ExternalInput").ap()

    x_d = din("x", (SEQ, D))
    mem_d = din("mem", (NMEM, D))
    gains_d = din("gains", (30, D))
    pool_w_in = din("pool_w_in", (2, D, D))
    pool_w_group = din("pool_w_group", (2, 4, 256, 256))
    moba_w_qkv = din("moba_w_qkv", (2, D, 3 * D))
    moba_w_o = din("moba_w_o", (2, D, D))
    xa_w_q = din("xa_w_q", (DEPTH, D, D))
    xa_w_kv = din("xa_w_kv", (DEPTH, D, 2 * D))
    xa_w_o = din("xa_w_o", (DEPTH, D, D))
    mlp_w1 = din("mlp_w1", (DEPTH, D, DFF))
    mlp_w2 = din("mlp_w2", (DEPTH, DFF, D))
    ident_d = din("c_ident", (128, 128))
    rsw_d = din("c_rswap", (128, 32))
    cos_d = din("c_cos", (32, SEQ))
    sin_d = din("c_sin", (32, SEQ))
    mown_d = din("c_mown", (2, 128, 256))
    rc_d = din("c_rc", (128, 16))
    out_d = nc.dram_tensor("out", [SEQ, D], F32, kind="ExternalOutput").ap()

    S = Sched()
    st = ExitStack()
    with st:
        NW = 212000 // 4
        arena_t = st.enter_context(nc.sbuf_tensor("arena", [128, NW], F32))
        A = Arena(arena_t, NW)
        banks = [st.enter_context(nc.psum_tensor(f"bank{i}", [128, 512], F32)) for i in range(8)]
        ps_rr = {"all": 0, "lo": 0}

        def ps(pool="all"):
            if pool == "all":
                i = ps_rr["all"] % 8
                ps_rr["all"] += 1
            else:
                i = ps_rr["lo"] % 4
                ps_rr["lo"] += 1
            return banks[i], ("ps", i)

        xT = A.f32(KC, SEQ)
        ident = A.f32(128)
        ident_bf = A.bf16(128)
        ones_bf = A.bf16(128)
        rsw = A.f32(32)
        epsb = A.f32(8)
        rc16 = A.f32(16)
        mown = A.bf16(2, 256)
        gT = A.f32(KC, 32)
        WA = A.bf16(KC, D)
        WB = A.bf16(KC, D)
        WAf = WA.rearrange("p a b -> p (a b)")
        WBf = WB.rearrange("p a b -> p (a b)")
        base_mark = A.mark()

        def xkeys(t512, ks=range(KC)):
            return [("x", k, t512) for k in ks]

        rot = {"sq": 0, "cp": 0}

        def wload(slot_ap, slotkey, src_ap, nk, ncols, nsplit=4):
            src = src_ap.rearrange("(k p) n -> p k n", p=128)
            step = max(1, nk // nsplit)
            for k0 in range(0, nk, step):
                S.add("pool", I("dma_start", out=slot_ap[:, k0:k0 + step, :], in_=src[:, k0:k0 + step, :]),
                      writes=[slotkey], slot="w_" + slotkey, nobarrier=True)

        def rms_stats(src_fn, W, src_reads, tag):
            sq = stats_bufs["sq"]
            bank, bk = ps()
            for k in range(KC):
                i = rot["sq"] % 2
                rot["sq"] += 1
                S.add("act", I("activation", out=sq[i][:, 0:W], in_=src_fn(k), func=AF.Square),
                      reads=[src_reads(k)], writes=[("sq", i)])
                S.add("pe", I("matmul", out=bank[:, 0:W], lhsT=ones_bf, rhs=sq[i][:, 0:W], start=(k == 0), stop=(k == KC - 1)),
                      reads=[("sq", i), "consts"], writes=[bk])
            ln = stats_bufs["ln"]
            rstd = stats_bufs["rstd"]
            S.add("act", I("activation", out=ln[:, 0:W], in_=bank[:, 0:W], func=AF.Ln, scale=1.0 / D, bias=epsb[:, 0:1]),
                  reads=[bk, "consts"], writes=["ln"])
            S.add("act", I("activation", out=rstd[:, 0:W], in_=ln[:, 0:W], func=AF.Exp, scale=-0.5),
                  reads=["ln"], writes=["rstd"])
            return rstd

        def prenorm(xcols, t512, W, gidx, dst, dstkey):
            c0 = xcols
            rstd = rms_stats(lambda k: xT[:, k, c0:c0 + W], W, lambda k: ("x", k, t512), "pre")
            for k in range(KC):
                S.add("dve", I("scalar_tensor_tensor", out=dst[:, k, 0:W], in0=xT[:, k, c0:c0 + W], scalar=gT[:, k, gidx:gidx + 1],
                                                                    in1=rstd[:, 0:W], op0=ALU.mult, op1=ALU.mult),
                      reads=[("x", k, t512), "rstd", "consts"], writes=[(dstkey, k)])

        def postnorm_residual(y, ykey, xcols, t512, W, gidx):
            c0 = xcols
            rstd = rms_stats(lambda k: y[:, k, 0:W], W, lambda k: (ykey, k), "post")
            for k in range(KC):
                S.add("dve", I("scalar_tensor_tensor", out=y[:, k, 0:W], in0=y[:, k, 0:W], scalar=gT[:, k, gidx:gidx + 1],
                                                                    in1=rstd[:, 0:W], op0=ALU.mult, op1=ALU.mult),
                      reads=["rstd", "consts"], writes=[(ykey, k)])
                S.add("pool", I("tensor_tensor", out=xT[:, k, c0:c0 + W], in0=xT[:, k, c0:c0 + W], in1=y[:, k, 0:W], op=ALU.add),
                      reads=[(ykey, k)], writes=[("x", k, t512)])

        def evac_copy(dst, src, reads, writes, scale=None):
            i = rot["cp"] % 2
            rot["cp"] += 1
            if scale is not None or i == 0:
                if scale is None:
                    S.add("act", I("activation", out=dst, in_=src, func=AF.Copy), reads=reads, writes=writes)
                else:
                    S.add("act", I("activation", out=dst, in_=src, func=AF.Copy, scale=scale), reads=reads + ["consts"], writes=writes)
            else:
                S.add("dve", I("tensor_copy", out=dst, in_=src), reads=reads, writes=writes)

        def proj(dst_fn, dstkey_fn, W_slot, wkey, src, srckey, W, nout=KC, col0=0, evac=None):
            for c in range(nout):
                bank, bk = ps()
                for k in range(KC):
                    S.add("pe", I("matmul", out=bank[:, 0:W], lhsT=W_slot[:, k, col0 + c * 128: col0 + (c + 1) * 128],
                                                                         rhs=src[:, k, 0:W], start=(k == 0), stop=(k == KC - 1)),
                          reads=[wkey, (srckey, k)], writes=[bk])
                if evac is None:
                    evac_copy(dst_fn(c), bank[:, 0:W], [bk], [dstkey_fn(c)])
                else:
                    evac(c, bank, bk)

        S.add("sp", I("dma_start", out=ident, in_=ident_d), writes=["c_id"], slot="c0")
        S.add("sp", I("dma_start", out=rsw, in_=rsw_d), writes=["c_rs"], slot="c0")
        S.add("sp", I("dma_start", out=rc16, in_=rc_d), writes=["c_rc"], slot="c0")
        S.add("pool", I("dma_start", out=mown, in_=mown_d.rearrange("a p n -> p a n")), writes=["c_mo"], slot="c1")
        S.add("dve", I("tensor_copy", out=ident_bf, in_=ident), reads=["c_id"], writes=["c_idbf"])
        S.add("dve", I("memset", ap=ones_bf, constant=1.0), writes=["c_ones"])
        S.add("dve", I("memset", ap=epsb, constant=EPS), writes=["c_eps"])
        graw = A.f32(D)
        S.add("sp", I("dma_start", out=graw[0:30, :], in_=gains_d), writes=["graw"], slot="c0")
        for k in range(KC):
            bank, bk = ps()
            S.add("pe", I("transpose", out=bank[:, 0:30], in_=graw[0:30, k * 128:(k + 1) * 128], identity=ident[0:30, 0:30]),
                  reads=["graw", "c_id"], writes=[bk])
            S.add("dve", I("tensor_copy", out=gT[:, k, 0:30], in_=bank[:, 0:30]), reads=[bk], writes=["c_g"])
        xst = [A.f32(D), A.f32(D)]
        for tt in range(SEQ // 128):
            i = tt % 2
            S.add("sp", I("dma_start", out=xst[i], in_=x_d[tt * 128:(tt + 1) * 128, :]), writes=[("xst", i)], slot=f"xs{i}")
            for hh in range(2):
                bank, bk = ps()
                for kk in range(4):
                    k = hh * 4 + kk
                    S.add("pe", I("transpose", out=bank[:, kk * 128:(kk + 1) * 128], in_=xst[i][:, k * 128:(k + 1) * 128], identity=ident),
                          reads=[("xst", i), "c_id"], writes=[bk])
                dst = xT[:, hh * 4:(hh + 1) * 4, tt * 128:(tt + 1) * 128]
                src = bank[:, 0:512].rearrange("p (a b) -> p a b", a=4)
                evac_copy(dst, src, [bk], [("x", k, tt // 4) for k in range(hh * 4, hh * 4 + 4)])
        S.barrier()
        A.release(base_mark)

        for li in layers:
            S.epoch += 1
            j = li // 2
            g0 = li * 6
            if li % 2 == 0:
                m0 = A.mark()
                stats_bufs = {"sq": [A.bf16(512), A.bf16(512)], "ln": A.f32(512), "rstd": A.f32(512)}
                hn = A.bf16(KC, 512)
                wg = A.bf16(4, 2, 256)
                U = A.f32(KC, 528)
                pp = [[A.f32(528), A.f32(528)] for _ in range(2)]
                pooled = A.bf16(KC, 512)
                fix = A.f32(16)
                y = A.f32(KC, 512)
                wload(WA, "WA", pool_w_in[j], KC, D)
                for g in range(4):
                    S.add("pool", I("dma_start", out=wg[:, g, :, :], in_=pool_w_group[j, g].rearrange("(c p) n -> p c n", p=128)),
                          writes=["wg"], slot="w_wg", nobarrier=False)
                S.add("dve", I("memset", ap=U[:, :, 0:16], constant=0.0), writes=[("U", c) for c in range(KC)])
                for t in range(4):
                    prenorm(t * 512, t, 512, g0 + 0, hn, "hn")
                    if t > 0:
                        S.add("dve", I("tensor_copy", out=U[:, :, 0:16], in_=U[:, :, 512:528]),
                              reads=[("U", c) for c in range(KC)], writes=[("U", c) for c in range(KC)])
                    proj(lambda c: U[:, c, 16:528], lambda c: ("U", c), WA, "WA", hn, "hn", 512)
                    for c in range(KC):
                        w = WINDOWS[c // 2]
                        nl = {2: 1, 4: 2, 8: 3, 16: 4}[w]
                        bufs = pp[c % 2]
                        cur, curkey = U[:, c, :], ("U", c)
                        sh = 1
                        lo = 0
                        for lv in range(nl):
                            nxt, nxtkey = bufs[lv % 2], ("pp", c % 2, lv % 2)
                            lo2 = lo + sh
                            S.add("pool", I("tensor_tensor", out=nxt[:, lo2:528], in0=cur[:, lo2:528],
                                                                                                   in1=cur[:, lo2 - sh:528 - sh], op=ALU.add),
                                  reads=[curkey], writes=[nxtkey])
                            cur, curkey = nxt, nxtkey
                            lo = lo2
                            sh *= 2
                        S.add("dve", I("scalar_tensor_tensor", out=pooled[:, c, :], in0=cur[:, 16:528], scalar=1.0 / w,
                                                                                          in1=U[:, c, 16:528], op0=ALU.mult, op1=ALU.subtract),
                              reads=[curkey, ("U", c)], writes=[("pooled", c)])
                        if t == 0:
                            S.add("dve", I("tensor_tensor", out=fix[:, 0:15], in0=cur[:, 16:31], in1=rc16[:, 0:15], op=ALU.mult),
                                  reads=[curkey, "consts"], writes=["fix"])
                            S.add("dve", I("tensor_tensor", out=pooled[:, c, 0:w - 1], in0=fix[:, 0:w - 1], in1=U[:, c, 16:16 + w - 1], op=ALU.subtract),
                                  reads=["fix", ("U", c)], writes=[("pooled", c)])
                    for g in range(4):
                        for oc in range(2):
                            co = 2 * g + oc
                            bank, bk = ps()
                            for kc in range(2):
                                S.add("pe", I("matmul", out=bank[:, :], lhsT=wg[:, g, kc, oc * 128:(oc + 1) * 128],
                                                                                              rhs=pooled[:, 2 * g + kc, :], start=(kc == 0), stop=(kc == 1)),
                                      reads=["wg", ("pooled", 2 * g + kc)], writes=[bk])
                            evac_copy(y[:, co, :], bank[:, :], [bk], [("y", co)], scale=gT[:, co, 28 + j:29 + j])
                    postnorm_residual(y, "y", t * 512, t, 512, g0 + 1)
                S.barrier()
                A.release(m0)
            else:
                m0 = A.mark()
                kT = A.bf16(KC, SEQ)
                V = A.bf16(16, D)
                kmT = A.bf16(8, 8)
                kms = A.f32(8, 8)
                m1 = A.mark()
                stats_bufs = {"sq": [A.bf16(512), A.bf16(512)], "ln": A.f32(512), "rstd": A.f32(512)}
                hn = A.bf16(KC, 512)
                kf = A.f32(512)
                t1 = A.f32(512)
                t2 = A.f32(512)
                rc_ = A.f32(512)
                rs_ = A.f32(512)
                wload(WA, "WA", moba_w_qkv[j, :, D:2 * D], KC, D)
                wload(WB, "WB", moba_w_qkv[j, :, 2 * D:3 * D], KC, D)

                def rope_part(dst, f32src, bank2, bk2, W, tag):
                    S.add("pe", I("matmul", out=bank2[0:32, 0:W], lhsT=rsw, rhs=f32src[:, 0:W], start=True, stop=True),
                          reads=[tag + "f", "consts"], writes=[bk2])
                    S.add("dve", I("tensor_tensor", out=t1[0:32, 0:W], in0=f32src[0:32, 0:W], in1=rc_[0:32, 0:W], op=ALU.mult),
                          reads=[tag + "f", "ropec"], writes=["t1"])
                    S.add("dve", I("tensor_tensor", out=t2[0:32, 0:W], in0=bank2[0:32, 0:W], in1=rs_[0:32, 0:W], op=ALU.mult),
                          reads=[bk2, "ropes"], writes=["t2"])

                for t in range(4):
                    prenorm(t * 512, t, 512, g0 + 0, hn, "hn")
                    S.add("sp", I("dma_start", out=rc_[0:32, :], in_=cos_d[:, t * 512:(t + 1) * 512]), writes=["ropec"], slot="rc")
                    S.add("sp", I("dma_start", out=rs_[0:32, :], in_=sin_d[:, t * 512:(t + 1) * 512]), writes=["ropes"], slot="rs")
                    for h in range(0 if "nok" in DBG else 8):
                        bank, bk = ps()
                        for k in range(KC):
                            S.add("pe", I("matmul", out=bank[:, :], lhsT=WA[:, k, h * 128:(h + 1) * 128], rhs=hn[:, k, :],
                                                                                 start=(k == 0), stop=(k == KC - 1)),
                                  reads=["WA", ("hn", k)], writes=[bk])
                        S.add("act", I("activation", out=kf, in_=bank[:, :], func=AF.Copy), reads=[bk], writes=["kf"])
                        kdst = kT[:, h, t * 512:(t + 1) * 512]
                        bank2, bk2 = ps()
                        rope_part(None, kf, bank2, bk2, 512, "k")
                        S.add("dve", I("tensor_tensor", out=kf[0:32, :], in0=t1[0:32, :], in1=t2[0:32, :], op=ALU.add),
                              reads=["t1", "t2"], writes=["kf"])
                        S.add("dve", I("tensor_copy", out=kdst, in_=kf), reads=["kf"], writes=[("kT", h)])
                        S.add("dve", I("tensor_reduce", out=kms[:, h, 2 * t:2 * t + 2], in_=kf.rearrange("p (n s) -> p n s", n=2), axis=AX.X, op=ALU.add),
                              reads=["kf"], writes=["kms"])
                    for ts in range(0 if "nov" in DBG else 4):
                        for hf in range(2):
                            bank, bk = ps()
                            for k in range(KC):
                                S.add("pe", I("matmul", out=bank[:, :], lhsT=hn[:, k, ts * 128:(ts + 1) * 128],
                                                                                              rhs=WB[:, k, hf * 512:(hf + 1) * 512], start=(k == 0), stop=(k == KC - 1)),
                                      reads=["WB", ("hn", k)], writes=[bk])
                            evac_copy(V[:, t * 4 + ts, hf * 512:(hf + 1) * 512], bank[:, :], [bk], [("V", t * 4 + ts)])
                keep = set([("kT", h) for h in range(8)] + [("V", i) for i in range(16)] + ["kms"])
                S.barrier(keep=keep)
                A.release(m1)
                S.epoch += 1
                NU = 0 if "nop2" in DBG else 8
                GATE_FROM = 99 if "nogate" in DBG else 4
                stats_bufs = {"sq": [A.bf16(256), A.bf16(256)], "ln": A.f32(256), "rstd": A.f32(256)}
                hn = A.bf16(KC, 256)
                qf = A.f32(256)
                t1 = A.f32(256)
                t2 = A.f32(256)
                rc_ = A.f32(256)
                rs_ = A.f32(256)
                qt_ = A.bf16(KC, 256)
                ao = A.bf16(KC, 256)
                y = A.f32(KC, 256)
                pT = [A.bf16(512) for _ in range(3)]
                rec = A.f32(256)
                maskb = A.bf16(2, 8, 8)
                Gs = A.f32(64)
                cmpb = A.f32(8 * 49)
                cntb = A.f32(56)
                wload(WA, "WA", moba_w_qkv[j, :, 0:D], KC, D)
                wload(WB, "WB", moba_w_o[j], KC, D)
                S.add("dve", I("memset", ap=maskb, constant=0.0), writes=["maskb"])
                SC = 128 ** -0.5
                prr = 0
                for u in range(NU):
                    t512 = u // 2
                    c0 = u * 256
                    prenorm(c0, t512, 256, g0 + 0, hn, "hn")
                    S.add("sp", I("dma_start", out=rc_[0:32, :], in_=cos_d[:, c0:c0 + 256]), writes=["ropec"], slot="rc")
                    S.add("sp", I("dma_start", out=rs_[0:32, :], in_=sin_d[:, c0:c0 + 256]), writes=["ropes"], slot="rs")
                    for h in range(8):
                        bank, bk = ps("lo")
                        for k in range(KC):
                            S.add("pe", I("matmul", out=bank[:, 0:256], lhsT=WA[:, k, h * 128:(h + 1) * 128], rhs=hn[:, k, :],
                                                                                 start=(k == 0), stop=(k == KC - 1)),
                                  reads=["WA", ("hn", k)], writes=[bk])
                        S.add("act", I("activation", out=qf, in_=bank[:, 0:256], func=AF.Copy), reads=[bk], writes=["qf"])
                        bank2, bk2 = ps("lo")
                        rope_part(None, qf, bank2, bk2, 256, "q")
                        S.add("dve", I("tensor_tensor", out=qf[0:32, :], in0=t1[0:32, :], in1=t2[0:32, :], op=ALU.add),
                              reads=["t1", "t2"], writes=["qf"])
                        S.add("dve", I("tensor_copy", out=qt_[:, h, :], in_=qf), reads=["qf"], writes=[("q", h)])
                        if u >= GATE_FROM:
                            for qi in range(2):
                                S.add("pe", I("matmul", out=banks[6 + qi][:, h * 8:(h + 1) * 8], lhsT=qf[:, qi * 128:(qi + 1) * 128],
                                              rhs=kms[:, h, :], start=True, stop=True),
                                      reads=["qf", "kms"], writes=[("ps", 6 + qi)])
                    if u >= GATE_FROM:
                        for qi in range(2):
                            bank, bk = banks[6 + qi], ("ps", 6 + qi)
                            S.add("dve", I("tensor_copy", out=Gs, in_=bank[:, 0:64]), reads=[bk], writes=["Gs"])
                            G3 = Gs.rearrange("p (h n) -> p h n", n=8)
                            cm = cmpb[:, 0:8 * u * u].rearrange("p (h n m) -> p h n m", n=u, m=u)
                            in_m = G3[:, :, 0:u].unsqueeze(2).broadcast_to([128, 8, u, u])
                            in_n = G3[:, :, 0:u].unsqueeze(3).broadcast_to([128, 8, u, u])
                            cn = cntb[:, 0:8 * u].rearrange("p (h n) -> p h n", n=u)
                            S.add("dve", I("tensor_tensor", out=cm, in0=in_m, in1=in_n, op=ALU.is_gt),
                                  reads=["Gs"], writes=["cmp"])
                            S.add("dve", I("tensor_reduce", out=cn, in_=cm, axis=AX.X, op=ALU.add), reads=["cmp"], writes=["cnt"])
                            S.add("dve", I("tensor_single_scalar", out=cn, in_=cn, scalar=3.0, op=ALU.is_ge), reads=["cnt"], writes=["cnt"])
                            S.add("dve", I("tensor_single_scalar", out=maskb[:, qi, :, 0:u], in_=cn, scalar=NEG, op=ALU.mult),
                                  reads=["cnt"], writes=["maskb"])
                    for h in range(8):
                        tiles = [("own", 0), ("own", 1)] + [(n, jj) for n in range(u) for jj in range(2)]
                        pairs = [tiles[i:i + 2] for i in range(0, len(tiles), 2)]
                        bo, bok = banks[4 + 2 * (h % 2)], ("ps", 4 + 2 * (h % 2))
                        br, brk = banks[5 + 2 * (h % 2)], ("ps", 5 + 2 * (h % 2))
                        npairs = len(pairs)

                        def keytile(tl):
                            return (u * 2 + tl[1]) if tl[0] == "own" else (tl[0] * 2 + tl[1])

                        def emit_qk(pi):
                            nonlocal prr
                            bank, bk = ps("lo")
                            pbuf = prr % 3
                            prr += 1
                            for idx, tl in enumerate(pairs[pi]):
                                kt = keytile(tl)
                                reg = bank[:, idx * 256:(idx + 1) * 256]
                                masked = (tl[0] == "own") or (u >= GATE_FROM)
                                S.add("pe", I("matmul", out=reg, lhsT=kT[:, h, kt * 128:(kt + 1) * 128], rhs=qt_[:, h, :],
                                                                                              start=True, stop=(not masked)),
                                      reads=[("kT", h), ("q", h)], writes=[bk])
                                if tl[0] == "own":
                                    S.add("pe", I("matmul", out=reg, lhsT=ident_bf, rhs=mown[:, tl[1], :], start=False, stop=True),
                                          reads=["consts"], writes=[bk])
                                elif u >= GATE_FROM:
                                    for qi in range(2):
                                        S.add("pe", I("matmul", out=reg[:, qi * 128:(qi + 1) * 128],
                                                                                                lhsT=maskb[:, qi, h, tl[0]:tl[0] + 1].broadcast_to([128, 128]),
                                                                                                rhs=ident_bf, start=False, stop=(qi == 1)),
                                              reads=["maskb", "consts"], writes=[bk])
                            S.add("act", I("activation", out=pT[pbuf], in_=bank[:, :], func=AF.Exp, scale=SC),
                                  reads=[bk], writes=[("pT", pbuf)])
                            return pbuf

                        def emit_pv(pi, pbuf):
                            for idx, tl in enumerate(pairs[pi]):
                                kt = keytile(tl)
                                first = (pi == 0 and idx == 0)
                                lastm = (pi == npairs - 1 and idx == 1)
                                S.add("pe", I("matmul", out=
                                    bo[:, 0:256], lhsT=V[:, kt, h * 128:(h + 1) * 128], rhs=pT[pbuf][:, idx * 256:(idx + 1) * 256], start=first, stop=lastm),
                                    reads=[("V", kt), ("pT", pbuf)], writes=[bok])
                                S.add("pe", I("matmul", out=
                                    br[:, 0:256], lhsT=ones_bf, rhs=pT[pbuf][:, idx * 256:(idx + 1) * 256], start=first, stop=lastm),
                                    reads=[("pT", pbuf), "consts"], writes=[brk])

                        pb = emit_qk(0)
                        for pi in range(npairs):
                            pbn = emit_qk(pi + 1) if pi + 1 < npairs else None
                            emit_pv(pi, pb)
                            pb = pbn
                        S.add("dve", I("reciprocal", out=rec, in_=br[:, 0:256]), reads=[brk], writes=["rec"])
                        S.add("dve", I("tensor_tensor", out=ao[:, h, :], in0=bo[:, 0:256], in1=rec, op=ALU.mult),
                              reads=[bok, "rec"], writes=[("ao", h)])
                    for c in range(KC):
                        bank, bk = ps("lo")
                        for k in range(KC):
                            S.add("pe", I("matmul", out=bank[:, 0:256], lhsT=WB[:, k, c * 128:(c + 1) * 128], rhs=ao[:, k, :],
                                                                                 start=(k == 0), stop=(k == KC - 1)),
                                  reads=["WB", ("ao", k)], writes=[bk])
                        evac_copy(y[:, c, :], bank[:, 0:256], [bk], [("y", c)])
                    postnorm_residual(y, "y", c0, t512, 256, g0 + 1)
                S.barrier()
                A.release(m0)

            if "noxm" in DBG:
                continue
            S.epoch += 1
            m0 = A.mark()
            stats_bufs = {"sq": [A.bf16(512), A.bf16(512)], "ln": A.f32(512), "rstd": A.f32(512)}
            mst = A.f32(2, D)
            memT = A.f32(KC, NMEM)
            memn = A.bf16(KC, NMEM)
            kTm = A.bf16(KC, NMEM)
            Vm = A.bf16(2, D)
            hn = A.bf16(KC, 512)
            qx = A.bf16(KC, 512)
            ao = A.bf16(KC, 512)
            y = A.f32(KC, 512)
            pT = [A.bf16(512) for _ in range(4)]
            rec = A.f32(512)
            wload(WA, "WA", xa_w_kv[li, :, 0:D], KC, D)
            wload(WB, "WB", xa_w_kv[li, :, D:2 * D], KC, D)
            S.add("sp", I("dma_start", out=mst, in_=mem_d.rearrange("(a p) d -> p a d", p=128)), writes=["mst"], slot="mst")
            for a in range(2):
                for hh in range(2):
                    bank, bk = ps()
                    for kk in range(4):
                        k = hh * 4 + kk
                        S.add("pe", I("transpose", out=bank[:, kk * 128:(kk + 1) * 128], in_=mst[:, a, k * 128:(k + 1) * 128], identity=ident),
                              reads=["mst", "consts"], writes=[bk])
                    evac_copy(memT[:, hh * 4:(hh + 1) * 4, a * 128:(a + 1) * 128], bank[:, 0:512].rearrange("p (a b) -> p a b", a=4), [bk],
                              [("memT", k) for k in range(hh * 4, hh * 4 + 4)])
            rstd = rms_stats(lambda k: memT[:, k, :], NMEM, lambda k: ("memT", k), "mem")
            for k in range(KC):
                S.add("dve", I("scalar_tensor_tensor", out=memn[:, k, :], in0=memT[:, k, :], scalar=gT[:, k, 24 + li:25 + li],
                                                                    in1=rstd[:, 0:NMEM], op0=ALU.mult, op1=ALU.mult),
                      reads=[("memT", k), "rstd", "consts"], writes=[("memn", k)])
            proj(lambda c: kTm[:, c, :], lambda c: ("kTm", c), WA, "WA", memn, "memn", NMEM)
            for a in range(2):
                for hf in range(2):
                    bank, bk = ps()
                    for k in range(KC):
                        S.add("pe", I("matmul", out=bank[:, :], lhsT=memn[:, k, a * 128:(a + 1) * 128],
                                                                                    rhs=WB[:, k, hf * 512:(hf + 1) * 512], start=(k == 0), stop=(k == KC - 1)),
                              reads=["WB", ("memn", k)], writes=[bk])
                    evac_copy(Vm[:, a, hf * 512:(hf + 1) * 512], bank[:, :], [bk], [("Vm", a)])
            wload(WA, "WA", xa_w_q[li], KC, D)
            wload(WB, "WB", xa_w_o[li], KC, D)
            SCX = 256 ** -0.5
            prr = 0
            for t in range(4):
                prenorm(t * 512, t, 512, g0 + 2, hn, "hn")
                proj(lambda c: qx[:, c, :], lambda c: ("qx", c), WA, "WA", hn, "hn", 512)
                for h in range(4):
                    pbs = []
                    for a in range(2):
                        bank, bk = ps()
                        for cc in range(2):
                            S.add("pe", I("matmul", out=bank[:, :], lhsT=kTm[:, 2 * h + cc, a * 128:(a + 1) * 128],
                                                                                        rhs=qx[:, 2 * h + cc, :], start=(cc == 0), stop=(cc == 1)),
                                  reads=[("kTm", 2 * h + cc), ("qx", 2 * h + cc)], writes=[bk])
                        pbuf = prr % 4
                        prr += 1
                        S.add("act", I("activation", out=pT[pbuf], in_=bank[:, :], func=AF.Exp, scale=SCX),
                              reads=[bk], writes=[("pT", pbuf)])
                        pbs.append(pbuf)
                    bankr, bkr = ps()
                    for a in range(2):
                        S.add("pe", I("matmul", out=bankr[:, :], lhsT=ones_bf, rhs=pT[pbs[a]], start=(a == 0), stop=(a == 1)),
                              reads=[("pT", pbs[a]), "consts"], writes=[bkr])
                    S.add("dve", I("reciprocal", out=rec, in_=bankr[:, :]), reads=[bkr], writes=["rec"])
                    for cc in range(2):
                        banko, bko = ps()
                        for a in range(2):
                            S.add("pe", I("matmul", out=banko[:, :], lhsT=Vm[:, a, (2 * h + cc) * 128:(2 * h + cc + 1) * 128],
                                                                                                    rhs=pT[pbs[a]], start=(a == 0), stop=(a == 1)),
                                  reads=[("Vm", a), ("pT", pbs[a])], writes=[bko])
                        S.add("dve", I("tensor_tensor", out=ao[:, 2 * h + cc, :], in0=banko[:, :], in1=rec, op=ALU.mult),
                              reads=[bko, "rec"], writes=[("ao", 2 * h + cc)])
                proj(lambda c: y[:, c, :], lambda c: ("y", c), WB, "WB", ao, "ao", 512)
                postnorm_residual(y, "y", t * 512, t, 512, g0 + 3)
            S.barrier()
            A.release(m0)

            S.epoch += 1
            m0 = A.mark()
            stats_bufs = {"sq": [A.bf16(512), A.bf16(512)], "ln": A.f32(512), "rstd": A.f32(512)}
            hnh = A.bf16(KC, 1024)
            yacc = A.f32(KC, 1024)
            aT = [A.bf16(4, 1024), A.bf16(4, 1024)]
            rl = [A.f32(512), A.f32(512)]
            rlr = 0
            gcount = 0
            for hf in range(2):
                for tt in range(2):
                    t = hf * 2 + tt
                    c0 = t * 512
                    rstd = rms_stats(lambda k, c0=c0: xT[:, k, c0:c0 + 512], 512, lambda k, t=t: ("x", k, t), "pre")
                    for k in range(KC):
                        S.add("dve", I("scalar_tensor_tensor", out=hnh[:, k, tt * 512:(tt + 1) * 512], in0=xT[:, k, c0:c0 + 512],
                                                                                          scalar=gT[:, k, g0 + 4:g0 + 5], in1=rstd, op0=ALU.mult, op1=ALU.mult),
                              reads=[("x", k, t), "rstd", "consts"], writes=[("hnh", k, tt)])
                for gi in range(8):
                    slot, slotf, skey = (WA, WAf, "WA") if gcount % 2 == 0 else (WB, WBf, "WB")
                    ab = gcount % 2
                    gcount += 1
                    w1g = slotf[:, 0:4096].rearrange("p (k n) -> p k n", k=8)
                    w2g = slotf[:, 4096:8192].rearrange("p (k n) -> p k n", k=4)
                    wload(w1g, skey, mlp_w1[li, :, gi * 512:(gi + 1) * 512], 8, 512, nsplit=2)
                    wload(w2g, skey, mlp_w2[li, gi * 512:(gi + 1) * 512, :], 4, D, nsplit=2)
                    for tt in range(2):
                        for fc in range(4):
                            bank, bk = ps()
                            for k in range(KC):
                                S.add("pe", I("matmul", out=bank[:, :], lhsT=w1g[:, k, fc * 128:(fc + 1) * 128],
                                                                                                      rhs=hnh[:, k, tt * 512:(tt + 1) * 512], start=(k == 0), stop=(k == KC - 1)),
                                      reads=[skey, ("hnh", k, tt)], writes=[bk])
                            ri = rlr % 2
                            rlr += 1
                            S.add("act", I("activation", out=rl[ri], in_=bank[:, :], func=AF.Relu), reads=[bk], writes=[("rl", ri)])
                            S.add("pool", I("tensor_tensor", out=aT[ab][:, fc, tt * 512:(tt + 1) * 512], in0=rl[ri], in1=rl[ri], op=ALU.mult),
                                  reads=[("rl", ri)], writes=[("aT", ab, fc, tt)])
                    for tt in range(2):
                        for c in range(KC):
                            bank, bk = ps()
                            for fc in range(4):
                                S.add("pe", I("matmul", out=bank[:, :], lhsT=w2g[:, fc, c * 128:(c + 1) * 128],
                                                                                                             rhs=aT[ab][:, fc, tt * 512:(tt + 1) * 512], start=(fc == 0), stop=(fc == 3)),
                                      reads=[skey, ("aT", ab, fc, tt)], writes=[bk])
                            dst = yacc[:, c, tt * 512:(tt + 1) * 512]
                            if gi == 0:
                                evac_copy(dst, bank[:, :], [bk], [("yacc", c, tt)])
                            else:
                                S.add("dve", I("tensor_tensor", out=dst, in0=bank[:, :], in1=dst, op=ALU.add),
                                      reads=[bk], writes=[("yacc", c, tt)])
                for tt in range(2):
                    t = hf * 2 + tt
                    c0 = t * 512
                    yv = yacc[:, :, tt * 512:(tt + 1) * 512]
                    rstd = rms_stats(lambda k, yv=yv: yv[:, k, :], 512, lambda k, tt=tt: ("yacc", k, tt), "post")
                    for k in range(KC):
                        S.add("dve", I("scalar_tensor_tensor", out=yv[:, k, :], in0=yv[:, k, :], scalar=gT[:, k, g0 + 5:g0 + 6],
                                                                                   in1=rstd, op0=ALU.mult, op1=ALU.mult),
                              reads=["rstd", "consts"], writes=[("yacc", k, tt)])
                        S.add("pool", I("tensor_tensor", out=xT[:, k, c0:c0 + 512], in0=xT[:, k, c0:c0 + 512], in1=yv[:, k, :], op=ALU.add),
                              reads=[("yacc", k, tt)], writes=[("x", k, t)])
            S.barrier()
            A.release(m0)

        ost = [A.f32(D), A.f32(D)]
        for tt in range(SEQ // 128):
            i = tt % 2
            for hh in range(2):
                bank, bk = ps()
                for kk in range(4):
                    k = hh * 4 + kk
                    S.add("pe", I("transpose", out=bank[:, kk * 128:(kk + 1) * 128], in_=xT[:, k, tt * 128:(tt + 1) * 128], identity=ident),
                          reads=[("x", k, tt // 4), "consts"], writes=[bk])
                evac_copy(ost[i][:, hh * 512:(hh + 1) * 512], bank[:, :], [bk], [("ost", i, hh)])
            S.add("sp", I("dma_start", out=out_d[tt * 128:(tt + 1) * 128, :], in_=ost[i]),
                  reads=[("ost", i, 0), ("ost", i, 1)], slot=f"o{i}")
        S.emit(nc, st, final_wait_slots=["o0", "o1"])
    return nc


def _consts():
    ident = np.eye(128, dtype=np.float32)
    rsw = np.zeros((128, 32), np.float32)
    for i in range(16):
        rsw[i + 16, i] = -1.0
        rsw[i, i + 16] = 1.0
    pos = np.arange(SEQ, dtype=np.float32)
    inv_freq = (np.float32(500000.0) ** (-np.arange(0, 32, 2, dtype=np.float32) / np.float32(32))).astype(np.float32)
    ang = (pos[:, None] * inv_freq[None, :]).astype(np.float32)
    cos = np.cos(ang).astype(np.float32).T
    sin = np.sin(ang).astype(np.float32).T
    cos32 = np.concatenate([cos, cos], axis=0)
    sin32 = np.concatenate([sin, sin], axis=0)
    kk = np.arange(128)[:, None]
    qq = np.arange(128)[None, :]
    tri = np.where(kk <= qq, 0.0, NEG).astype(np.float32)
    mown = np.zeros((2, 128, 256), np.float32)
    mown[0, :, 0:128] = tri
    mown[1, :, 0:128] = NEG
    mown[1, :, 128:256] = tri
    rc = np.broadcast_to((1.0 / np.arange(1, 17, dtype=np.float32))[None, :], (128, 16)).astype(np.float32).copy()
    return dict(c_ident=ident, c_rswap=rsw, c_cos=np.ascontiguousarray(cos32), c_sin=np.ascontiguousarray(sin32), c_mown=mown, c_rc=rc)


_PROG_CACHE = {}


def _get_prog(layers):
    key = tuple(layers)
    if key not in _PROG_CACHE:
        _PROG_CACHE[key] = build_program(list(layers))
    return _PROG_CACHE[key]


def _run(layers, x, mem, shared):
    nc = _get_prog(layers)
    in_maps = []
    for c in range(8):
        m = dict(shared)
        m["x"] = np.ascontiguousarray(x[c])
        m["mem"] = np.ascontiguousarray(mem[c])
        in_maps.append(m)
    res = run_bass_kernel_spmd(nc, in_maps, core_ids=list(range(8)))
    return np.stack([np.asarray(r["out"]) for r in res.results], axis=0)


LAUNCH_GROUPS = [[0, 1, 2, 3]]


def kernel(x, mem, norm_gains, mem_norm, pool_w_in, pool_w_group, pool_scale,
           moba_w_qkv, moba_w_o, xa_w_q, xa_w_kv, xa_w_o, mlp_w1, mlp_w2):
    f = lambda a: np.ascontiguousarray(np.asarray(a, dtype=np.float32))
    x = f(x)
    mem = f(mem)
    gains = np.concatenate([f(norm_gains).reshape(24, D), f(mem_norm).reshape(4, D), f(pool_scale).reshape(2, D)], axis=0)
    shared = dict(gains=np.ascontiguousarray(gains), pool_w_in=f(pool_w_in), pool_w_group=f(pool_w_group),
                  moba_w_qkv=f(moba_w_qkv), moba_w_o=f(moba_w_o), xa_w_q=f(xa_w_q), xa_w_kv=f(xa_w_kv),
                  xa_w_o=f(xa_w_o), mlp_w1=f(mlp_w1), mlp_w2=f(mlp_w2))
    shared.update(_consts())
    cur = x
    for grp in LAUNCH_GROUPS:
        cur = _run(grp, cur, mem, shared)
    return cur.astype(np.float32)
```

```python
import numpy as np
from contextlib import ExitStack
import concourse.bass as bass
import concourse.mybir as mybir
from concourse.bass_utils import run_bass_kernel_spmd

F32 = mybir.dt.float32
BF16 = mybir.dt.bfloat16
ALU = mybir.AluOpType
AF = mybir.ActivationFunctionType
AX = mybir.AxisListType

ENGS = ("pe", "act", "dve", "pool", "sp")


def I(method, **kw):
    return (method, kw)

D = 1024
KC = 8
SEQ = 2048
NMEM = 256
DEPTH = 4
DFF = 4096
NEG = -30000.0
WINDOWS = (2, 4, 8, 16)
EPS = 1e-6
DBG = set()


class Op:
    __slots__ = ("eng", "fn", "deps", "slot", "seq", "epoch", "ndep", "gid")

    def __init__(self, eng, fn, slot, epoch, gid):
        self.eng = eng
        self.fn = fn
        self.deps = []
        self.slot = slot
        self.seq = None
        self.epoch = epoch
        self.ndep = 0
        self.gid = gid


class Sched:
    def __init__(self):
        self.ops = {e: [] for e in ENGS}
        self.lw = {}
        self.rd = {}
        self.epoch = 0
        self.gid = 0
        self.last = {e: None for e in ENGS}
        self.barrier_deps = []
        self.dma_ops = []
        self.lastslot = {}

    def add(self, eng, fn, reads=(), writes=(), slot=None, nobarrier=False):
        o = Op(eng, fn, slot, self.epoch, self.gid)
        self.gid += 1
        deps = {}
        for k in reads:
            w = self.lw.get(k)
            if w is not None:
                deps[w.gid] = w
        for k in writes:
            w = self.lw.get(k)
            if w is not None:
                deps[w.gid] = w
            for r in self.rd.get(k, {}).values():
                deps[r.gid] = r
        if not nobarrier:
            for b in self.barrier_deps:
                deps[b.gid] = b
        if eng == "pe" and slot is None:
            deps = {g: d for g, d in deps.items() if not (d.eng == "pe" and d.slot is None)}
        o.deps = list(deps.values())
        for d in o.deps:
            d.ndep += 1
        rk = eng if slot is None else ("dma", o.gid)
        for k in reads:
            self.rd.setdefault(k, {})[rk] = o
        for k in writes:
            self.lw[k] = o
            self.rd[k] = {}
        self.ops[eng].append(o)
        if slot is not None:
            self.dma_ops.append(o)
            self.lastslot[slot] = o
        else:
            self.last[eng] = o
        return o

    def barrier(self, keep=()):
        deps = {}
        for e in ENGS:
            if self.last[e] is not None:
                deps[self.last[e].gid] = self.last[e]
        for o in self.lastslot.values():
            deps[o.gid] = o
        self.barrier_deps = list(deps.values())
        keep = set(keep) | {"WA", "WB"}
        self.lw = {k: v for k, v in self.lw.items() if k in keep}
        self.rd = {k: v for k, v in self.rd.items() if k in keep}

    def emit(self, nc, stack, final_wait_slots=()):
        n_epochs = self.epoch + 1
        sems = {}
        for e in ("pe", "act", "dve", "pool"):
            for ep in range(n_epochs):
                sems[(e, ep)] = stack.enter_context(nc.semaphore(f"s_{e}_{ep}"))
        slot_names = []
        for o in self.dma_ops:
            if o.slot not in slot_names:
                slot_names.append(o.slot)
        for s in slot_names:
            sems[("dma", s)] = stack.enter_context(nc.semaphore(f"d_{s}"))
        cnt = {}
        for e in ENGS:
            for o in self.ops[e]:
                if o.slot is not None:
                    k = ("dma", o.slot)
                    cnt[k] = cnt.get(k, 0) + 16
                    o.seq = cnt[k]
                elif o.ndep > 0:
                    k = (e, o.epoch)
                    cnt[k] = cnt.get(k, 0) + 1
                    o.seq = cnt[k]
        self.sem_counts = cnt

        def semkey(d):
            return ("dma", d.slot) if d.slot is not None else (d.eng, d.epoch)

        block = stack.enter_context(nc.Block())
        engobj = {"pe": "tensor", "act": "scalar", "dve": "vector", "pool": "gpsimd", "sp": "sync"}

        def run(e, eng):
            waited = {}
            for o in self.ops[e]:
                for d in o.deps:
                    if d.slot is None and d.eng == e and e == "pe":
                        continue
                    k = semkey(d)
                    if waited.get(k, 0) >= d.seq:
                        continue
                    eng.wait_ge(sems[k], d.seq)
                    waited[k] = d.seq
                ins = getattr(eng, o.fn[0])(**o.fn[1])
                if o.slot is not None:
                    ins.then_inc(sems[("dma", o.slot)], 16)
                elif o.ndep > 0:
                    ins.then_inc(sems[(e, o.epoch)], 1)
            if e == "sp":
                for s in final_wait_slots:
                    k = ("dma", s)
                    eng.wait_ge(sems[k], cnt[k])

        for e in ENGS:
            if not self.ops[e] and e != "sp":
                continue
            deco = getattr(block, engobj[e])

            def _f(eng, e=e):
                run(e, eng)
            deco(_f)


class Arena:
    def __init__(self, tensor, nwords):
        self.t = tensor
        self.n = nwords
        self.off = 0

    def f32(self, *shape):
        n = int(np.prod(shape))
        n_al = (n + 7) // 8 * 8
        assert self.off + n_al <= self.n, f"arena overflow {self.off}+{n_al}>{self.n}"
        a = self.t[:, self.off:self.off + n]
        self.off += n_al
        return self._shape(a, shape)

    def bf16(self, *shape):
        n = int(np.prod(shape))
        assert n % 2 == 0
        nw = n // 2
        n_al = (nw + 7) // 8 * 8
        assert self.off + n_al <= self.n, f"arena overflow {self.off}+{n_al}>{self.n}"
        a = self.t[:, self.off:self.off + nw].bitcast(BF16)
        self.off += n_al
        return self._shape(a, shape)

    @staticmethod
    def _shape(a, shape):
        if len(shape) == 1:
            return a
        if len(shape) == 2:
            return a.rearrange("p (a b) -> p a b", a=shape[0])
        if len(shape) == 3:
            return a.rearrange("p (a b c) -> p a b c", a=shape[0], b=shape[1])
        raise ValueError

    def mark(self):
        return self.off

    def release(self, m):
        self.off = m


def build_program(layers, first=True, last=True):
    nc = bass.Bass("TRN2", target_bir_lowering=False)

    def din(name, shape):
        return nc.dram_tensor(name, list(shape), F32, kind="ExternalInput").ap()

    x_d = din("x", (SEQ, D))
    mem_d = din("mem", (NMEM, D))
    gains_d = din("gains", (30, D))
    pool_w_in = din("pool_w_in", (2, D, D))
    pool_w_group = din("pool_w_group", (2, 4, 256, 256))
    moba_w_qkv = din("moba_w_qkv", (2, D, 3 * D))
    moba_w_o = din("moba_w_o", (2, D, D))
    xa_w_q = din("xa_w_q", (DEPTH, D, D))
    xa_w_kv = din("xa_w_kv", (DEPTH, D, 2 * D))
    xa_w_o = din("xa_w_o", (DEPTH, D, D))
    mlp_w1 = din("mlp_w1", (DEPTH, D, DFF))
    mlp_w2 = din("mlp_w2", (DEPTH, DFF, D))
    ident_d = din("c_ident", (128, 128))
    rsw_d = din("c_rswap", (128, 32))
    cos_d = din("c_cos", (32, SEQ))
    sin_d = din("c_sin", (32, SEQ))
    mown_d = din("c_mown", (2, 128, 256))
    rc_d = din("c_rc", (128, 16))
    out_d = nc.dram_tensor("out", [SEQ, D], F32, kind="ExternalOutput").ap()

    S = Sched()
    st = ExitStack()
    with st:
        NW = 212000 // 4
        arena_t = st.enter_context(nc.sbuf_tensor("arena", [128, NW], F32))
        A = Arena(arena_t, NW)
        banks = [st.enter_context(nc.psum_tensor(f"bank{i}", [128, 512], F32)) for i in range(8)]
        ps_rr = {"all": 0, "lo": 0}

        def ps(pool="all"):
            if pool == "all":
                i = ps_rr["all"] % 8
                ps_rr["all"] += 1
            else:
                i = ps_rr["lo"] % 4
                ps_rr["lo"] += 1
            return banks[i], ("ps", i)

        xT = A.f32(KC, SEQ)
        ident = A.f32(128)
        ident_bf = A.bf16(128)
        ones_bf = A.bf16(128)
        rsw = A.f32(32)
        epsb = A.f32(8)
        rc16 = A.f32(16)
        mown = A.bf16(2, 256)
        gT = A.f32(KC, 32)
        WA = A.bf16(KC, D)
        WB = A.bf16(KC, D)
        WAf = WA.rearrange("p a b -> p (a b)")
        WBf = WB.rearrange("p a b -> p (a b)")
        base_mark = A.mark()

        def xkeys(t512, ks=range(KC)):
            return [("x", k, t512) for k in ks]

        rot = {"sq": 0, "cp": 0}

        def wload(slot_ap, slotkey, src_ap, nk, ncols, nsplit=4):
            src = src_ap.rearrange("(k p) n -> p k n", p=128)
            step = max(1, nk // nsplit)
            for k0 in range(0, nk, step):
                S.add("pool", I("dma_start", out=slot_ap[:, k0:k0 + step, :], in_=src[:, k0:k0 + step, :]),
                      writes=[slotkey], slot="w_" + slotkey, nobarrier=True)

        def rms_stats(src_fn, W, src_reads, tag):
            sq = stats_bufs["sq"]
            bank, bk = ps()
            for k in range(KC):
                i = rot["sq"] % 2
                rot["sq"] += 1
                S.add("act", I("activation", out=sq[i][:, 0:W], in_=src_fn(k), func=AF.Square),
                      reads=[src_reads(k)], writes=[("sq", i)])
                S.add("pe", I("matmul", out=bank[:, 0:W], lhsT=ones_bf, rhs=sq[i][:, 0:W], start=(k == 0), stop=(k == KC - 1)),
                      reads=[("sq", i), "consts"], writes=[bk])
            ln = stats_bufs["ln"]
            rstd = stats_bufs["rstd"]
            S.add("act", I("activation", out=ln[:, 0:W], in_=bank[:, 0:W], func=AF.Ln, scale=1.0 / D, bias=epsb[:, 0:1]),
                  reads=[bk, "consts"], writes=["ln"])
            S.add("act", I("activation", out=rstd[:, 0:W], in_=ln[:, 0:W], func=AF.Exp, scale=-0.5),
                  reads=["ln"], writes=["rstd"])
            return rstd

        def prenorm(xcols, t512, W, gidx, dst, dstkey):
            c0 = xcols
            rstd = rms_stats(lambda k: xT[:, k, c0:c0 + W], W, lambda k: ("x", k, t512), "pre")
            for k in range(KC):
                S.add("dve", I("scalar_tensor_tensor", out=dst[:, k, 0:W], in0=xT[:, k, c0:c0 + W], scalar=gT[:, k, gidx:gidx + 1],
                                                                    in1=rstd[:, 0:W], op0=ALU.mult, op1=ALU.mult),
                      reads=[("x", k, t512), "rstd", "consts"], writes=[(dstkey, k)])

        def postnorm_residual(y, ykey, xcols, t512, W, gidx):
            c0 = xcols
            rstd = rms_stats(lambda k: y[:, k, 0:W], W, lambda k: (ykey, k), "post")
            for k in range(KC):
                S.add("dve", I("scalar_tensor_tensor", out=y[:, k, 0:W], in0=y[:, k, 0:W], scalar=gT[:, k, gidx:gidx + 1],
                                                                    in1=rstd[:, 0:W], op0=ALU.mult, op1=ALU.mult),
                      reads=["rstd", "consts"], writes=[(ykey, k)])
                S.add("pool", I("tensor_tensor", out=xT[:, k, c0:c0 + W], in0=xT[:, k, c0:c0 + W], in1=y[:, k, 0:W], op=ALU.add),
                      reads=[(ykey, k)], writes=[("x", k, t512)])

        def evac_copy(dst, src, reads, writes, scale=None):
            i = rot["cp"] % 2
            rot["cp"] += 1
            if scale is not None or i == 0:
                if scale is None:
                    S.add("act", I("activation", out=dst, in_=src, func=AF.Copy), reads=reads, writes=writes)
                else:
                    S.add("act", I("activation", out=dst, in_=src, func=AF.Copy, scale=scale), reads=reads + ["consts"], writes=writes)
            else:
                S.add("dve", I("tensor_copy", out=dst, in_=src), reads=reads, writes=writes)

        def proj(dst_fn, dstkey_fn, W_slot, wkey, src, srckey, W, nout=KC, col0=0, evac=None):
            for c in range(nout):
                bank, bk = ps()
                for k in range(KC):
                    S.add("pe", I("matmul", out=bank[:, 0:W], lhsT=W_slot[:, k, col0 + c * 128: col0 + (c + 1) * 128],
                                                                         rhs=src[:, k, 0:W], start=(k == 0), stop=(k == KC - 1)),
                          reads=[wkey, (srckey, k)], writes=[bk])
                if evac is None:
                    evac_copy(dst_fn(c), bank[:, 0:W], [bk], [dstkey_fn(c)])
                else:
                    evac(c, bank, bk)

        S.add("sp", I("dma_start", out=ident, in_=ident_d), writes=["c_id"], slot="c_id")
        S.add("sp", I("dma_start", out=rsw, in_=rsw_d), writes=["c_rs"], slot="c_rs")
        S.add("sp", I("dma_start", out=rc16, in_=rc_d), writes=["c_rc"], slot="c_rc")
        S.add("pool", I("dma_start", out=mown, in_=mown_d.rearrange("a p n -> p a n")), writes=["c_mo"], slot="c1")
        S.add("dve", I("tensor_copy", out=ident_bf, in_=ident), reads=["c_id"], writes=["c_idbf"])
        S.add("dve", I("memset", ap=ones_bf, constant=1.0), writes=["c_ones"])
        S.add("dve", I("memset", ap=epsb, constant=EPS), writes=["c_eps"])
        graw = A.f32(D)
        S.add("sp", I("dma_start", out=graw[0:30, :], in_=gains_d), writes=["graw"], slot="c_gr")
        for k in range(KC):
            bank, bk = ps()
            S.add("pe", I("transpose", out=bank[:, 0:30], in_=graw[0:30, k * 128:(k + 1) * 128], identity=ident[0:30, 0:30]),
                  reads=["graw", "c_id"], writes=[bk])
            S.add("dve", I("tensor_copy", out=gT[:, k, 0:30], in_=bank[:, 0:30]), reads=[bk], writes=["c_g"])
        xst = [A.f32(D), A.f32(D)]
        for tt in range(SEQ // 128):
            i = tt % 2
            S.add("sp", I("dma_start", out=xst[i], in_=x_d[tt * 128:(tt + 1) * 128, :]), writes=[("xst", i)], slot=f"xs{i}")
            for hh in range(2):
                bank, bk = ps()
                for kk in range(4):
                    k = hh * 4 + kk
                    S.add("pe", I("transpose", out=bank[:, kk * 128:(kk + 1) * 128], in_=xst[i][:, k * 128:(k + 1) * 128], identity=ident),
                          reads=[("xst", i), "c_id"], writes=[bk])
                dst = xT[:, hh * 4:(hh + 1) * 4, tt * 128:(tt + 1) * 128]
                src = bank[:, 0:512].rearrange("p (a b) -> p a b", a=4)
                evac_copy(dst, src, [bk], [("x", k, tt // 4) for k in range(hh * 4, hh * 4 + 4)])
        S.barrier()
        A.release(base_mark)

        for li in layers:
            S.epoch += 1
            j = li // 2
            g0 = li * 6
            if li % 2 == 0:
                m0 = A.mark()
                stats_bufs = {"sq": [A.bf16(512), A.bf16(512)], "ln": A.f32(512), "rstd": A.f32(512)}
                hnb = [A.bf16(KC, 512), A.bf16(KC, 512)]
                wg = A.bf16(4, 2, 256)
                U = A.f32(KC, 528)
                pp = [[A.f32(528), A.f32(528)] for _ in range(2)]
                pooled = A.bf16(KC, 512)
                fix = A.f32(16)
                y = A.f32(KC, 512)
                wload(WA, "WA", pool_w_in[j], KC, D)
                for g in range(4):
                    S.add("pool", I("dma_start", out=wg[:, g, :, :], in_=pool_w_group[j, g].rearrange("(c p) n -> p c n", p=128)),
                          writes=["wg"], slot="w_wg", nobarrier=False)
                S.add("dve", I("memset", ap=U[:, :, 0:16], constant=0.0), writes=[("U", c) for c in range(KC)])
                prenorm(0, 0, 512, g0 + 0, hnb[0], ("hn", 0))
                for t in range(4):
                    if t + 1 < 4:
                        prenorm((t + 1) * 512, t + 1, 512, g0 + 0, hnb[(t + 1) % 2], ("hn", (t + 1) % 2))
                    hn, hk = hnb[t % 2], ("hn", t % 2)
                    if t > 0:
                        S.add("dve", I("tensor_copy", out=U[:, :, 0:16], in_=U[:, :, 512:528]),
                              reads=[("U", c) for c in range(KC)], writes=[("U", c) for c in range(KC)])
                    proj(lambda c: U[:, c, 16:528], lambda c: ("U", c), WA, "WA", hn, hk, 512)
                    for c in range(KC):
                        w = WINDOWS[c // 2]
                        nl = {2: 1, 4: 2, 8: 3, 16: 4}[w]
                        bufs = pp[c % 2]
                        cur, curkey = U[:, c, :], ("U", c)
                        sh = 1
                        lo = 0
                        for lv in range(nl):
                            nxt, nxtkey = bufs[lv % 2], ("pp", c % 2, lv % 2)
                            lo2 = lo + sh
                            S.add("pool", I("tensor_tensor", out=nxt[:, lo2:528], in0=cur[:, lo2:528],
                                                                                                   in1=cur[:, lo2 - sh:528 - sh], op=ALU.add),
                                  reads=[curkey], writes=[nxtkey])
                            cur, curkey = nxt, nxtkey
                            lo = lo2
                            sh *= 2
                        S.add("dve", I("scalar_tensor_tensor", out=pooled[:, c, :], in0=cur[:, 16:528], scalar=1.0 / w,
                                                                                          in1=U[:, c, 16:528], op0=ALU.mult, op1=ALU.subtract),
                              reads=[curkey, ("U", c)], writes=[("pooled", c)])
                        if t == 0:
                            S.add("dve", I("tensor_tensor", out=fix[:, 0:15], in0=cur[:, 16:31], in1=rc16[:, 0:15], op=ALU.mult),
                                  reads=[curkey, "consts"], writes=["fix"])
                            S.add("dve", I("tensor_tensor", out=pooled[:, c, 0:w - 1], in0=fix[:, 0:w - 1], in1=U[:, c, 16:16 + w - 1], op=ALU.subtract),
                                  reads=["fix", ("U", c)], writes=[("pooled", c)])
                    for g in range(4):
                        for oc in range(2):
                            co = 2 * g + oc
                            bank, bk = ps()
                            for kc in range(2):
                                S.add("pe", I("matmul", out=bank[:, :], lhsT=wg[:, g, kc, oc * 128:(oc + 1) * 128],
                                                                                              rhs=pooled[:, 2 * g + kc, :], start=(kc == 0), stop=(kc == 1)),
                                      reads=["wg", ("pooled", 2 * g + kc)], writes=[bk])
                            evac_copy(y[:, co, :], bank[:, :], [bk], [("y", co)], scale=gT[:, co, 28 + j:29 + j])
                    postnorm_residual(y, "y", t * 512, t, 512, g0 + 1)
                S.barrier()
                A.release(m0)
            else:
                m0 = A.mark()
                kT = A.bf16(KC, SEQ)
                V = A.bf16(16, D)
                kmT = A.bf16(8, 8)
                kms = A.f32(8, 8)
                m1 = A.mark()
                stats_bufs = {"sq": [A.bf16(512), A.bf16(512)], "ln": A.f32(512), "rstd": A.f32(512)}
                hnb = [A.bf16(KC, 512), A.bf16(KC, 512)]
                kf = A.f32(512)
                t1 = A.f32(512)
                t2 = A.f32(512)
                rc_ = A.f32(512)
                rs_ = A.f32(512)
                wload(WA, "WA", moba_w_qkv[j, :, D:2 * D], KC, D)
                wload(WB, "WB", moba_w_qkv[j, :, 2 * D:3 * D], KC, D)

                def rope_part(dst, f32src, bank2, bk2, W, tag):
                    S.add("pe", I("matmul", out=bank2[0:32, 0:W], lhsT=rsw, rhs=f32src[:, 0:W], start=True, stop=True),
                          reads=[tag + "f", "consts"], writes=[bk2])
                    S.add("dve", I("tensor_tensor", out=t1[0:32, 0:W], in0=f32src[0:32, 0:W], in1=rc_[0:32, 0:W], op=ALU.mult),
                          reads=[tag + "f", "ropec"], writes=["t1"])
                    S.add("dve", I("tensor_tensor", out=t2[0:32, 0:W], in0=bank2[0:32, 0:W], in1=rs_[0:32, 0:W], op=ALU.mult),
                          reads=[bk2, "ropes"], writes=["t2"])

                prenorm(0, 0, 512, g0 + 0, hnb[0], ("hn", 0))
                for t in range(4):
                    if t + 1 < 4:
                        prenorm((t + 1) * 512, t + 1, 512, g0 + 0, hnb[(t + 1) % 2], ("hn", (t + 1) % 2))
                    hn, hk = hnb[t % 2], ("hn", t % 2)
                    S.add("sp", I("dma_start", out=rc_[0:32, :], in_=cos_d[:, t * 512:(t + 1) * 512]), writes=["ropec"], slot="rc")
                    S.add("sp", I("dma_start", out=rs_[0:32, :], in_=sin_d[:, t * 512:(t + 1) * 512]), writes=["ropes"], slot="rs")
                    for h in range(0 if "nok" in DBG else 8):
                        bank, bk = ps()
                        for k in range(KC):
                            S.add("pe", I("matmul", out=bank[:, :], lhsT=WA[:, k, h * 128:(h + 1) * 128], rhs=hn[:, k, :],
                                                                                 start=(k == 0), stop=(k == KC - 1)),
                                  reads=["WA", (hk, k)], writes=[bk])
                        S.add("act", I("activation", out=kf, in_=bank[:, :], func=AF.Copy), reads=[bk], writes=["kf"])
                        kdst = kT[:, h, t * 512:(t + 1) * 512]
                        bank2, bk2 = ps()
                        rope_part(None, kf, bank2, bk2, 512, "k")
                        S.add("dve", I("tensor_tensor", out=kf[0:32, :], in0=t1[0:32, :], in1=t2[0:32, :], op=ALU.add),
                              reads=["t1", "t2"], writes=["kf"])
                        S.add("dve", I("tensor_copy", out=kdst, in_=kf), reads=["kf"], writes=[("kT", h)])
                        S.add("dve", I("tensor_reduce", out=kms[:, h, 2 * t:2 * t + 2], in_=kf.rearrange("p (n s) -> p n s", n=2), axis=AX.X, op=ALU.add),
                              reads=["kf"], writes=["kms"])
                    for ts in range(0 if "nov" in DBG else 4):
                        for hf in range(2):
                            bank, bk = ps()
                            for k in range(KC):
                                S.add("pe", I("matmul", out=bank[:, :], lhsT=hn[:, k, ts * 128:(ts + 1) * 128],
                                                                                              rhs=WB[:, k, hf * 512:(hf + 1) * 512], start=(k == 0), stop=(k == KC - 1)),
                                      reads=["WB", (hk, k)], writes=[bk])
                            evac_copy(V[:, t * 4 + ts, hf * 512:(hf + 1) * 512], bank[:, :], [bk], [("V", t * 4 + ts)])
                keep = set([("kT", h) for h in range(8)] + [("V", i) for i in range(16)] + ["kms"])
                S.barrier(keep=keep)
                A.release(m1)
                S.epoch += 1
                NU = 0 if "nop2" in DBG else 8
                GATE_FROM = 99 if "nogate" in DBG else 4
                stats_bufs = {"sq": [A.bf16(256), A.bf16(256)], "ln": A.f32(256), "rstd": A.f32(256)}
                hnb = [A.bf16(KC, 256), A.bf16(KC, 256)]
                qf = A.f32(256)
                t1 = A.f32(256)
                t2 = A.f32(256)
                rc_ = A.f32(256)
                rs_ = A.f32(256)
                qt_ = A.bf16(KC, 256)
                ao = A.bf16(KC, 256)
                y = A.f32(KC, 256)
                pT = [A.bf16(512) for _ in range(3)]
                rec = A.f32(256)
                maskb = A.bf16(2, 8, 8)
                Gs = A.f32(64)
                cmpb = A.f32(8 * 49)
                cntb = A.f32(56)
                wload(WA, "WA", moba_w_qkv[j, :, 0:D], KC, D)
                wload(WB, "WB", moba_w_o[j], KC, D)
                S.add("dve", I("memset", ap=maskb, constant=0.0), writes=["maskb"])
                SC = 128 ** -0.5
                prr = 0
                for u in range(NU):
                    t512 = u // 2
                    c0 = u * 256
                    if u == 0:
                        prenorm(0, 0, 256, g0 + 0, hnb[0], ("hn", 0))
                    if u + 1 < NU:
                        prenorm((u + 1) * 256, (u + 1) // 2, 256, g0 + 0, hnb[(u + 1) % 2], ("hn", (u + 1) % 2))
                    hn, hk = hnb[u % 2], ("hn", u % 2)
                    S.add("sp", I("dma_start", out=rc_[0:32, :], in_=cos_d[:, c0:c0 + 256]), writes=["ropec"], slot="rc")
                    S.add("sp", I("dma_start", out=rs_[0:32, :], in_=sin_d[:, c0:c0 + 256]), writes=["ropes"], slot="rs")
                    for h in range(8):
                        bank, bk = ps("lo")
                        for k in range(KC):
                            S.add("pe", I("matmul", out=bank[:, 0:256], lhsT=WA[:, k, h * 128:(h + 1) * 128], rhs=hn[:, k, :],
                                                                                 start=(k == 0), stop=(k == KC - 1)),
                                  reads=["WA", (hk, k)], writes=[bk])
                        S.add("act", I("activation", out=qf, in_=bank[:, 0:256], func=AF.Copy), reads=[bk], writes=["qf"])
                        bank2, bk2 = ps("lo")
                        rope_part(None, qf, bank2, bk2, 256, "q")
                        S.add("dve", I("tensor_tensor", out=qf[0:32, :], in0=t1[0:32, :], in1=t2[0:32, :], op=ALU.add),
                              reads=["t1", "t2"], writes=["qf"])
                        S.add("dve", I("tensor_copy", out=qt_[:, h, :], in_=qf), reads=["qf"], writes=[("q", h)])
                        if u >= GATE_FROM:
                            for qi in range(2):
                                S.add("pe", I("matmul", out=banks[6 + qi][:, h * 8:(h + 1) * 8], lhsT=qf[:, qi * 128:(qi + 1) * 128],
                                              rhs=kms[:, h, :], start=True, stop=True),
                                      reads=["qf", "kms"], writes=[("ps", 6 + qi)])
                    if u >= GATE_FROM:
                        for qi in range(2):
                            bank, bk = banks[6 + qi], ("ps", 6 + qi)
                            S.add("dve", I("tensor_copy", out=Gs, in_=bank[:, 0:64]), reads=[bk], writes=["Gs"])
                            G3 = Gs.rearrange("p (h n) -> p h n", n=8)
                            cm = cmpb[:, 0:8 * u * u].rearrange("p (h n m) -> p h n m", n=u, m=u)
                            in_m = G3[:, :, 0:u].unsqueeze(2).broadcast_to([128, 8, u, u])
                            in_n = G3[:, :, 0:u].unsqueeze(3).broadcast_to([128, 8, u, u])
                            cn = cntb[:, 0:8 * u].rearrange("p (h n) -> p h n", n=u)
                            S.add("dve", I("tensor_tensor", out=cm, in0=in_m, in1=in_n, op=ALU.is_gt),
                                  reads=["Gs"], writes=["cmp"])
                            S.add("dve", I("tensor_reduce", out=cn, in_=cm, axis=AX.X, op=ALU.add), reads=["cmp"], writes=["cnt"])
                            S.add("dve", I("tensor_single_scalar", out=cn, in_=cn, scalar=3.0, op=ALU.is_ge), reads=["cnt"], writes=["cnt"])
                            S.add("dve", I("tensor_single_scalar", out=maskb[:, qi, :, 0:u], in_=cn, scalar=NEG, op=ALU.mult),
                                  reads=["cnt"], writes=["maskb"])
                    for h in range(8):
                        tiles = [("own", 0), ("own", 1)] + [(n, jj) for n in range(u) for jj in range(2)]
                        pairs = [tiles[i:i + 2] for i in range(0, len(tiles), 2)]
                        bo, bok = banks[4 + 2 * (h % 2)], ("ps", 4 + 2 * (h % 2))
                        br, brk = banks[5 + 2 * (h % 2)], ("ps", 5 + 2 * (h % 2))
                        npairs = len(pairs)

                        def keytile(tl):
                            return (u * 2 + tl[1]) if tl[0] == "own" else (tl[0] * 2 + tl[1])

                        def emit_qk(pi):
                            nonlocal prr
                            bank, bk = ps("lo")
                            pbuf = prr % 3
                            prr += 1
                            for idx, tl in enumerate(pairs[pi]):
                                kt = keytile(tl)
                                reg = bank[:, idx * 256:(idx + 1) * 256]
                                masked = (tl[0] == "own") or (u >= GATE_FROM)
                                S.add("pe", I("matmul", out=reg, lhsT=kT[:, h, kt * 128:(kt + 1) * 128], rhs=qt_[:, h, :],
                                                                                              start=True, stop=(not masked)),
                                      reads=[("kT", h), ("q", h)], writes=[bk])
                                if tl[0] == "own":
                                    S.add("pe", I("matmul", out=reg, lhsT=ident_bf, rhs=mown[:, tl[1], :], start=False, stop=True),
                                          reads=["consts"], writes=[bk])
                                elif u >= GATE_FROM:
                                    for qi in range(2):
                                        S.add("pe", I("matmul", out=reg[:, qi * 128:(qi + 1) * 128],
                                                                                                lhsT=maskb[:, qi, h, tl[0]:tl[0] + 1].broadcast_to([128, 128]),
                                                                                                rhs=ident_bf, start=False, stop=(qi == 1)),
                                              reads=["maskb", "consts"], writes=[bk])
                            S.add("act", I("activation", out=pT[pbuf], in_=bank[:, :], func=AF.Exp, scale=SC),
                                  reads=[bk], writes=[("pT", pbuf)])
                            return pbuf

                        def emit_pv(pi, pbuf):
                            for idx, tl in enumerate(pairs[pi]):
                                kt = keytile(tl)
                                first = (pi == 0 and idx == 0)
                                lastm = (pi == npairs - 1 and idx == 1)
                                S.add("pe", I("matmul", out=
                                    bo[:, 0:256], lhsT=V[:, kt, h * 128:(h + 1) * 128], rhs=pT[pbuf][:, idx * 256:(idx + 1) * 256], start=first, stop=lastm),
                                    reads=[("V", kt), ("pT", pbuf)], writes=[bok])
                                S.add("pe", I("matmul", out=
                                    br[:, 0:256], lhsT=ones_bf, rhs=pT[pbuf][:, idx * 256:(idx + 1) * 256], start=first, stop=lastm),
                                    reads=[("pT", pbuf), "consts"], writes=[brk])

                        pb = emit_qk(0)
                        for pi in range(npairs):
                            pbn = emit_qk(pi + 1) if pi + 1 < npairs else None
                            emit_pv(pi, pb)
                            pb = pbn
                        S.add("dve", I("reciprocal", out=rec, in_=br[:, 0:256]), reads=[brk], writes=["rec"])
                        S.add("dve", I("tensor_tensor", out=ao[:, h, :], in0=bo[:, 0:256], in1=rec, op=ALU.mult),
                              reads=[bok, "rec"], writes=[("ao", h)])
                    for c in range(KC):
                        bank, bk = ps("lo")
                        for k in range(KC):
                            S.add("pe", I("matmul", out=bank[:, 0:256], lhsT=WB[:, k, c * 128:(c + 1) * 128], rhs=ao[:, k, :],
                                                                                 start=(k == 0), stop=(k == KC - 1)),
                                  reads=["WB", ("ao", k)], writes=[bk])
                        evac_copy(y[:, c, :], bank[:, 0:256], [bk], [("y", c)])
                    postnorm_residual(y, "y", c0, t512, 256, g0 + 1)
                S.barrier()
                A.release(m0)

            if "noxm" in DBG:
                continue
            S.epoch += 1
            m0 = A.mark()
            stats_bufs = {"sq": [A.bf16(512), A.bf16(512)], "ln": A.f32(512), "rstd": A.f32(512)}
            mst = A.f32(2, D)
            memT = A.f32(KC, NMEM)
            memn = A.bf16(KC, NMEM)
            kTm = A.bf16(KC, NMEM)
            Vm = A.bf16(2, D)
            hnb = [A.bf16(KC, 512), A.bf16(KC, 512)]
            qx = A.bf16(KC, 512)
            ao = A.bf16(KC, 512)
            y = A.f32(KC, 512)
            pT = [A.bf16(512) for _ in range(4)]
            rec = A.f32(512)
            wload(WA, "WA", xa_w_kv[li, :, 0:D], KC, D)
            wload(WB, "WB", xa_w_kv[li, :, D:2 * D], KC, D)
            S.add("sp", I("dma_start", out=mst, in_=mem_d.rearrange("(a p) d -> p a d", p=128)), writes=["mst"], slot="mst")
            for a in range(2):
                for hh in range(2):
                    bank, bk = ps()
                    for kk in range(4):
                        k = hh * 4 + kk
                        S.add("pe", I("transpose", out=bank[:, kk * 128:(kk + 1) * 128], in_=mst[:, a, k * 128:(k + 1) * 128], identity=ident),
                              reads=["mst", "consts"], writes=[bk])
                    evac_copy(memT[:, hh * 4:(hh + 1) * 4, a * 128:(a + 1) * 128], bank[:, 0:512].rearrange("p (a b) -> p a b", a=4), [bk],
                              [("memT", k) for k in range(hh * 4, hh * 4 + 4)])
            rstd = rms_stats(lambda k: memT[:, k, :], NMEM, lambda k: ("memT", k), "mem")
            for k in range(KC):
                S.add("dve", I("scalar_tensor_tensor", out=memn[:, k, :], in0=memT[:, k, :], scalar=gT[:, k, 24 + li:25 + li],
                                                                    in1=rstd[:, 0:NMEM], op0=ALU.mult, op1=ALU.mult),
                      reads=[("memT", k), "rstd", "consts"], writes=[("memn", k)])
            proj(lambda c: kTm[:, c, :], lambda c: ("kTm", c), WA, "WA", memn, "memn", NMEM)
            for a in range(2):
                for hf in range(2):
                    bank, bk = ps()
                    for k in range(KC):
                        S.add("pe", I("matmul", out=bank[:, :], lhsT=memn[:, k, a * 128:(a + 1) * 128],
                                                                                    rhs=WB[:, k, hf * 512:(hf + 1) * 512], start=(k == 0), stop=(k == KC - 1)),
                              reads=["WB", ("memn", k)], writes=[bk])
                    evac_copy(Vm[:, a, hf * 512:(hf + 1) * 512], bank[:, :], [bk], [("Vm", a)])
            wload(WA, "WA", xa_w_q[li], KC, D)
            wload(WB, "WB", xa_w_o[li], KC, D)
            SCX = 256 ** -0.5
            prr = 0
            prenorm(0, 0, 512, g0 + 2, hnb[0], ("hn", 0))
            for t in range(4):
                if t + 1 < 4:
                    prenorm((t + 1) * 512, t + 1, 512, g0 + 2, hnb[(t + 1) % 2], ("hn", (t + 1) % 2))
                hn, hk = hnb[t % 2], ("hn", t % 2)
                proj(lambda c: qx[:, c, :], lambda c: ("qx", c), WA, "WA", hn, hk, 512)
                for h in range(4):
                    pbs = []
                    for a in range(2):
                        bank, bk = ps()
                        for cc in range(2):
                            S.add("pe", I("matmul", out=bank[:, :], lhsT=kTm[:, 2 * h + cc, a * 128:(a + 1) * 128],
                                                                                        rhs=qx[:, 2 * h + cc, :], start=(cc == 0), stop=(cc == 1)),
                                  reads=[("kTm", 2 * h + cc), ("qx", 2 * h + cc)], writes=[bk])
                        pbuf = prr % 4
                        prr += 1
                        S.add("act", I("activation", out=pT[pbuf], in_=bank[:, :], func=AF.Exp, scale=SCX),
                              reads=[bk], writes=[("pT", pbuf)])
                        pbs.append(pbuf)
                    bankr, bkr = ps()
                    for a in range(2):
                        S.add("pe", I("matmul", out=bankr[:, :], lhsT=ones_bf, rhs=pT[pbs[a]], start=(a == 0), stop=(a == 1)),
                              reads=[("pT", pbs[a]), "consts"], writes=[bkr])
                    S.add("dve", I("reciprocal", out=rec, in_=bankr[:, :]), reads=[bkr], writes=["rec"])
                    for cc in range(2):
                        banko, bko = ps()
                        for a in range(2):
                            S.add("pe", I("matmul", out=banko[:, :], lhsT=Vm[:, a, (2 * h + cc) * 128:(2 * h + cc + 1) * 128],
                                                                                                    rhs=pT[pbs[a]], start=(a == 0), stop=(a == 1)),
                                  reads=[("Vm", a), ("pT", pbs[a])], writes=[bko])
                        S.add("dve", I("tensor_tensor", out=ao[:, 2 * h + cc, :], in0=banko[:, :], in1=rec, op=ALU.mult),
                              reads=[bko, "rec"], writes=[("ao", 2 * h + cc)])
                proj(lambda c: y[:, c, :], lambda c: ("y", c), WB, "WB", ao, "ao", 512)
                postnorm_residual(y, "y", t * 512, t, 512, g0 + 3)
            S.barrier()
            A.release(m0)

            S.epoch += 1
            m0 = A.mark()
            stats_bufs = {"sq": [A.bf16(512), A.bf16(512)], "ln": A.f32(512), "rstd": A.f32(512)}
            hnhb = [A.bf16(KC, 1024), A.bf16(KC, 1024)]
            yacc = A.f32(KC, 1024)
            aT = [A.bf16(4, 1024), A.bf16(4, 1024)]
            rl = [A.f32(512), A.f32(512)]
            rlr = 0
            gcount = 0
            def mlp_prenorm(hf_):
                for tt in range(2):
                    t = hf_ * 2 + tt
                    c0 = t * 512
                    rstd = rms_stats(lambda k, c0=c0: xT[:, k, c0:c0 + 512], 512, lambda k, t=t: ("x", k, t), "pre")
                    for k in range(KC):
                        S.add("dve", I("scalar_tensor_tensor", out=hnhb[hf_][:, k, tt * 512:(tt + 1) * 512], in0=xT[:, k, c0:c0 + 512],
                                       scalar=gT[:, k, g0 + 4:g0 + 5], in1=rstd, op0=ALU.mult, op1=ALU.mult),
                              reads=[("x", k, t), "rstd", "consts"], writes=[("hnh", hf_, k, tt)])

            mlp_prenorm(0)
            for hf in range(2):
                hnh = hnhb[hf]
                if hf == 0:
                    mlp_prenorm(1)
                for gi in range(8):
                    slot, slotf, skey = (WA, WAf, "WA") if gcount % 2 == 0 else (WB, WBf, "WB")
                    ab = gcount % 2
                    gcount += 1
                    w1g = slotf[:, 0:4096].rearrange("p (k n) -> p k n", k=8)
                    w2g = slotf[:, 4096:8192].rearrange("p (k n) -> p k n", k=4)
                    wload(w1g, skey, mlp_w1[li, :, gi * 512:(gi + 1) * 512], 8, 512, nsplit=2)
                    wload(w2g, skey, mlp_w2[li, gi * 512:(gi + 1) * 512, :], 4, D, nsplit=2)
                    for tt in range(2):
                        for fc in range(4):
                            bank, bk = ps()
                            for k in range(KC):
                                S.add("pe", I("matmul", out=bank[:, :], lhsT=w1g[:, k, fc * 128:(fc + 1) * 128],
                                                                                                      rhs=hnh[:, k, tt * 512:(tt + 1) * 512], start=(k == 0), stop=(k == KC - 1)),
                                      reads=[skey, ("hnh", hf, k, tt)], writes=[bk])
                            ri = rlr % 2
                            rlr += 1
                            S.add("act", I("activation", out=rl[ri], in_=bank[:, :], func=AF.Relu), reads=[bk], writes=[("rl", ri)])
                            S.add("pool", I("tensor_tensor", out=aT[ab][:, fc, tt * 512:(tt + 1) * 512], in0=rl[ri], in1=rl[ri], op=ALU.mult),
                                  reads=[("rl", ri)], writes=[("aT", ab, fc, tt)])
                    for tt in range(2):
                        for c in range(KC):
                            bank, bk = ps()
                            for fc in range(4):
                                S.add("pe", I("matmul", out=bank[:, :], lhsT=w2g[:, fc, c * 128:(c + 1) * 128],
                                                                                                             rhs=aT[ab][:, fc, tt * 512:(tt + 1) * 512], start=(fc == 0), stop=(fc == 3)),
                                      reads=[skey, ("aT", ab, fc, tt)], writes=[bk])
                            dst = yacc[:, c, tt * 512:(tt + 1) * 512]
                            if gi == 0:
                                evac_copy(dst, bank[:, :], [bk], [("yacc", c, tt)])
                            else:
                                S.add("dve", I("tensor_tensor", out=dst, in0=bank[:, :], in1=dst, op=ALU.add),
                                      reads=[bk], writes=[("yacc", c, tt)])
                for tt in range(2):
                    t = hf * 2 + tt
                    c0 = t * 512
                    yv = yacc[:, :, tt * 512:(tt + 1) * 512]
                    rstd = rms_stats(lambda k, yv=yv: yv[:, k, :], 512, lambda k, tt=tt: ("yacc", k, tt), "post")
                    for k in range(KC):
                        S.add("dve", I("scalar_tensor_tensor", out=yv[:, k, :], in0=yv[:, k, :], scalar=gT[:, k, g0 + 5:g0 + 6],
                                                                                   in1=rstd, op0=ALU.mult, op1=ALU.mult),
                              reads=["rstd", "consts"], writes=[("yacc", k, tt)])
                        S.add("pool", I("tensor_tensor", out=xT[:, k, c0:c0 + 512], in0=xT[:, k, c0:c0 + 512], in1=yv[:, k, :], op=ALU.add),
                              reads=[("yacc", k, tt)], writes=[("x", k, t)])
            S.barrier()
            A.release(m0)

        ost = [A.f32(D), A.f32(D)]
        for tt in range(SEQ // 128):
            i = tt % 2
            for hh in range(2):
                bank, bk = ps()
                for kk in range(4):
                    k = hh * 4 + kk
                    S.add("pe", I("transpose", out=bank[:, kk * 128:(kk + 1) * 128], in_=xT[:, k, tt * 128:(tt + 1) * 128], identity=ident),
                          reads=[("x", k, tt // 4), "consts"], writes=[bk])
                evac_copy(ost[i][:, hh * 512:(hh + 1) * 512], bank[:, :], [bk], [("ost", i, hh)])
            S.add("sp", I("dma_start", out=out_d[tt * 128:(tt + 1) * 128, :], in_=ost[i]),
                  reads=[("ost", i, 0), ("ost", i, 1)], slot=f"o{i}")
        S.emit(nc, st, final_wait_slots=["o0", "o1"])
    return nc


def _consts():
    ident = np.eye(128, dtype=np.float32)
    rsw = np.zeros((128, 32), np.float32)
    for i in range(16):
        rsw[i + 16, i] = -1.0
        rsw[i, i + 16] = 1.0
    pos = np.arange(SEQ, dtype=np.float32)
    inv_freq = (np.float32(500000.0) ** (-np.arange(0, 32, 2, dtype=np.float32) / np.float32(32))).astype(np.float32)
    ang = (pos[:, None] * inv_freq[None, :]).astype(np.float32)
    cos = np.cos(ang).astype(np.float32).T
    sin = np.sin(ang).astype(np.float32).T
    cos32 = np.concatenate([cos, cos], axis=0)
    sin32 = np.concatenate([sin, sin], axis=0)
    kk = np.arange(128)[:, None]
    qq = np.arange(128)[None, :]
    tri = np.where(kk <= qq, 0.0, NEG).astype(np.float32)
    mown = np.zeros((2, 128, 256), np.float32)
    mown[0, :, 0:128] = tri
    mown[1, :, 0:128] = NEG
    mown[1, :, 128:256] = tri
    rc = np.broadcast_to((1.0 / np.arange(1, 17, dtype=np.float32))[None, :], (128, 16)).astype(np.float32).copy()
    return dict(c_ident=ident, c_rswap=rsw, c_cos=np.ascontiguousarray(cos32), c_sin=np.ascontiguousarray(sin32), c_mown=mown, c_rc=rc)


_PROG_CACHE = {}


def _get_prog(layers):
    key = tuple(layers)
    if key not in _PROG_CACHE:
        _PROG_CACHE[key] = build_program(list(layers))
    return _PROG_CACHE[key]


def _run(layers, x, mem, shared):
    nc = _get_prog(layers)
    in_maps = []
    for c in range(8):
        m = dict(shared)
        m["x"] = np.ascontiguousarray(x[c])
        m["mem"] = np.ascontiguousarray(mem[c])
        in_maps.append(m)
    res = run_bass_kernel_spmd(nc, in_maps, core_ids=list(range(8)))
    return np.stack([np.asarray(r["out"]) for r in res.results], axis=0)


LAUNCH_GROUPS = [[0, 1, 2, 3]]


def kernel(x, mem, norm_gains, mem_norm, pool_w_in, pool_w_group, pool_scale,
           moba_w_qkv, moba_w_o, xa_w_q, xa_w_kv, xa_w_o, mlp_w1, mlp_w2):
    f = lambda a: np.ascontiguousarray(np.asarray(a, dtype=np.float32))
    x = f(x)
    mem = f(mem)
    gains = np.concatenate([f(norm_gains).reshape(24, D), f(mem_norm).reshape(4, D), f(pool_scale).reshape(2, D)], axis=0)
    shared = dict(gains=np.ascontiguousarray(gains), pool_w_in=f(pool_w_in), pool_w_group=f(pool_w_group),
                  moba_w_qkv=f(moba_w_qkv), moba_w_o=f(moba_w_o), xa_w_q=f(xa_w_q), xa_w_kv=f(xa_w_kv),
                  xa_w_o=f(xa_w_o), mlp_w1=f(mlp_w1), mlp_w2=f(mlp_w2))
    shared.update(_consts())
    cur = x
    for grp in LAUNCH_GROUPS:
        cur = _run(grp, cur, mem, shared)
    return cur.astype(np.float32)
```

```python
import numpy as np
from contextlib import ExitStack
import concourse.bass as bass
import concourse.mybir as mybir
from concourse.bass_utils import run_bass_kernel_spmd

F32 = mybir.dt.float32
BF16 = mybir.dt.bfloat16
ALU = mybir.AluOpType
AF = mybir.ActivationFunctionType
AX = mybir.AxisListType

ENGS = ("pe", "act", "dve", "pool", "sp")


def I(method, **kw):
    return (method, kw)

D = 1024
KC = 8
SEQ = 2048
NMEM = 256
DEPTH = 4
DFF = 4096
NEG = -30000.0
WINDOWS = (2, 4, 8, 16)
EPS = 1e-6
DBG = set()


class Op:
    __slots__ = ("eng", "fn", "deps", "slot", "seq", "epoch", "ndep", "gid")

    def __init__(self, eng, fn, slot, epoch, gid):
        self.eng = eng
        self.fn = fn
        self.deps = []
        self.slot = slot
        self.seq = None
        self.epoch = epoch
        self.ndep = 0
        self.gid = gid


class Sched:
    def __init__(self):
        self.ops = {e: [] for e in ENGS}
        self.lw = {}
        self.rd = {}
        self.epoch = 0
        self.gid = 0
        self.last = {e: None for e in ENGS}
        self.barrier_deps = []
        self.dma_ops = []
        self.lastslot = {}

    def add(self, eng, fn, reads=(), writes=(), slot=None, nobarrier=False):
        o = Op(eng, fn, slot, self.epoch, self.gid)
        self.gid += 1
        deps = {}
        for k in reads:
            w = self.lw.get(k)
            if w is not None:
                deps[w.gid] = w
        for k in writes:
            w = self.lw.get(k)
            if w is not None:
                deps[w.gid] = w
            for r in self.rd.get(k, {}).values():
                deps[r.gid] = r
        if not nobarrier:
            for b in self.barrier_deps:
                deps[b.gid] = b
        if eng == "pe" and slot is None:
            deps = {g: d for g, d in deps.items() if not (d.eng == "pe" and d.slot is None)}
        o.deps = list(deps.values())
        for d in o.deps:
            d.ndep += 1
        rk = eng if slot is None else ("dma", o.gid)
        for k in reads:
            self.rd.setdefault(k, {})[rk] = o
        for k in writes:
            self.lw[k] = o
            self.rd[k] = {}
        self.ops[eng].append(o)
        if slot is not None:
            self.dma_ops.append(o)
            self.lastslot[slot] = o
        else:
            self.last[eng] = o
        return o

    def barrier(self, keep=()):
        deps = {}
        for e in ENGS:
            if self.last[e] is not None:
                deps[self.last[e].gid] = self.last[e]
        for o in self.lastslot.values():
            deps[o.gid] = o
        self.barrier_deps = list(deps.values())
        keep = set(keep) | {"WA", "WB"}
        self.lw = {k: v for k, v in self.lw.items() if k in keep}
        self.rd = {k: v for k, v in self.rd.items() if k in keep}

    def emit(self, nc, stack, final_wait_slots=()):
        n_epochs = self.epoch + 1
        sems = {}
        for e in ("pe", "act", "dve", "pool"):
            for ep in range(n_epochs):
                sems[(e, ep)] = stack.enter_context(nc.semaphore(f"s_{e}_{ep}"))
        slot_names = []
        for o in self.dma_ops:
            if o.slot not in slot_names:
                slot_names.append(o.slot)
        for s in slot_names:
            sems[("dma", s)] = stack.enter_context(nc.semaphore(f"d_{s}"))
        cnt = {}
        for e in ENGS:
            for o in self.ops[e]:
                if o.slot is not None:
                    k = ("dma", o.slot)
                    cnt[k] = cnt.get(k, 0) + 16
                    o.seq = cnt[k]
                elif o.ndep > 0:
                    k = (e, o.epoch)
                    cnt[k] = cnt.get(k, 0) + 1
                    o.seq = cnt[k]
        self.sem_counts = cnt

        def semkey(d):
            return ("dma", d.slot) if d.slot is not None else (d.eng, d.epoch)

        block = stack.enter_context(nc.Block())
        engobj = {"pe": "tensor", "act": "scalar", "dve": "vector", "pool": "gpsimd", "sp": "sync"}

        def run(e, eng):
            waited = {}
            for o in self.ops[e]:
                for d in o.deps:
                    if d.slot is None and d.eng == e and e == "pe":
                        continue
                    k = semkey(d)
                    if waited.get(k, 0) >= d.seq:
                        continue
                    eng.wait_ge(sems[k], d.seq)
                    waited[k] = d.seq
                ins = getattr(eng, o.fn[0])(**o.fn[1])
                if o.slot is not None:
                    ins.then_inc(sems[("dma", o.slot)], 16)
                elif o.ndep > 0:
                    ins.then_inc(sems[(e, o.epoch)], 1)
            if e == "sp":
                for s in final_wait_slots:
                    k = ("dma", s)
                    eng.wait_ge(sems[k], cnt[k])

        for e in ENGS:
            if not self.ops[e] and e != "sp":
                continue
            deco = getattr(block, engobj[e])

            def _f(eng, e=e):
                run(e, eng)
            deco(_f)


class Arena:
    def __init__(self, tensor, nwords):
        self.t = tensor
        self.n = nwords
        self.off = 0

    def f32(self, *shape):
        n = int(np.prod(shape))
        n_al = (n + 7) // 8 * 8
        assert self.off + n_al <= self.n, f"arena overflow {self.off}+{n_al}>{self.n}"
        a = self.t[:, self.off:self.off + n]
        self.off += n_al
        return self._shape(a, shape)

    def bf16(self, *shape):
        n = int(np.prod(shape))
        assert n % 2 == 0
        nw = n // 2
        n_al = (nw + 7) // 8 * 8
        assert self.off + n_al <= self.n, f"arena overflow {self.off}+{n_al}>{self.n}"
        a = self.t[:, self.off:self.off + nw].bitcast(BF16)
        self.off += n_al
        return self._shape(a, shape)

    @staticmethod
    def _shape(a, shape):
        if len(shape) == 1:
            return a
        if len(shape) == 2:
            return a.rearrange("p (a b) -> p a b", a=shape[0])
        if len(shape) == 3:
            return a.rearrange("p (a b c) -> p a b c", a=shape[0], b=shape[1])
        raise ValueError

    def mark(self):
        return self.off

    def release(self, m):
        self.off = m


def build_program(layers, first=True, last=True):
    nc = bass.Bass("TRN2", target_bir_lowering=False)

    def din(name, shape):
        return nc.dram_tensor(name, list(shape), F32, kind="ExternalInput").ap()

    x_d = din("x", (SEQ, D))
    mem_d = din("mem", (NMEM, D))
    gains_d = din("gains", (30, D))
    pool_w_in = din("pool_w_in", (2, D, D))
    pool_w_group = din("pool_w_group", (2, 4, 256, 256))
    moba_w_qkv = din("moba_w_qkv", (2, D, 3 * D))
    moba_w_o = din("moba_w_o", (2, D, D))
    xa_w_q = din("xa_w_q", (DEPTH, D, D))
    xa_w_kv = din("xa_w_kv", (DEPTH, D, 2 * D))
    xa_w_o = din("xa_w_o", (DEPTH, D, D))
    mlp_w1 = din("mlp_w1", (DEPTH, D, DFF))
    mlp_w2 = din("mlp_w2", (DEPTH, DFF, D))
    ident_d = din("c_ident", (128, 128))
    rsw_d = din("c_rswap", (128, 32))
    cos_d = din("c_cos", (32, SEQ))
    sin_d = din("c_sin", (32, SEQ))
    mown_d = din("c_mown", (2, 128, 256))
    rc_d = din("c_rc", (128, 16))
    out_d = nc.dram_tensor("out", [SEQ, D], F32, kind="ExternalOutput").ap()

    S = Sched()
    st = ExitStack()
    with st:
        NW = 212000 // 4
        arena_t = st.enter_context(nc.sbuf_tensor("arena", [128, NW], F32))
        A = Arena(arena_t, NW)
        banks = [st.enter_context(nc.psum_tensor(f"bank{i}", [128, 512], F32)) for i in range(8)]
        ps_rr = {"all": 0, "lo": 0}

        def ps(pool="all"):
            if pool == "all":
                i = ps_rr["all"] % 8
                ps_rr["all"] += 1
            else:
                i = ps_rr["lo"] % 4
                ps_rr["lo"] += 1
            return banks[i], ("ps", i)

        xT = A.f32(KC, SEQ)
        ident = A.f32(128)
        ident_bf = A.bf16(128)
        ones_bf = A.bf16(128)
        rsw = A.f32(32)
        epsb = A.f32(8)
        rc16 = A.f32(16)
        mown = A.bf16(2, 256)
        gT = A.f32(KC, 32)
        WA = A.bf16(KC, D)
        WB = A.bf16(KC, D)
        WAf = WA.rearrange("p a b -> p (a b)")
        WBf = WB.rearrange("p a b -> p (a b)")
        base_mark = A.mark()

        def xkeys(t512, ks=range(KC)):
            return [("x", k, t512) for k in ks]

        rot = {"sq": 0, "cp": 0}

        def wload(slot_ap, slotkey, src_ap, nk, ncols, nsplit=4, nb=True):
            src = src_ap.rearrange("(k p) n -> p k n", p=128)
            step = max(1, nk // nsplit)
            for k0 in range(0, nk, step):
                S.add("pool", I("dma_start", out=slot_ap[:, k0:k0 + step, :], in_=src[:, k0:k0 + step, :]),
                      writes=[slotkey], slot="w_" + slotkey, nobarrier=nb)

        def rms_stats(src_fn, W, src_reads, tag):
            sq = stats_bufs["sq"]
            bank, bk = ps()
            for k in range(KC):
                i = rot["sq"] % 2
                rot["sq"] += 1
                S.add("act", I("activation", out=sq[i][:, 0:W], in_=src_fn(k), func=AF.Square),
                      reads=[src_reads(k)], writes=[("sq", i)])
                S.add("pe", I("matmul", out=bank[:, 0:W], lhsT=ones_bf, rhs=sq[i][:, 0:W], start=(k == 0), stop=(k == KC - 1)),
                      reads=[("sq", i), "consts"], writes=[bk])
            ln = stats_bufs["ln"]
            rstd = stats_bufs["rstd"]
            S.add("act", I("activation", out=ln[:, 0:W], in_=bank[:, 0:W], func=AF.Ln, scale=1.0 / D, bias=epsb[:, 0:1]),
                  reads=[bk, "consts"], writes=["ln"])
            S.add("act", I("activation", out=rstd[:, 0:W], in_=ln[:, 0:W], func=AF.Exp, scale=-0.5),
                  reads=["ln"], writes=["rstd"])
            return rstd

        def prenorm(xcols, t512, W, gidx, dst, dstkey):
            c0 = xcols
            rstd = rms_stats(lambda k: xT[:, k, c0:c0 + W], W, lambda k: ("x", k, t512), "pre")
            for k in range(KC):
                S.add("dve", I("scalar_tensor_tensor", out=dst[:, k, 0:W], in0=xT[:, k, c0:c0 + W], scalar=gT[:, k, gidx:gidx + 1],
                                                                    in1=rstd[:, 0:W], op0=ALU.mult, op1=ALU.mult),
                      reads=[("x", k, t512), "rstd", "consts"], writes=[(dstkey, k)])

        def postnorm_residual(y, ykey, xcols, t512, W, gidx):
            c0 = xcols
            rstd = rms_stats(lambda k: y[:, k, 0:W], W, lambda k: (ykey, k), "post")
            for k in range(KC):
                S.add("dve", I("scalar_tensor_tensor", out=y[:, k, 0:W], in0=y[:, k, 0:W], scalar=gT[:, k, gidx:gidx + 1],
                                                                    in1=rstd[:, 0:W], op0=ALU.mult, op1=ALU.mult),
                      reads=["rstd", "consts"], writes=[(ykey, k)])
                S.add("pool", I("tensor_tensor", out=xT[:, k, c0:c0 + W], in0=xT[:, k, c0:c0 + W], in1=y[:, k, 0:W], op=ALU.add),
                      reads=[(ykey, k)], writes=[("x", k, t512)])

        def evac_copy(dst, src, reads, writes, scale=None):
            i = rot["cp"] % 2
            rot["cp"] += 1
            if scale is not None or i == 0:
                if scale is None:
                    S.add("act", I("activation", out=dst, in_=src, func=AF.Copy), reads=reads, writes=writes)
                else:
                    S.add("act", I("activation", out=dst, in_=src, func=AF.Copy, scale=scale), reads=reads + ["consts"], writes=writes)
            else:
                S.add("dve", I("tensor_copy", out=dst, in_=src), reads=reads, writes=writes)

        def proj(dst_fn, dstkey_fn, W_slot, wkey, src, srckey, W, nout=KC, col0=0, evac=None):
            for c in range(nout):
                bank, bk = ps()
                for k in range(KC):
                    S.add("pe", I("matmul", out=bank[:, 0:W], lhsT=W_slot[:, k, col0 + c * 128: col0 + (c + 1) * 128],
                                                                         rhs=src[:, k, 0:W], start=(k == 0), stop=(k == KC - 1)),
                          reads=[wkey, (srckey, k)], writes=[bk])
                if evac is None:
                    evac_copy(dst_fn(c), bank[:, 0:W], [bk], [dstkey_fn(c)])
                else:
                    evac(c, bank, bk)

        S.add("sp", I("dma_start", out=ident, in_=ident_d), writes=["c_id"], slot="c_id")
        S.add("sp", I("dma_start", out=rsw, in_=rsw_d), writes=["c_rs"], slot="c_rs")
        S.add("sp", I("dma_start", out=rc16, in_=rc_d), writes=["c_rc"], slot="c_rc")
        S.add("pool", I("dma_start", out=mown, in_=mown_d.rearrange("a p n -> p a n")), writes=["c_mo"], slot="c1")
        S.add("dve", I("tensor_copy", out=ident_bf, in_=ident), reads=["c_id"], writes=["c_idbf"])
        S.add("dve", I("memset", ap=ones_bf, constant=1.0), writes=["c_ones"])
        S.add("dve", I("memset", ap=epsb, constant=EPS), writes=["c_eps"])
        graw = A.f32(D)
        S.add("sp", I("dma_start", out=graw[0:30, :], in_=gains_d), writes=["graw"], slot="c_gr")
        for k in range(KC):
            bank, bk = ps()
            S.add("pe", I("transpose", out=bank[:, 0:30], in_=graw[0:30, k * 128:(k + 1) * 128], identity=ident[0:30, 0:30]),
                  reads=["graw", "c_id"], writes=[bk])
            S.add("dve", I("tensor_copy", out=gT[:, k, 0:30], in_=bank[:, 0:30]), reads=[bk], writes=["c_g"])
        xst = [A.f32(D), A.f32(D)]
        for tt in range(SEQ // 128):
            i = tt % 2
            S.add("sp", I("dma_start", out=xst[i], in_=x_d[tt * 128:(tt + 1) * 128, :]), writes=[("xst", i)], slot=f"xs{i}")
            for hh in range(2):
                bank, bk = ps()
                for kk in range(4):
                    k = hh * 4 + kk
                    S.add("pe", I("transpose", out=bank[:, kk * 128:(kk + 1) * 128], in_=xst[i][:, k * 128:(k + 1) * 128], identity=ident),
                          reads=[("xst", i), "c_id"], writes=[bk])
                dst = xT[:, hh * 4:(hh + 1) * 4, tt * 128:(tt + 1) * 128]
                src = bank[:, 0:512].rearrange("p (a b) -> p a b", a=4)
                evac_copy(dst, src, [bk], [("x", k, tt // 4) for k in range(hh * 4, hh * 4 + 4)])
        S.barrier()
        A.release(base_mark)

        for li in layers:
            S.epoch += 1
            j = li // 2
            g0 = li * 6
            if li % 2 == 0:
                m0 = A.mark()
                stats_bufs = {"sq": [A.bf16(512), A.bf16(512)], "ln": A.f32(512), "rstd": A.f32(512)}
                hnb = [A.bf16(KC, 512), A.bf16(KC, 512)]
                wg = A.bf16(4, 2, 256)
                U = A.f32(KC, 528)
                pp = [[A.f32(528), A.f32(528)] for _ in range(2)]
                pooled = A.bf16(KC, 512)
                fix = A.f32(16)
                y = A.f32(KC, 512)
                wload(WA, "WA", pool_w_in[j], KC, D)
                for g in range(4):
                    S.add("pool", I("dma_start", out=wg[:, g, :, :], in_=pool_w_group[j, g].rearrange("(c p) n -> p c n", p=128)),
                          writes=["wg"], slot="w_wg", nobarrier=False)
                S.add("dve", I("memset", ap=U[:, :, 0:16], constant=0.0), writes=[("U", c) for c in range(KC)])
                prenorm(0, 0, 512, g0 + 0, hnb[0], ("hn", 0))
                for t in range(4):
                    if t + 1 < 4:
                        prenorm((t + 1) * 512, t + 1, 512, g0 + 0, hnb[(t + 1) % 2], ("hn", (t + 1) % 2))
                    hn, hk = hnb[t % 2], ("hn", t % 2)
                    if t > 0:
                        S.add("dve", I("tensor_copy", out=U[:, :, 0:16], in_=U[:, :, 512:528]),
                              reads=[("U", c) for c in range(KC)], writes=[("U", c) for c in range(KC)])
                    proj(lambda c: U[:, c, 16:528], lambda c: ("U", c), WA, "WA", hn, hk, 512)
                    for c in range(KC):
                        w = WINDOWS[c // 2]
                        nl = {2: 1, 4: 2, 8: 3, 16: 4}[w]
                        bufs = pp[c % 2]
                        cur, curkey = U[:, c, :], ("U", c)
                        sh = 1
                        lo = 0
                        for lv in range(nl):
                            nxt, nxtkey = bufs[lv % 2], ("pp", c % 2, lv % 2)
                            lo2 = lo + sh
                            S.add("pool", I("tensor_tensor", out=nxt[:, lo2:528], in0=cur[:, lo2:528],
                                                                                                   in1=cur[:, lo2 - sh:528 - sh], op=ALU.add),
                                  reads=[curkey], writes=[nxtkey])
                            cur, curkey = nxt, nxtkey
                            lo = lo2
                            sh *= 2
                        S.add("dve", I("scalar_tensor_tensor", out=pooled[:, c, :], in0=cur[:, 16:528], scalar=1.0 / w,
                                                                                          in1=U[:, c, 16:528], op0=ALU.mult, op1=ALU.subtract),
                              reads=[curkey, ("U", c)], writes=[("pooled", c)])
                        if t == 0:
                            S.add("dve", I("tensor_tensor", out=fix[:, 0:15], in0=cur[:, 16:31], in1=rc16[:, 0:15], op=ALU.mult),
                                  reads=[curkey, "consts"], writes=["fix"])
                            S.add("dve", I("tensor_tensor", out=pooled[:, c, 0:w - 1], in0=fix[:, 0:w - 1], in1=U[:, c, 16:16 + w - 1], op=ALU.subtract),
                                  reads=["fix", ("U", c)], writes=[("pooled", c)])
                    for g in range(4):
                        for oc in range(2):
                            co = 2 * g + oc
                            bank, bk = ps()
                            for kc in range(2):
                                S.add("pe", I("matmul", out=bank[:, :], lhsT=wg[:, g, kc, oc * 128:(oc + 1) * 128],
                                                                                              rhs=pooled[:, 2 * g + kc, :], start=(kc == 0), stop=(kc == 1)),
                                      reads=["wg", ("pooled", 2 * g + kc)], writes=[bk])
                            evac_copy(y[:, co, :], bank[:, :], [bk], [("y", co)], scale=gT[:, co, 28 + j:29 + j])
                    postnorm_residual(y, "y", t * 512, t, 512, g0 + 1)
                S.barrier()
                A.release(m0)
            else:
                m0 = A.mark()
                kT = A.bf16(KC, SEQ)
                V = A.bf16(16, D)
                kmT = A.bf16(8, 8)
                kms = A.f32(8, 8)
                m1 = A.mark()
                stats_bufs = {"sq": [A.bf16(512), A.bf16(512)], "ln": A.f32(512), "rstd": A.f32(512)}
                hnb = [A.bf16(KC, 512), A.bf16(KC, 512)]
                kf = A.f32(512)
                t1 = A.f32(512)
                t2 = A.f32(512)
                rc_ = A.f32(512)
                rs_ = A.f32(512)
                wload(WA, "WA", moba_w_qkv[j, :, D:2 * D], KC, D)
                wload(WB, "WB", moba_w_qkv[j, :, 2 * D:3 * D], KC, D)

                def rope_part(dst, f32src, bank2, bk2, W, tag):
                    S.add("pe", I("matmul", out=bank2[0:32, 0:W], lhsT=rsw, rhs=f32src[:, 0:W], start=True, stop=True),
                          reads=[tag + "f", "consts"], writes=[bk2])
                    S.add("dve", I("tensor_tensor", out=t1[0:32, 0:W], in0=f32src[0:32, 0:W], in1=rc_[0:32, 0:W], op=ALU.mult),
                          reads=[tag + "f", "ropec"], writes=["t1"])
                    S.add("dve", I("tensor_tensor", out=t2[0:32, 0:W], in0=bank2[0:32, 0:W], in1=rs_[0:32, 0:W], op=ALU.mult),
                          reads=[bk2, "ropes"], writes=["t2"])

                prenorm(0, 0, 512, g0 + 0, hnb[0], ("hn", 0))
                for t in range(4):
                    if t + 1 < 4:
                        prenorm((t + 1) * 512, t + 1, 512, g0 + 0, hnb[(t + 1) % 2], ("hn", (t + 1) % 2))
                    hn, hk = hnb[t % 2], ("hn", t % 2)
                    S.add("sp", I("dma_start", out=rc_[0:32, :], in_=cos_d[:, t * 512:(t + 1) * 512]), writes=["ropec"], slot="rc")
                    S.add("sp", I("dma_start", out=rs_[0:32, :], in_=sin_d[:, t * 512:(t + 1) * 512]), writes=["ropes"], slot="rs")
                    for h in range(0 if "nok" in DBG else 8):
                        bank, bk = ps()
                        for k in range(KC):
                            S.add("pe", I("matmul", out=bank[:, :], lhsT=WA[:, k, h * 128:(h + 1) * 128], rhs=hn[:, k, :],
                                                                                 start=(k == 0), stop=(k == KC - 1)),
                                  reads=["WA", (hk, k)], writes=[bk])
                        S.add("act", I("activation", out=kf, in_=bank[:, :], func=AF.Copy), reads=[bk], writes=["kf"])
                        kdst = kT[:, h, t * 512:(t + 1) * 512]
                        bank2, bk2 = ps()
                        rope_part(None, kf, bank2, bk2, 512, "k")
                        S.add("dve", I("tensor_tensor", out=kf[0:32, :], in0=t1[0:32, :], in1=t2[0:32, :], op=ALU.add),
                              reads=["t1", "t2"], writes=["kf"])
                        S.add("dve", I("tensor_copy", out=kdst, in_=kf), reads=["kf"], writes=[("kT", h)])
                        S.add("dve", I("tensor_reduce", out=kms[:, h, 2 * t:2 * t + 2], in_=kf.rearrange("p (n s) -> p n s", n=2), axis=AX.X, op=ALU.add),
                              reads=["kf"], writes=["kms"])
                    for ts in range(0 if "nov" in DBG else 4):
                        for hf in range(2):
                            bank, bk = ps()
                            for k in range(KC):
                                S.add("pe", I("matmul", out=bank[:, :], lhsT=hn[:, k, ts * 128:(ts + 1) * 128],
                                                                                              rhs=WB[:, k, hf * 512:(hf + 1) * 512], start=(k == 0), stop=(k == KC - 1)),
                                      reads=["WB", (hk, k)], writes=[bk])
                            evac_copy(V[:, t * 4 + ts, hf * 512:(hf + 1) * 512], bank[:, :], [bk], [("V", t * 4 + ts)])
                keep = set([("kT", h) for h in range(8)] + [("V", i) for i in range(16)] + ["kms"])
                S.barrier(keep=keep)
                A.release(m1)
                S.epoch += 1
                NU = 0 if "nop2" in DBG else 8
                GATE_FROM = 99 if "nogate" in DBG else 4
                stats_bufs = {"sq": [A.bf16(256), A.bf16(256)], "ln": A.f32(256), "rstd": A.f32(256)}
                hnb = [A.bf16(KC, 256), A.bf16(KC, 256)]
                qf = A.f32(256)
                t1 = A.f32(256)
                t2 = A.f32(256)
                rc_ = A.f32(256)
                rs_ = A.f32(256)
                qt_ = A.bf16(KC, 256)
                ao = A.bf16(KC, 256)
                y = A.f32(KC, 256)
                pT = [A.bf16(512) for _ in range(3)]
                rec = A.f32(256)
                maskb = A.bf16(2, 8, 8)
                Gs = A.f32(64)
                cmpb = A.f32(8 * 49)
                cntb = A.f32(56)
                wload(WA, "WA", moba_w_qkv[j, :, 0:D], KC, D)
                wload(WB, "WB", moba_w_o[j], KC, D)
                S.add("dve", I("memset", ap=maskb, constant=0.0), writes=["maskb"])
                SC = 128 ** -0.5
                prr = 0
                for u in range(NU):
                    t512 = u // 2
                    c0 = u * 256
                    if u == 0:
                        prenorm(0, 0, 256, g0 + 0, hnb[0], ("hn", 0))
                    if u + 1 < NU:
                        prenorm((u + 1) * 256, (u + 1) // 2, 256, g0 + 0, hnb[(u + 1) % 2], ("hn", (u + 1) % 2))
                    hn, hk = hnb[u % 2], ("hn", u % 2)
                    S.add("sp", I("dma_start", out=rc_[0:32, :], in_=cos_d[:, c0:c0 + 256]), writes=["ropec"], slot="rc")
                    S.add("sp", I("dma_start", out=rs_[0:32, :], in_=sin_d[:, c0:c0 + 256]), writes=["ropes"], slot="rs")
                    for h in range(8):
                        bank, bk = ps("lo")
                        for k in range(KC):
                            S.add("pe", I("matmul", out=bank[:, 0:256], lhsT=WA[:, k, h * 128:(h + 1) * 128], rhs=hn[:, k, :],
                                                                                 start=(k == 0), stop=(k == KC - 1)),
                                  reads=["WA", (hk, k)], writes=[bk])
                        S.add("act", I("activation", out=qf, in_=bank[:, 0:256], func=AF.Copy), reads=[bk], writes=["qf"])
                        bank2, bk2 = ps("lo")
                        rope_part(None, qf, bank2, bk2, 256, "q")
                        S.add("dve", I("tensor_tensor", out=qf[0:32, :], in0=t1[0:32, :], in1=t2[0:32, :], op=ALU.add),
                              reads=["t1", "t2"], writes=["qf"])
                        S.add("dve", I("tensor_copy", out=qt_[:, h, :], in_=qf), reads=["qf"], writes=[("q", h)])
                        if u >= GATE_FROM:
                            for qi in range(2):
                                S.add("pe", I("matmul", out=banks[6 + qi][:, h * 8:(h + 1) * 8], lhsT=qf[:, qi * 128:(qi + 1) * 128],
                                              rhs=kms[:, h, :], start=True, stop=True),
                                      reads=["qf", "kms"], writes=[("ps", 6 + qi)])
                    if u >= GATE_FROM:
                        for qi in range(2):
                            bank, bk = banks[6 + qi], ("ps", 6 + qi)
                            S.add("dve", I("tensor_copy", out=Gs, in_=bank[:, 0:64]), reads=[bk], writes=["Gs"])
                            G3 = Gs.rearrange("p (h n) -> p h n", n=8)
                            cm = cmpb[:, 0:8 * u * u].rearrange("p (h n m) -> p h n m", n=u, m=u)
                            in_m = G3[:, :, 0:u].unsqueeze(2).broadcast_to([128, 8, u, u])
                            in_n = G3[:, :, 0:u].unsqueeze(3).broadcast_to([128, 8, u, u])
                            cn = cntb[:, 0:8 * u].rearrange("p (h n) -> p h n", n=u)
                            S.add("dve", I("tensor_tensor", out=cm, in0=in_m, in1=in_n, op=ALU.is_gt),
                                  reads=["Gs"], writes=["cmp"])
                            S.add("dve", I("tensor_reduce", out=cn, in_=cm, axis=AX.X, op=ALU.add), reads=["cmp"], writes=["cnt"])
                            S.add("dve", I("tensor_single_scalar", out=cn, in_=cn, scalar=3.0, op=ALU.is_ge), reads=["cnt"], writes=["cnt"])
                            S.add("dve", I("tensor_single_scalar", out=maskb[:, qi, :, 0:u], in_=cn, scalar=NEG, op=ALU.mult),
                                  reads=["cnt"], writes=["maskb"])
                    for h in range(8):
                        tiles = [("own", 0), ("own", 1)] + [(n, jj) for n in range(u) for jj in range(2)]
                        pairs = [tiles[i:i + 2] for i in range(0, len(tiles), 2)]
                        bo, bok = banks[4 + 2 * (h % 2)], ("ps", 4 + 2 * (h % 2))
                        br, brk = banks[5 + 2 * (h % 2)], ("ps", 5 + 2 * (h % 2))
                        npairs = len(pairs)

                        def keytile(tl):
                            return (u * 2 + tl[1]) if tl[0] == "own" else (tl[0] * 2 + tl[1])

                        def emit_qk(pi):
                            nonlocal prr
                            bank, bk = ps("lo")
                            pbuf = prr % 3
                            prr += 1
                            for idx, tl in enumerate(pairs[pi]):
                                kt = keytile(tl)
                                reg = bank[:, idx * 256:(idx + 1) * 256]
                                masked = (tl[0] == "own") or (u >= GATE_FROM)
                                S.add("pe", I("matmul", out=reg, lhsT=kT[:, h, kt * 128:(kt + 1) * 128], rhs=qt_[:, h, :],
                                                                                              start=True, stop=(not masked)),
                                      reads=[("kT", h), ("q", h)], writes=[bk])
                                if tl[0] == "own":
                                    S.add("pe", I("matmul", out=reg, lhsT=ident_bf, rhs=mown[:, tl[1], :], start=False, stop=True),
                                          reads=["consts"], writes=[bk])
                                elif u >= GATE_FROM:
                                    for qi in range(2):
                                        S.add("pe", I("matmul", out=reg[:, qi * 128:(qi + 1) * 128],
                                                                                                lhsT=maskb[:, qi, h, tl[0]:tl[0] + 1].broadcast_to([128, 128]),
                                                                                                rhs=ident_bf, start=False, stop=(qi == 1)),
                                              reads=["maskb", "consts"], writes=[bk])
                            S.add("act", I("activation", out=pT[pbuf], in_=bank[:, :], func=AF.Exp, scale=SC),
                                  reads=[bk], writes=[("pT", pbuf)])
                            return pbuf

                        def emit_pv(pi, pbuf):
                            for idx, tl in enumerate(pairs[pi]):
                                kt = keytile(tl)
                                first = (pi == 0 and idx == 0)
                                lastm = (pi == npairs - 1 and idx == 1)
                                S.add("pe", I("matmul", out=
                                    bo[:, 0:256], lhsT=V[:, kt, h * 128:(h + 1) * 128], rhs=pT[pbuf][:, idx * 256:(idx + 1) * 256], start=first, stop=lastm),
                                    reads=[("V", kt), ("pT", pbuf)], writes=[bok])
                                S.add("pe", I("matmul", out=
                                    br[:, 0:256], lhsT=ones_bf, rhs=pT[pbuf][:, idx * 256:(idx + 1) * 256], start=first, stop=lastm),
                                    reads=[("pT", pbuf), "consts"], writes=[brk])

                        pb = emit_qk(0)
                        for pi in range(npairs):
                            pbn = emit_qk(pi + 1) if pi + 1 < npairs else None
                            emit_pv(pi, pb)
                            pb = pbn
                        S.add("dve", I("reciprocal", out=rec, in_=br[:, 0:256]), reads=[brk], writes=["rec"])
                        S.add("dve", I("tensor_tensor", out=ao[:, h, :], in0=bo[:, 0:256], in1=rec, op=ALU.mult),
                              reads=[bok, "rec"], writes=[("ao", h)])
                    for c in range(KC):
                        bank, bk = ps("lo")
                        for k in range(KC):
                            S.add("pe", I("matmul", out=bank[:, 0:256], lhsT=WB[:, k, c * 128:(c + 1) * 128], rhs=ao[:, k, :],
                                                                                 start=(k == 0), stop=(k == KC - 1)),
                                  reads=["WB", ("ao", k)], writes=[bk])
                        evac_copy(y[:, c, :], bank[:, 0:256], [bk], [("y", c)])
                    postnorm_residual(y, "y", c0, t512, 256, g0 + 1)
                S.barrier()
                A.release(m0)

            if "noxm" in DBG:
                continue
            S.epoch += 1
            m0 = A.mark()
            stats_bufs = {"sq": [A.bf16(512), A.bf16(512)], "ln": A.f32(512), "rstd": A.f32(512)}
            mst = A.f32(2, D)
            memT = A.f32(KC, NMEM)
            memn = A.bf16(KC, NMEM)
            kTm = A.bf16(KC, NMEM)
            Vm = A.bf16(2, D)
            hnb = [A.bf16(KC, 512), A.bf16(KC, 512)]
            qx = A.bf16(KC, 512)
            ao = A.bf16(KC, 512)
            y = A.f32(KC, 512)
            pT = [A.bf16(512) for _ in range(4)]
            rec = A.f32(512)
            WC = A.bf16(KC, D)
            wload(WA, "WA", xa_w_kv[li, :, 0:D], KC, D)
            wload(WB, "WB", xa_w_kv[li, :, D:2 * D], KC, D)
            wload(WC, "WC", xa_w_q[li], KC, D, nb=False)
            S.add("sp", I("dma_start", out=mst, in_=mem_d.rearrange("(a p) d -> p a d", p=128)), writes=["mst"], slot="mst")
            for a in range(2):
                for hh in range(2):
                    bank, bk = ps()
                    for kk in range(4):
                        k = hh * 4 + kk
                        S.add("pe", I("transpose", out=bank[:, kk * 128:(kk + 1) * 128], in_=mst[:, a, k * 128:(k + 1) * 128], identity=ident),
                              reads=["mst", "consts"], writes=[bk])
                    evac_copy(memT[:, hh * 4:(hh + 1) * 4, a * 128:(a + 1) * 128], bank[:, 0:512].rearrange("p (a b) -> p a b", a=4), [bk],
                              [("memT", k) for k in range(hh * 4, hh * 4 + 4)])
            rstd = rms_stats(lambda k: memT[:, k, :], NMEM, lambda k: ("memT", k), "mem")
            for k in range(KC):
                S.add("dve", I("scalar_tensor_tensor", out=memn[:, k, :], in0=memT[:, k, :], scalar=gT[:, k, 24 + li:25 + li],
                                                                    in1=rstd[:, 0:NMEM], op0=ALU.mult, op1=ALU.mult),
                      reads=[("memT", k), "rstd", "consts"], writes=[("memn", k)])
            proj(lambda c: kTm[:, c, :], lambda c: ("kTm", c), WA, "WA", memn, "memn", NMEM)
            for a in range(2):
                for hf in range(2):
                    bank, bk = ps()
                    for k in range(KC):
                        S.add("pe", I("matmul", out=bank[:, :], lhsT=memn[:, k, a * 128:(a + 1) * 128],
                                                                                    rhs=WB[:, k, hf * 512:(hf + 1) * 512], start=(k == 0), stop=(k == KC - 1)),
                              reads=["WB", ("memn", k)], writes=[bk])
                    evac_copy(Vm[:, a, hf * 512:(hf + 1) * 512], bank[:, :], [bk], [("Vm", a)])
            wload(WB, "WB", xa_w_o[li], KC, D)
            SCX = 256 ** -0.5
            prr = 0
            prenorm(0, 0, 512, g0 + 2, hnb[0], ("hn", 0))
            for t in range(4):
                if t + 1 < 4:
                    prenorm((t + 1) * 512, t + 1, 512, g0 + 2, hnb[(t + 1) % 2], ("hn", (t + 1) % 2))
                hn, hk = hnb[t % 2], ("hn", t % 2)
                proj(lambda c: qx[:, c, :], lambda c: ("qx", c), WC, "WC", hn, hk, 512)
                for h in range(4):
                    pbs = []
                    for a in range(2):
                        bank, bk = ps()
                        for cc in range(2):
                            S.add("pe", I("matmul", out=bank[:, :], lhsT=kTm[:, 2 * h + cc, a * 128:(a + 1) * 128],
                                                                                        rhs=qx[:, 2 * h + cc, :], start=(cc == 0), stop=(cc == 1)),
                                  reads=[("kTm", 2 * h + cc), ("qx", 2 * h + cc)], writes=[bk])
                        pbuf = prr % 4
                        prr += 1
                        S.add("act", I("activation", out=pT[pbuf], in_=bank[:, :], func=AF.Exp, scale=SCX),
                              reads=[bk], writes=[("pT", pbuf)])
                        pbs.append(pbuf)
                    bankr, bkr = ps()
                    for a in range(2):
                        S.add("pe", I("matmul", out=bankr[:, :], lhsT=ones_bf, rhs=pT[pbs[a]], start=(a == 0), stop=(a == 1)),
                              reads=[("pT", pbs[a]), "consts"], writes=[bkr])
                    S.add("dve", I("reciprocal", out=rec, in_=bankr[:, :]), reads=[bkr], writes=["rec"])
                    for cc in range(2):
                        banko, bko = ps()
                        for a in range(2):
                            S.add("pe", I("matmul", out=banko[:, :], lhsT=Vm[:, a, (2 * h + cc) * 128:(2 * h + cc + 1) * 128],
                                                                                                    rhs=pT[pbs[a]], start=(a == 0), stop=(a == 1)),
                                  reads=[("Vm", a), ("pT", pbs[a])], writes=[bko])
                        S.add("dve", I("tensor_tensor", out=ao[:, 2 * h + cc, :], in0=banko[:, :], in1=rec, op=ALU.mult),
                              reads=[bko, "rec"], writes=[("ao", 2 * h + cc)])
                proj(lambda c: y[:, c, :], lambda c: ("y", c), WB, "WB", ao, "ao", 512)
                postnorm_residual(y, "y", t * 512, t, 512, g0 + 3)
            S.barrier()
            A.release(m0)

            S.epoch += 1
            m0 = A.mark()
            stats_bufs = {"sq": [A.bf16(512), A.bf16(512)], "ln": A.f32(512), "rstd": A.f32(512)}
            hnhb = [A.bf16(KC, 1024), A.bf16(KC, 1024)]
            yacc = A.f32(KC, 1024)
            aT = [A.bf16(4, 1024), A.bf16(4, 1024)]
            rl = [A.f32(512), A.f32(512)]
            rlr = 0
            gcount = 0
            def mlp_prenorm(hf_):
                for tt in range(2):
                    t = hf_ * 2 + tt
                    c0 = t * 512
                    rstd = rms_stats(lambda k, c0=c0: xT[:, k, c0:c0 + 512], 512, lambda k, t=t: ("x", k, t), "pre")
                    for k in range(KC):
                        S.add("dve", I("scalar_tensor_tensor", out=hnhb[hf_][:, k, tt * 512:(tt + 1) * 512], in0=xT[:, k, c0:c0 + 512],
                                       scalar=gT[:, k, g0 + 4:g0 + 5], in1=rstd, op0=ALU.mult, op1=ALU.mult),
                              reads=[("x", k, t), "rstd", "consts"], writes=[("hnh", hf_, k, tt)])

            def mlp_wload(n):
                gi_ = n % 8
                slotf_, skey_ = (WAf, "WA") if n % 2 == 0 else (WBf, "WB")
                w1g_ = slotf_[:, 0:4096].rearrange("p (k n) -> p k n", k=8)
                w2g_ = slotf_[:, 4096:8192].rearrange("p (k n) -> p k n", k=4)
                wload(w1g_, skey_, mlp_w1[li, :, gi_ * 512:(gi_ + 1) * 512], 8, 512, nsplit=2)
                wload(w2g_, skey_, mlp_w2[li, gi_ * 512:(gi_ + 1) * 512, :], 4, D, nsplit=2)

            mlp_wload(0)
            mlp_prenorm(0)
            for hf in range(2):
                hnh = hnhb[hf]
                if hf == 0:
                    mlp_prenorm(1)
                for gi in range(8):
                    slot, slotf, skey = (WA, WAf, "WA") if gcount % 2 == 0 else (WB, WBf, "WB")
                    ab = gcount % 2
                    gcount += 1
                    w1g = slotf[:, 0:4096].rearrange("p (k n) -> p k n", k=8)
                    w2g = slotf[:, 4096:8192].rearrange("p (k n) -> p k n", k=4)
                    if gcount < 16:
                        mlp_wload(gcount)
                    for tt in range(2):
                        for fc in range(4):
                            bank, bk = ps()
                            for k in range(KC):
                                S.add("pe", I("matmul", out=bank[:, :], lhsT=w1g[:, k, fc * 128:(fc + 1) * 128],
                                                                                                      rhs=hnh[:, k, tt * 512:(tt + 1) * 512], start=(k == 0), stop=(k == KC - 1)),
                                      reads=[skey, ("hnh", hf, k, tt)], writes=[bk])
                            ri = rlr % 2
                            rlr += 1
                            S.add("act", I("activation", out=rl[ri], in_=bank[:, :], func=AF.Relu), reads=[bk], writes=[("rl", ri)])
                            S.add("pool", I("tensor_tensor", out=aT[ab][:, fc, tt * 512:(tt + 1) * 512], in0=rl[ri], in1=rl[ri], op=ALU.mult),
                                  reads=[("rl", ri)], writes=[("aT", ab, fc, tt)])
                    for tt in range(2):
                        for c in range(KC):
                            bank, bk = ps()
                            for fc in range(4):
                                S.add("pe", I("matmul", out=bank[:, :], lhsT=w2g[:, fc, c * 128:(c + 1) * 128],
                                                                                                             rhs=aT[ab][:, fc, tt * 512:(tt + 1) * 512], start=(fc == 0), stop=(fc == 3)),
                                      reads=[skey, ("aT", ab, fc, tt)], writes=[bk])
                            dst = yacc[:, c, tt * 512:(tt + 1) * 512]
                            if gi == 0:
                                evac_copy(dst, bank[:, :], [bk], [("yacc", c, tt)])
                            else:
                                S.add("dve", I("tensor_tensor", out=dst, in0=bank[:, :], in1=dst, op=ALU.add),
                                      reads=[bk], writes=[("yacc", c, tt)])
                for tt in range(2):
                    t = hf * 2 + tt
                    c0 = t * 512
                    yv = yacc[:, :, tt * 512:(tt + 1) * 512]
                    rstd = rms_stats(lambda k, yv=yv: yv[:, k, :], 512, lambda k, tt=tt: ("yacc", k, tt), "post")
                    for k in range(KC):
                        S.add("dve", I("scalar_tensor_tensor", out=yv[:, k, :], in0=yv[:, k, :], scalar=gT[:, k, g0 + 5:g0 + 6],
                                                                                   in1=rstd, op0=ALU.mult, op1=ALU.mult),
                              reads=["rstd", "consts"], writes=[("yacc", k, tt)])
                        S.add("pool", I("tensor_tensor", out=xT[:, k, c0:c0 + 512], in0=xT[:, k, c0:c0 + 512], in1=yv[:, k, :], op=ALU.add),
                              reads=[("yacc", k, tt)], writes=[("x", k, t)])
            S.barrier()
            A.release(m0)

        ost = [A.f32(D), A.f32(D)]
        for tt in range(SEQ // 128):
            i = tt % 2
            for hh in range(2):
                bank, bk = ps()
                for kk in range(4):
                    k = hh * 4 + kk
                    S.add("pe", I("transpose", out=bank[:, kk * 128:(kk + 1) * 128], in_=xT[:, k, tt * 128:(tt + 1) * 128], identity=ident),
                          reads=[("x", k, tt // 4), "consts"], writes=[bk])
                evac_copy(ost[i][:, hh * 512:(hh + 1) * 512], bank[:, :], [bk], [("ost", i, hh)])
            S.add("sp", I("dma_start", out=out_d[tt * 128:(tt + 1) * 128, :], in_=ost[i]),
                  reads=[("ost", i, 0), ("ost", i, 1)], slot=f"o{i}")
        S.emit(nc, st, final_wait_slots=["o0", "o1"])
    return nc


def _consts():
    ident = np.eye(128, dtype=np.float32)
    rsw = np.zeros((128, 32), np.float32)
    for i in range(16):
        rsw[i + 16, i] = -1.0
        rsw[i, i + 16] = 1.0
    pos = np.arange(SEQ, dtype=np.float32)
    inv_freq = (np.float32(500000.0) ** (-np.arange(0, 32, 2, dtype=np.float32) / np.float32(32))).astype(np.float32)
    ang = (pos[:, None] * inv_freq[None, :]).astype(np.float32)
    cos = np.cos(ang).astype(np.float32).T
    sin = np.sin(ang).astype(np.float32).T
    cos32 = np.concatenate([cos, cos], axis=0)
    sin32 = np.concatenate([sin, sin], axis=0)
    kk = np.arange(128)[:, None]
    qq = np.arange(128)[None, :]
    tri = np.where(kk <= qq, 0.0, NEG).astype(np.float32)
    mown = np.zeros((2, 128, 256), np.float32)
    mown[0, :, 0:128] = tri
    mown[1, :, 0:128] = NEG
    mown[1, :, 128:256] = tri
    rc = np.broadcast_to((1.0 / np.arange(1, 17, dtype=np.float32))[None, :], (128, 16)).astype(np.float32).copy()
    return dict(c_ident=ident, c_rswap=rsw, c_cos=np.ascontiguousarray(cos32), c_sin=np.ascontiguousarray(sin32), c_mown=mown, c_rc=rc)


_PROG_CACHE = {}


def _get_prog(layers):
    key = tuple(layers)
    if key not in _PROG_CACHE:
        _PROG_CACHE[key] = build_program(list(layers))
    return _PROG_CACHE[key]


def _run(layers, x, mem, shared):
    nc = _get_prog(layers)
    in_maps = []
    for c in range(8):
        m = dict(shared)
        m["x"] = np.ascontiguousarray(x[c])
        m["mem"] = np.ascontiguousarray(mem[c])
        in_maps.append(m)
    res = run_bass_kernel_spmd(nc, in_maps, core_ids=list(range(8)))
    return np.stack([np.asarray(r["out"]) for r in res.results], axis=0)


LAUNCH_GROUPS = [[0, 1, 2, 3]]


def kernel(x, mem, norm_gains, mem_norm, pool_w_in, pool_w_group, pool_scale,
           moba_w_qkv, moba_w_o, xa_w_q, xa_w_kv, xa_w_o, mlp_w1, mlp_w2):
    f = lambda a: np.ascontiguousarray(np.asarray(a, dtype=np.float32))
    x = f(x)
    mem = f(mem)
    gains = np.concatenate([f(norm_gains).reshape(24, D), f(mem_norm).reshape(4, D), f(pool_scale).reshape(2, D)], axis=0)
    shared = dict(gains=np.ascontiguousarray(gains), pool_w_in=f(pool_w_in), pool_w_group=f(pool_w_group),
                  moba_w_qkv=f(moba_w_qkv), moba_w_o=f(moba_w_o), xa_w_q=f(xa_w_q), xa_w_kv=f(xa_w_kv),
                  xa_w_o=f(xa_w_o), mlp_w1=f(mlp_w1), mlp_w2=f(mlp_w2))
    shared.update(_consts())
    cur = x
    for grp in LAUNCH_GROUPS:
        cur = _run(grp, cur, mem, shared)
    return cur.astype(np.float32)
```

```python
import numpy as np
from contextlib import ExitStack
import concourse.bass as bass
import concourse.mybir as mybir
from concourse.bass_utils import run_bass_kernel_spmd

F32 = mybir.dt.float32
BF16 = mybir.dt.bfloat16
ALU = mybir.AluOpType
AF = mybir.ActivationFunctionType
AX = mybir.AxisListType

ENGS = ("pe", "act", "dve", "pool", "sp")


def I(method, **kw):
    return (method, kw)

D = 1024
KC = 8
SEQ = 2048
NMEM = 256
DEPTH = 4
DFF = 4096
NEG = -30000.0
WINDOWS = (2, 4, 8, 16)
EPS = 1e-6
DBG = set()


class Op:
    __slots__ = ("eng", "fn", "deps", "slot", "seq", "epoch", "ndep", "gid")

    def __init__(self, eng, fn, slot, epoch, gid):
        self.eng = eng
        self.fn = fn
        self.deps = []
        self.slot = slot
        self.seq = None
        self.epoch = epoch
        self.ndep = 0
        self.gid = gid


class Sched:
    def __init__(self):
        self.ops = {e: [] for e in ENGS}
        self.lw = {}
        self.rd = {}
        self.epoch = 0
        self.gid = 0
        self.last = {e: None for e in ENGS}
        self.barrier_deps = []
        self.dma_ops = []
        self.lastslot = {}

    def add(self, eng, fn, reads=(), writes=(), slot=None, nobarrier=False):
        o = Op(eng, fn, slot, self.epoch, self.gid)
        self.gid += 1
        deps = {}
        for k in reads:
            w = self.lw.get(k)
            if w is not None:
                deps[w.gid] = w
        for k in writes:
            w = self.lw.get(k)
            if w is not None:
                deps[w.gid] = w
            for r in self.rd.get(k, {}).values():
                deps[r.gid] = r
        if not nobarrier:
            for b in self.barrier_deps:
                deps[b.gid] = b
        if eng == "pe" and slot is None:
            deps = {g: d for g, d in deps.items() if not (d.eng == "pe" and d.slot is None)}
        o.deps = list(deps.values())
        for d in o.deps:
            d.ndep += 1
        rk = eng if slot is None else ("dma", o.gid)
        for k in reads:
            self.rd.setdefault(k, {})[rk] = o
        for k in writes:
            self.lw[k] = o
            self.rd[k] = {}
        self.ops[eng].append(o)
        if slot is not None:
            self.dma_ops.append(o)
            self.lastslot[slot] = o
        else:
            self.last[eng] = o
        return o

    def barrier(self, keep=()):
        deps = {}
        for e in ENGS:
            if self.last[e] is not None:
                deps[self.last[e].gid] = self.last[e]
        for o in self.lastslot.values():
            deps[o.gid] = o
        self.barrier_deps = list(deps.values())
        keep = set(keep) | {"WA", "WB"}
        self.lw = {k: v for k, v in self.lw.items() if k in keep}
        self.rd = {k: v for k, v in self.rd.items() if k in keep}

    def emit(self, nc, stack, final_wait_slots=()):
        n_epochs = self.epoch + 1
        sems = {}
        for e in ("pe", "act", "dve", "pool"):
            for ep in range(n_epochs):
                sems[(e, ep)] = stack.enter_context(nc.semaphore(f"s_{e}_{ep}"))
        slot_names = []
        for o in self.dma_ops:
            if o.slot not in slot_names:
                slot_names.append(o.slot)
        for s in slot_names:
            sems[("dma", s)] = stack.enter_context(nc.semaphore(f"d_{s}"))
        cnt = {}
        for e in ENGS:
            for o in self.ops[e]:
                if o.slot is not None:
                    k = ("dma", o.slot)
                    cnt[k] = cnt.get(k, 0) + 16
                    o.seq = cnt[k]
                elif o.ndep > 0:
                    k = (e, o.epoch)
                    cnt[k] = cnt.get(k, 0) + 1
                    o.seq = cnt[k]
        self.sem_counts = cnt

        def semkey(d):
            return ("dma", d.slot) if d.slot is not None else (d.eng, d.epoch)

        block = stack.enter_context(nc.Block())
        engobj = {"pe": "tensor", "act": "scalar", "dve": "vector", "pool": "gpsimd", "sp": "sync"}

        def run(e, eng):
            waited = {}
            for o in self.ops[e]:
                for d in o.deps:
                    if d.slot is None and d.eng == e and e == "pe":
                        continue
                    k = semkey(d)
                    if waited.get(k, 0) >= d.seq:
                        continue
                    eng.wait_ge(sems[k], d.seq)
                    waited[k] = d.seq
                ins = getattr(eng, o.fn[0])(**o.fn[1])
                if o.slot is not None:
                    ins.then_inc(sems[("dma", o.slot)], 16)
                elif o.ndep > 0:
                    ins.then_inc(sems[(e, o.epoch)], 1)
            if e == "sp":
                for s in final_wait_slots:
                    k = ("dma", s)
                    eng.wait_ge(sems[k], cnt[k])

        for e in ENGS:
            if not self.ops[e] and e != "sp":
                continue
            deco = getattr(block, engobj[e])

            def _f(eng, e=e):
                run(e, eng)
            deco(_f)


class Arena:
    def __init__(self, tensor, nwords):
        self.t = tensor
        self.n = nwords
        self.off = 0

    def f32(self, *shape):
        n = int(np.prod(shape))
        n_al = (n + 7) // 8 * 8
        assert self.off + n_al <= self.n, f"arena overflow {self.off}+{n_al}>{self.n}"
        a = self.t[:, self.off:self.off + n]
        self.off += n_al
        return self._shape(a, shape)

    def bf16(self, *shape):
        n = int(np.prod(shape))
        assert n % 2 == 0
        nw = n // 2
        n_al = (nw + 7) // 8 * 8
        assert self.off + n_al <= self.n, f"arena overflow {self.off}+{n_al}>{self.n}"
        a = self.t[:, self.off:self.off + nw].bitcast(BF16)
        self.off += n_al
        return self._shape(a, shape)

    @staticmethod
    def _shape(a, shape):
        if len(shape) == 1:
            return a
        if len(shape) == 2:
            return a.rearrange("p (a b) -> p a b", a=shape[0])
        if len(shape) == 3:
            return a.rearrange("p (a b c) -> p a b c", a=shape[0], b=shape[1])
        raise ValueError

    def mark(self):
        return self.off

    def release(self, m):
        self.off = m


def build_program(layers, first=True, last=True):
    nc = bass.Bass("TRN2", target_bir_lowering=False)

    def din(name, shape):
        return nc.dram_tensor(name, list(shape), F32, kind="ExternalInput").ap()

    x_d = din("x", (SEQ, D))
    mem_d = din("mem", (NMEM, D))
    gains_d = din("gains", (30, D))
    pool_w_in = din("pool_w_in", (2, D, D))
    pool_w_group = din("pool_w_group", (2, 4, 256, 256))
    moba_w_qkv = din("moba_w_qkv", (2, D, 3 * D))
    moba_w_o = din("moba_w_o", (2, D, D))
    xa_w_q = din("xa_w_q", (DEPTH, D, D))
    xa_w_kv = din("xa_w_kv", (DEPTH, D, 2 * D))
    xa_w_o = din("xa_w_o", (DEPTH, D, D))
    mlp_w1 = din("mlp_w1", (DEPTH, D, DFF))
    mlp_w2 = din("mlp_w2", (DEPTH, DFF, D))
    ident_d = din("c_ident", (128, 128))
    rsw_d = din("c_rswap", (128, 32))
    cos_d = din("c_cos", (32, SEQ))
    sin_d = din("c_sin", (32, SEQ))
    mown_d = din("c_mown", (2, 128, 256))
    rc_d = din("c_rc", (128, 16))
    out_d = nc.dram_tensor("out", [SEQ, D], F32, kind="ExternalOutput").ap()

    S = Sched()
    st = ExitStack()
    with st:
        NW = 212000 // 4
        arena_t = st.enter_context(nc.sbuf_tensor("arena", [128, NW], F32))
        A = Arena(arena_t, NW)
        banks = [st.enter_context(nc.psum_tensor(f"bank{i}", [128, 512], F32)) for i in range(8)]
        ps_rr = {"all": 0, "lo": 0}

        def ps(pool="all"):
            if pool == "all":
                i = ps_rr["all"] % 8
                ps_rr["all"] += 1
            else:
                i = ps_rr["lo"] % 4
                ps_rr["lo"] += 1
            return banks[i], ("ps", i)

        xT = A.f32(KC, SEQ)
        ident = A.f32(128)
        ident_bf = A.bf16(128)
        ones_bf = A.bf16(128)
        rsw = A.f32(32)
        epsb = A.f32(8)
        rc16 = A.f32(16)
        mown = A.bf16(2, 256)
        gT = A.f32(KC, 32)
        WA = A.bf16(KC, D)
        WB = A.bf16(KC, D)
        WAf = WA.rearrange("p a b -> p (a b)")
        WBf = WB.rearrange("p a b -> p (a b)")
        base_mark = A.mark()

        def xkeys(t512, ks=range(KC)):
            return [("x", k, t512) for k in ks]

        rot = {"sq": 0, "cp": 0}

        def wload(slot_ap, slotkey, src_ap, nk, ncols, nsplit=4, nb=True):
            src = src_ap.rearrange("(k p) n -> p k n", p=128)
            step = max(1, nk // nsplit)
            for k0 in range(0, nk, step):
                S.add("pool", I("dma_start", out=slot_ap[:, k0:k0 + step, :], in_=src[:, k0:k0 + step, :]),
                      writes=[slotkey], slot="w_" + slotkey, nobarrier=nb)

        def rms_stats(src_fn, W, src_reads, tag):
            sq = stats_bufs["sq"]
            bank, bk = ps()
            for k in range(KC):
                i = rot["sq"] % 2
                rot["sq"] += 1
                S.add("act", I("activation", out=sq[i][:, 0:W], in_=src_fn(k), func=AF.Square),
                      reads=[src_reads(k)], writes=[("sq", i)])
                S.add("pe", I("matmul", out=bank[:, 0:W], lhsT=ones_bf, rhs=sq[i][:, 0:W], start=(k == 0), stop=(k == KC - 1)),
                      reads=[("sq", i), "consts"], writes=[bk])
            ln = stats_bufs["ln"]
            rstd = stats_bufs["rstd"]
            S.add("act", I("activation", out=ln[:, 0:W], in_=bank[:, 0:W], func=AF.Ln, scale=1.0 / D, bias=epsb[:, 0:1]),
                  reads=[bk, "consts"], writes=["ln"])
            S.add("act", I("activation", out=rstd[:, 0:W], in_=ln[:, 0:W], func=AF.Exp, scale=-0.5),
                  reads=["ln"], writes=["rstd"])
            return rstd

        def prenorm(xcols, t512, W, gidx, dst, dstkey):
            c0 = xcols
            rstd = rms_stats(lambda k: xT[:, k, c0:c0 + W], W, lambda k: ("x", k, t512), "pre")
            for k in range(KC):
                S.add("dve", I("scalar_tensor_tensor", out=dst[:, k, 0:W], in0=xT[:, k, c0:c0 + W], scalar=gT[:, k, gidx:gidx + 1],
                                                                    in1=rstd[:, 0:W], op0=ALU.mult, op1=ALU.mult),
                      reads=[("x", k, t512), "rstd", "consts"], writes=[(dstkey, k)])

        def postnorm_residual(y, ykey, xcols, t512, W, gidx):
            c0 = xcols
            rstd = rms_stats(lambda k: y[:, k, 0:W], W, lambda k: (ykey, k), "post")
            for k in range(KC):
                S.add("dve", I("scalar_tensor_tensor", out=y[:, k, 0:W], in0=y[:, k, 0:W], scalar=gT[:, k, gidx:gidx + 1],
                                                                    in1=rstd[:, 0:W], op0=ALU.mult, op1=ALU.mult),
                      reads=["rstd", "consts"], writes=[(ykey, k)])
                S.add("pool", I("tensor_tensor", out=xT[:, k, c0:c0 + W], in0=xT[:, k, c0:c0 + W], in1=y[:, k, 0:W], op=ALU.add),
                      reads=[(ykey, k)], writes=[("x", k, t512)])

        def evac_copy(dst, src, reads, writes, scale=None):
            i = rot["cp"] % 2
            rot["cp"] += 1
            if scale is not None or i == 0:
                if scale is None:
                    S.add("act", I("activation", out=dst, in_=src, func=AF.Copy), reads=reads, writes=writes)
                else:
                    S.add("act", I("activation", out=dst, in_=src, func=AF.Copy, scale=scale), reads=reads + ["consts"], writes=writes)
            else:
                S.add("dve", I("tensor_copy", out=dst, in_=src), reads=reads, writes=writes)

        def proj(dst_fn, dstkey_fn, W_slot, wkey, src, srckey, W, nout=KC, col0=0, evac=None):
            for c in range(nout):
                bank, bk = ps()
                for k in range(KC):
                    S.add("pe", I("matmul", out=bank[:, 0:W], lhsT=W_slot[:, k, col0 + c * 128: col0 + (c + 1) * 128],
                                                                         rhs=src[:, k, 0:W], start=(k == 0), stop=(k == KC - 1)),
                          reads=[wkey, (srckey, k)], writes=[bk])
                if evac is None:
                    evac_copy(dst_fn(c), bank[:, 0:W], [bk], [dstkey_fn(c)])
                else:
                    evac(c, bank, bk)

        S.add("sp", I("dma_start", out=ident, in_=ident_d), writes=["c_id"], slot="c_id")
        S.add("sp", I("dma_start", out=rsw, in_=rsw_d), writes=["c_rs"], slot="c_rs")
        S.add("sp", I("dma_start", out=rc16, in_=rc_d), writes=["c_rc"], slot="c_rc")
        S.add("pool", I("dma_start", out=mown, in_=mown_d.rearrange("a p n -> p a n")), writes=["c_mo"], slot="c1")
        S.add("dve", I("tensor_copy", out=ident_bf, in_=ident), reads=["c_id"], writes=["c_idbf"])
        S.add("dve", I("memset", ap=ones_bf, constant=1.0), writes=["c_ones"])
        S.add("dve", I("memset", ap=epsb, constant=EPS), writes=["c_eps"])
        graw = A.f32(D)
        S.add("sp", I("dma_start", out=graw[0:30, :], in_=gains_d), writes=["graw"], slot="c_gr")
        for k in range(KC):
            bank, bk = ps()
            S.add("pe", I("transpose", out=bank[:, 0:30], in_=graw[0:30, k * 128:(k + 1) * 128], identity=ident[0:30, 0:30]),
                  reads=["graw", "c_id"], writes=[bk])
            S.add("dve", I("tensor_copy", out=gT[:, k, 0:30], in_=bank[:, 0:30]), reads=[bk], writes=["c_g"])
        xst = [A.f32(D), A.f32(D)]
        for tt in range(SEQ // 128):
            i = tt % 2
            S.add("sp", I("dma_start", out=xst[i], in_=x_d[tt * 128:(tt + 1) * 128, :]), writes=[("xst", i)], slot=f"xs{i}")
            for hh in range(2):
                bank, bk = ps()
                for kk in range(4):
                    k = hh * 4 + kk
                    S.add("pe", I("transpose", out=bank[:, kk * 128:(kk + 1) * 128], in_=xst[i][:, k * 128:(k + 1) * 128], identity=ident),
                          reads=[("xst", i), "c_id"], writes=[bk])
                dst = xT[:, hh * 4:(hh + 1) * 4, tt * 128:(tt + 1) * 128]
                src = bank[:, 0:512].rearrange("p (a b) -> p a b", a=4)
                evac_copy(dst, src, [bk], [("x", k, tt // 4) for k in range(hh * 4, hh * 4 + 4)])
        S.barrier()
        A.release(base_mark)

        for li in layers:
            S.epoch += 1
            j = li // 2
            g0 = li * 6
            if li % 2 == 0:
                m0 = A.mark()
                stats_bufs = {"sq": [A.bf16(512), A.bf16(512)], "ln": A.f32(512), "rstd": A.f32(512)}
                hnb = [A.bf16(KC, 512), A.bf16(KC, 512)]
                wg = A.bf16(4, 2, 256)
                U = A.f32(KC, 528)
                pp = [[A.f32(528), A.f32(528)] for _ in range(2)]
                pooled = A.bf16(KC, 512)
                fix = A.f32(16)
                y = A.f32(KC, 512)
                wload(WA, "WA", pool_w_in[j], KC, D)
                for g in range(4):
                    S.add("pool", I("dma_start", out=wg[:, g, :, :], in_=pool_w_group[j, g].rearrange("(c p) n -> p c n", p=128)),
                          writes=["wg"], slot="w_wg", nobarrier=False)
                S.add("dve", I("memset", ap=U[:, :, 0:16], constant=0.0), writes=[("U", c) for c in range(KC)])
                prenorm(0, 0, 512, g0 + 0, hnb[0], ("hn", 0))
                for t in range(4):
                    if t + 1 < 4:
                        prenorm((t + 1) * 512, t + 1, 512, g0 + 0, hnb[(t + 1) % 2], ("hn", (t + 1) % 2))
                    hn, hk = hnb[t % 2], ("hn", t % 2)
                    if t > 0:
                        S.add("dve", I("tensor_copy", out=U[:, :, 0:16], in_=U[:, :, 512:528]),
                              reads=[("U", c) for c in range(KC)], writes=[("U", c) for c in range(KC)])
                    proj(lambda c: U[:, c, 16:528], lambda c: ("U", c), WA, "WA", hn, hk, 512)
                    for c in range(KC):
                        w = WINDOWS[c // 2]
                        nl = {2: 1, 4: 2, 8: 3, 16: 4}[w]
                        bufs = pp[c % 2]
                        cur, curkey = U[:, c, :], ("U", c)
                        sh = 1
                        lo = 0
                        for lv in range(nl):
                            nxt, nxtkey = bufs[lv % 2], ("pp", c % 2, lv % 2)
                            lo2 = lo + sh
                            S.add("pool", I("tensor_tensor", out=nxt[:, lo2:528], in0=cur[:, lo2:528],
                                                                                                   in1=cur[:, lo2 - sh:528 - sh], op=ALU.add),
                                  reads=[curkey], writes=[nxtkey])
                            cur, curkey = nxt, nxtkey
                            lo = lo2
                            sh *= 2
                        S.add("dve", I("scalar_tensor_tensor", out=pooled[:, c, :], in0=cur[:, 16:528], scalar=1.0 / w,
                                                                                          in1=U[:, c, 16:528], op0=ALU.mult, op1=ALU.subtract),
                              reads=[curkey, ("U", c)], writes=[("pooled", c)])
                        if t == 0:
                            S.add("dve", I("tensor_tensor", out=fix[:, 0:15], in0=cur[:, 16:31], in1=rc16[:, 0:15], op=ALU.mult),
                                  reads=[curkey, "consts"], writes=["fix"])
                            S.add("dve", I("tensor_tensor", out=pooled[:, c, 0:w - 1], in0=fix[:, 0:w - 1], in1=U[:, c, 16:16 + w - 1], op=ALU.subtract),
                                  reads=["fix", ("U", c)], writes=[("pooled", c)])
                    for g in range(4):
                        for oc in range(2):
                            co = 2 * g + oc
                            bank, bk = ps()
                            for kc in range(2):
                                S.add("pe", I("matmul", out=bank[:, :], lhsT=wg[:, g, kc, oc * 128:(oc + 1) * 128],
                                                                                              rhs=pooled[:, 2 * g + kc, :], start=(kc == 0), stop=(kc == 1)),
                                      reads=["wg", ("pooled", 2 * g + kc)], writes=[bk])
                            evac_copy(y[:, co, :], bank[:, :], [bk], [("y", co)], scale=gT[:, co, 28 + j:29 + j])
                    postnorm_residual(y, "y", t * 512, t, 512, g0 + 1)
                S.barrier()
                A.release(m0)
            else:
                m0 = A.mark()
                kT = A.bf16(KC, SEQ)
                V = A.bf16(16, D)
                kmT = A.bf16(8, 8)
                kms = A.f32(8, 8)
                m1 = A.mark()
                stats_bufs = {"sq": [A.bf16(512), A.bf16(512)], "ln": A.f32(512), "rstd": A.f32(512)}
                hnb = [A.bf16(KC, 512), A.bf16(KC, 512)]
                kf = A.f32(512)
                t1 = A.f32(512)
                t2 = A.f32(512)
                rc_ = A.f32(512)
                rs_ = A.f32(512)
                wload(WA, "WA", moba_w_qkv[j, :, D:2 * D], KC, D)
                wload(WB, "WB", moba_w_qkv[j, :, 2 * D:3 * D], KC, D)

                def rope_part(dst, f32src, bank2, bk2, W, tag):
                    S.add("pe", I("matmul", out=bank2[0:32, 0:W], lhsT=rsw, rhs=f32src[:, 0:W], start=True, stop=True),
                          reads=[tag + "f", "consts"], writes=[bk2])
                    S.add("dve", I("tensor_tensor", out=t1[0:32, 0:W], in0=f32src[0:32, 0:W], in1=rc_[0:32, 0:W], op=ALU.mult),
                          reads=[tag + "f", "ropec"], writes=["t1"])
                    S.add("dve", I("tensor_tensor", out=t2[0:32, 0:W], in0=bank2[0:32, 0:W], in1=rs_[0:32, 0:W], op=ALU.mult),
                          reads=[bk2, "ropes"], writes=["t2"])

                prenorm(0, 0, 512, g0 + 0, hnb[0], ("hn", 0))
                for t in range(4):
                    if t + 1 < 4:
                        prenorm((t + 1) * 512, t + 1, 512, g0 + 0, hnb[(t + 1) % 2], ("hn", (t + 1) % 2))
                    hn, hk = hnb[t % 2], ("hn", t % 2)
                    S.add("sp", I("dma_start", out=rc_[0:32, :], in_=cos_d[:, t * 512:(t + 1) * 512]), writes=["ropec"], slot="rc")
                    S.add("sp", I("dma_start", out=rs_[0:32, :], in_=sin_d[:, t * 512:(t + 1) * 512]), writes=["ropes"], slot="rs")
                    for h in range(0 if "nok" in DBG else 8):
                        bank, bk = ps()
                        for k in range(KC):
                            S.add("pe", I("matmul", out=bank[:, :], lhsT=WA[:, k, h * 128:(h + 1) * 128], rhs=hn[:, k, :],
                                                                                 start=(k == 0), stop=(k == KC - 1)),
                                  reads=["WA", (hk, k)], writes=[bk])
                        S.add("act", I("activation", out=kf, in_=bank[:, :], func=AF.Copy), reads=[bk], writes=["kf"])
                        kdst = kT[:, h, t * 512:(t + 1) * 512]
                        bank2, bk2 = ps()
                        rope_part(None, kf, bank2, bk2, 512, "k")
                        S.add("dve", I("tensor_tensor", out=kf[0:32, :], in0=t1[0:32, :], in1=t2[0:32, :], op=ALU.add),
                              reads=["t1", "t2"], writes=["kf"])
                        S.add("dve", I("tensor_copy", out=kdst, in_=kf), reads=["kf"], writes=[("kT", h)])
                        S.add("dve", I("tensor_reduce", out=kms[:, h, 2 * t:2 * t + 2], in_=kf.rearrange("p (n s) -> p n s", n=2), axis=AX.X, op=ALU.add),
                              reads=["kf"], writes=["kms"])
                    for ts in range(0 if "nov" in DBG else 4):
                        for hf in range(2):
                            bank, bk = ps()
                            for k in range(KC):
                                S.add("pe", I("matmul", out=bank[:, :], lhsT=hn[:, k, ts * 128:(ts + 1) * 128],
                                                                                              rhs=WB[:, k, hf * 512:(hf + 1) * 512], start=(k == 0), stop=(k == KC - 1)),
                                      reads=["WB", (hk, k)], writes=[bk])
                            evac_copy(V[:, t * 4 + ts, hf * 512:(hf + 1) * 512], bank[:, :], [bk], [("V", t * 4 + ts)])
                keep = set([("kT", h) for h in range(8)] + [("V", i) for i in range(16)] + ["kms"])
                S.barrier(keep=keep)
                A.release(m1)
                S.epoch += 1
                NU = 0 if "nop2" in DBG else 8
                GATE_FROM = 99 if "nogate" in DBG else 4
                stats_bufs = {"sq": [A.bf16(256), A.bf16(256)], "ln": A.f32(256), "rstd": A.f32(256)}
                hnb = [A.bf16(KC, 256), A.bf16(KC, 256)]
                qf = A.f32(256)
                t1 = A.f32(256)
                t2 = A.f32(256)
                rc_ = A.f32(256)
                rs_ = A.f32(256)
                qt_ = A.bf16(KC, 256)
                ao = A.bf16(KC, 256)
                y = A.f32(KC, 256)
                pT = [A.bf16(512) for _ in range(3)]
                rec = A.f32(256)
                maskb = A.bf16(2, 8, 8)
                Gs = A.f32(64)
                cmpb = A.f32(8 * 49)
                cntb = A.f32(56)
                wload(WA, "WA", moba_w_qkv[j, :, 0:D], KC, D)
                wload(WB, "WB", moba_w_o[j], KC, D)
                S.add("dve", I("memset", ap=maskb, constant=0.0), writes=["maskb"])
                SC = 128 ** -0.5
                prr = 0
                for u in range(NU):
                    t512 = u // 2
                    c0 = u * 256
                    if u == 0:
                        prenorm(0, 0, 256, g0 + 0, hnb[0], ("hn", 0))
                    if u + 1 < NU:
                        prenorm((u + 1) * 256, (u + 1) // 2, 256, g0 + 0, hnb[(u + 1) % 2], ("hn", (u + 1) % 2))
                    hn, hk = hnb[u % 2], ("hn", u % 2)
                    S.add("sp", I("dma_start", out=rc_[0:32, :], in_=cos_d[:, c0:c0 + 256]), writes=["ropec"], slot="rc")
                    S.add("sp", I("dma_start", out=rs_[0:32, :], in_=sin_d[:, c0:c0 + 256]), writes=["ropes"], slot="rs")
                    for h in range(8):
                        bank, bk = ps("lo")
                        for k in range(KC):
                            S.add("pe", I("matmul", out=bank[:, 0:256], lhsT=WA[:, k, h * 128:(h + 1) * 128], rhs=hn[:, k, :],
                                                                                 start=(k == 0), stop=(k == KC - 1)),
                                  reads=["WA", (hk, k)], writes=[bk])
                        S.add("act", I("activation", out=qf, in_=bank[:, 0:256], func=AF.Copy), reads=[bk], writes=["qf"])
                        bank2, bk2 = ps("lo")
                        rope_part(None, qf, bank2, bk2, 256, "q")
                        S.add("dve", I("tensor_tensor", out=qf[0:32, :], in0=t1[0:32, :], in1=t2[0:32, :], op=ALU.add),
                              reads=["t1", "t2"], writes=["qf"])
                        S.add("dve", I("tensor_copy", out=qt_[:, h, :], in_=qf), reads=["qf"], writes=[("q", h)])
                        if u >= GATE_FROM:
                            for qi in range(2):
                                S.add("pe", I("matmul", out=banks[6 + qi][:, h * 8:(h + 1) * 8], lhsT=qf[:, qi * 128:(qi + 1) * 128],
                                              rhs=kms[:, h, :], start=True, stop=True),
                                      reads=["qf", "kms"], writes=[("ps", 6 + qi)])
                    if u >= GATE_FROM:
                        for qi in range(2):
                            bank, bk = banks[6 + qi], ("ps", 6 + qi)
                            S.add("dve", I("tensor_copy", out=Gs, in_=bank[:, 0:64]), reads=[bk], writes=["Gs"])
                            G3 = Gs.rearrange("p (h n) -> p h n", n=8)
                            cm = cmpb[:, 0:8 * u * u].rearrange("p (h n m) -> p h n m", n=u, m=u)
                            in_m = G3[:, :, 0:u].unsqueeze(2).broadcast_to([128, 8, u, u])
                            in_n = G3[:, :, 0:u].unsqueeze(3).broadcast_to([128, 8, u, u])
                            cn = cntb[:, 0:8 * u].rearrange("p (h n) -> p h n", n=u)
                            S.add("dve", I("tensor_tensor", out=cm, in0=in_m, in1=in_n, op=ALU.is_gt),
                                  reads=["Gs"], writes=["cmp"])
                            S.add("dve", I("tensor_reduce", out=cn, in_=cm, axis=AX.X, op=ALU.add), reads=["cmp"], writes=["cnt"])
                            S.add("dve", I("tensor_single_scalar", out=cn, in_=cn, scalar=3.0, op=ALU.is_ge), reads=["cnt"], writes=["cnt"])
                            S.add("dve", I("tensor_single_scalar", out=maskb[:, qi, :, 0:u], in_=cn, scalar=NEG, op=ALU.mult),
                                  reads=["cnt"], writes=["maskb"])
                    for h in range(8):
                        tiles = [("own", 0), ("own", 1)] + [(n, jj) for n in range(u) for jj in range(2)]
                        pairs = [tiles[i:i + 2] for i in range(0, len(tiles), 2)]
                        bo, bok = banks[4 + 2 * (h % 2)], ("ps", 4 + 2 * (h % 2))
                        br, brk = banks[5 + 2 * (h % 2)], ("ps", 5 + 2 * (h % 2))
                        npairs = len(pairs)

                        def keytile(tl):
                            return (u * 2 + tl[1]) if tl[0] == "own" else (tl[0] * 2 + tl[1])

                        def emit_qk(pi):
                            nonlocal prr
                            bank, bk = ps("lo")
                            pbuf = prr % 3
                            prr += 1
                            for idx, tl in enumerate(pairs[pi]):
                                kt = keytile(tl)
                                reg = bank[:, idx * 256:(idx + 1) * 256]
                                masked = (tl[0] == "own") or (u >= GATE_FROM)
                                S.add("pe", I("matmul", out=reg, lhsT=kT[:, h, kt * 128:(kt + 1) * 128], rhs=qt_[:, h, :],
                                                                                              start=True, stop=(not masked)),
                                      reads=[("kT", h), ("q", h)], writes=[bk])
                                if tl[0] == "own":
                                    S.add("pe", I("matmul", out=reg, lhsT=ident_bf, rhs=mown[:, tl[1], :], start=False, stop=True),
                                          reads=["consts"], writes=[bk])
                                elif u >= GATE_FROM:
                                    for qi in range(2):
                                        S.add("pe", I("matmul", out=reg[:, qi * 128:(qi + 1) * 128],
                                                                                                lhsT=maskb[:, qi, h, tl[0]:tl[0] + 1].broadcast_to([128, 128]),
                                                                                                rhs=ident_bf, start=False, stop=(qi == 1)),
                                              reads=["maskb", "consts"], writes=[bk])
                            S.add("act", I("activation", out=pT[pbuf], in_=bank[:, :], func=AF.Exp, scale=SC),
                                  reads=[bk], writes=[("pT", pbuf)])
                            return pbuf

                        def emit_pv(pi, pbuf):
                            for idx, tl in enumerate(pairs[pi]):
                                kt = keytile(tl)
                                first = (pi == 0 and idx == 0)
                                lastm = (pi == npairs - 1 and idx == 1)
                                S.add("pe", I("matmul", out=
                                    bo[:, 0:256], lhsT=V[:, kt, h * 128:(h + 1) * 128], rhs=pT[pbuf][:, idx * 256:(idx + 1) * 256], start=first, stop=lastm),
                                    reads=[("V", kt), ("pT", pbuf)], writes=[bok])
                                S.add("pe", I("matmul", out=
                                    br[:, 0:256], lhsT=ones_bf, rhs=pT[pbuf][:, idx * 256:(idx + 1) * 256], start=first, stop=lastm),
                                    reads=[("pT", pbuf), "consts"], writes=[brk])

                        pb = emit_qk(0)
                        for pi in range(npairs):
                            pbn = emit_qk(pi + 1) if pi + 1 < npairs else None
                            emit_pv(pi, pb)
                            pb = pbn
                        S.add("dve", I("reciprocal", out=rec, in_=br[:, 0:256]), reads=[brk], writes=["rec"])
                        S.add("dve", I("tensor_tensor", out=ao[:, h, :], in0=bo[:, 0:256], in1=rec, op=ALU.mult),
                              reads=[bok, "rec"], writes=[("ao", h)])
                    for c in range(KC):
                        bank, bk = ps("lo")
                        for k in range(KC):
                            S.add("pe", I("matmul", out=bank[:, 0:256], lhsT=WB[:, k, c * 128:(c + 1) * 128], rhs=ao[:, k, :],
                                                                                 start=(k == 0), stop=(k == KC - 1)),
                                  reads=["WB", ("ao", k)], writes=[bk])
                        evac_copy(y[:, c, :], bank[:, 0:256], [bk], [("y", c)])
                    postnorm_residual(y, "y", c0, t512, 256, g0 + 1)
                S.barrier()
                A.release(m0)

            if "noxm" in DBG:
                continue
            S.epoch += 1
            m0 = A.mark()
            stats_bufs = {"sq": [A.bf16(512), A.bf16(512)], "ln": A.f32(512), "rstd": A.f32(512)}
            mst = A.f32(2, D)
            memT = A.f32(KC, NMEM)
            memn = A.bf16(KC, NMEM)
            kTm = A.bf16(KC, NMEM)
            Vm = A.bf16(2, D)
            hnb = [A.bf16(KC, 512), A.bf16(KC, 512)]
            qx = A.bf16(KC, 512)
            ao = A.bf16(KC, 512)
            y = A.f32(KC, 512)
            pT = [A.bf16(512) for _ in range(4)]
            rec = A.f32(512)
            WC = A.bf16(KC, D)
            wload(WA, "WA", xa_w_kv[li, :, 0:D], KC, D)
            wload(WB, "WB", xa_w_kv[li, :, D:2 * D], KC, D)
            wload(WC, "WC", xa_w_q[li], KC, D, nb=False)
            S.add("sp", I("dma_start", out=mst, in_=mem_d.rearrange("(a p) d -> p a d", p=128)), writes=["mst"], slot="mst")
            for a in range(2):
                for hh in range(2):
                    bank, bk = ps()
                    for kk in range(4):
                        k = hh * 4 + kk
                        S.add("pe", I("transpose", out=bank[:, kk * 128:(kk + 1) * 128], in_=mst[:, a, k * 128:(k + 1) * 128], identity=ident),
                              reads=["mst", "consts"], writes=[bk])
                    evac_copy(memT[:, hh * 4:(hh + 1) * 4, a * 128:(a + 1) * 128], bank[:, 0:512].rearrange("p (a b) -> p a b", a=4), [bk],
                              [("memT", k) for k in range(hh * 4, hh * 4 + 4)])
            rstd = rms_stats(lambda k: memT[:, k, :], NMEM, lambda k: ("memT", k), "mem")
            for k in range(KC):
                S.add("dve", I("scalar_tensor_tensor", out=memn[:, k, :], in0=memT[:, k, :], scalar=gT[:, k, 24 + li:25 + li],
                                                                    in1=rstd[:, 0:NMEM], op0=ALU.mult, op1=ALU.mult),
                      reads=[("memT", k), "rstd", "consts"], writes=[("memn", k)])
            proj(lambda c: kTm[:, c, :], lambda c: ("kTm", c), WA, "WA", memn, "memn", NMEM)
            for a in range(2):
                for hf in range(2):
                    bank, bk = ps()
                    for k in range(KC):
                        S.add("pe", I("matmul", out=bank[:, :], lhsT=memn[:, k, a * 128:(a + 1) * 128],
                                                                                    rhs=WB[:, k, hf * 512:(hf + 1) * 512], start=(k == 0), stop=(k == KC - 1)),
                              reads=["WB", ("memn", k)], writes=[bk])
                    evac_copy(Vm[:, a, hf * 512:(hf + 1) * 512], bank[:, :], [bk], [("Vm", a)])
            wload(WB, "WB", xa_w_o[li], KC, D)
            SCX = 256 ** -0.5
            prr = 0
            prenorm(0, 0, 512, g0 + 2, hnb[0], ("hn", 0))
            for t in range(4):
                if t + 1 < 4:
                    prenorm((t + 1) * 512, t + 1, 512, g0 + 2, hnb[(t + 1) % 2], ("hn", (t + 1) % 2))
                hn, hk = hnb[t % 2], ("hn", t % 2)
                proj(lambda c: qx[:, c, :], lambda c: ("qx", c), WC, "WC", hn, hk, 512)
                for h in range(4):
                    pbs = []
                    for a in range(2):
                        bank, bk = ps()
                        for cc in range(2):
                            S.add("pe", I("matmul", out=bank[:, :], lhsT=kTm[:, 2 * h + cc, a * 128:(a + 1) * 128],
                                                                                        rhs=qx[:, 2 * h + cc, :], start=(cc == 0), stop=(cc == 1)),
                                  reads=[("kTm", 2 * h + cc), ("qx", 2 * h + cc)], writes=[bk])
                        pbuf = prr % 4
                        prr += 1
                        S.add("act", I("activation", out=pT[pbuf], in_=bank[:, :], func=AF.Exp, scale=SCX),
                              reads=[bk], writes=[("pT", pbuf)])
                        pbs.append(pbuf)
                    bankr, bkr = ps()
                    for a in range(2):
                        S.add("pe", I("matmul", out=bankr[:, :], lhsT=ones_bf, rhs=pT[pbs[a]], start=(a == 0), stop=(a == 1)),
                              reads=[("pT", pbs[a]), "consts"], writes=[bkr])
                    S.add("dve", I("reciprocal", out=rec, in_=bankr[:, :]), reads=[bkr], writes=["rec"])
                    for cc in range(2):
                        banko, bko = ps()
                        for a in range(2):
                            S.add("pe", I("matmul", out=banko[:, :], lhsT=Vm[:, a, (2 * h + cc) * 128:(2 * h + cc + 1) * 128],
                                                                                                    rhs=pT[pbs[a]], start=(a == 0), stop=(a == 1)),
                                  reads=[("Vm", a), ("pT", pbs[a])], writes=[bko])
                        S.add("dve", I("tensor_tensor", out=ao[:, 2 * h + cc, :], in0=banko[:, :], in1=rec, op=ALU.mult),
                              reads=[bko, "rec"], writes=[("ao", 2 * h + cc)])
                proj(lambda c: y[:, c, :], lambda c: ("y", c), WB, "WB", ao, "ao", 512)
                postnorm_residual(y, "y", t * 512, t, 512, g0 + 3)
            S.barrier()
            A.release(m0)

            S.epoch += 1
            m0 = A.mark()
            stats_bufs = {"sq": [A.bf16(512), A.bf16(512)], "ln": A.f32(512), "rstd": A.f32(512)}
            hnhb = [A.bf16(KC, 1024), A.bf16(KC, 1024)]
            yacc = A.f32(KC, 1024)
            aT = [A.bf16(4, 1024), A.bf16(4, 1024)]
            rl = [A.f32(512), A.f32(512)]
            WC = A.bf16(KC, D)
            WCf = WC.rearrange("p a b -> p (a b)")
            slots3 = [(WAf, "WA", True), (WBf, "WB", True), (WCf, "WC", False)]
            rlr = 0
            NG = 16

            def mlp_views(n):
                slotf_, skey_, nb_ = slots3[(n + 2) % 3]
                w1g_ = slotf_[:, 0:4096].rearrange("p (k n) -> p k n", k=8)
                w2g_ = slotf_[:, 4096:8192].rearrange("p (k n) -> p k n", k=4)
                return w1g_, w2g_, skey_, nb_

            def mlp_wload(n):
                gi_ = n % 8
                w1g_, w2g_, skey_, nb_ = mlp_views(n)
                wload(w1g_, skey_, mlp_w1[li, :, gi_ * 512:(gi_ + 1) * 512], 8, 512, nsplit=2, nb=nb_)
                wload(w2g_, skey_, mlp_w2[li, gi_ * 512:(gi_ + 1) * 512, :], 4, D, nsplit=2, nb=nb_)

            def mlp_prenorm(hf_):
                for tt in range(2):
                    t = hf_ * 2 + tt
                    c0 = t * 512
                    rstd = rms_stats(lambda k, c0=c0: xT[:, k, c0:c0 + 512], 512, lambda k, t=t: ("x", k, t), "pre")
                    for k in range(KC):
                        S.add("dve", I("scalar_tensor_tensor", out=hnhb[hf_][:, k, tt * 512:(tt + 1) * 512], in0=xT[:, k, c0:c0 + 512],
                                       scalar=gT[:, k, g0 + 4:g0 + 5], in1=rstd, op0=ALU.mult, op1=ALU.mult),
                              reads=[("x", k, t), "rstd", "consts"], writes=[("hnh", hf_, k, tt)])

            def mlp_up(n):
                nonlocal rlr
                hf_, gi_ = divmod(n, 8)
                ab = n % 2
                w1g_, w2g_, skey_, nb_ = mlp_views(n)
                hnh_ = hnhb[hf_]
                for tt in range(2):
                    for fc in range(4):
                        bank, bk = ps()
                        for k in range(KC):
                            S.add("pe", I("matmul", out=bank[:, :], lhsT=w1g_[:, k, fc * 128:(fc + 1) * 128],
                                          rhs=hnh_[:, k, tt * 512:(tt + 1) * 512], start=(k == 0), stop=(k == KC - 1)),
                                  reads=[skey_, ("hnh", hf_, k, tt)], writes=[bk])
                        ri = rlr % 2
                        rlr += 1
                        S.add("act", I("activation", out=rl[ri], in_=bank[:, :], func=AF.Relu), reads=[bk], writes=[("rl", ri)])
                        S.add("dve", I("tensor_tensor", out=aT[ab][:, fc, tt * 512:(tt + 1) * 512], in0=rl[ri], in1=rl[ri], op=ALU.mult),
                              reads=[("rl", ri)], writes=[("aT", ab, fc, tt)])

            def mlp_down(n):
                hf_, gi_ = divmod(n, 8)
                ab = n % 2
                w1g_, w2g_, skey_, nb_ = mlp_views(n)
                for tt in range(2):
                    for c in range(KC):
                        bank, bk = ps()
                        for fc in range(4):
                            S.add("pe", I("matmul", out=bank[:, :], lhsT=w2g_[:, fc, c * 128:(c + 1) * 128],
                                          rhs=aT[ab][:, fc, tt * 512:(tt + 1) * 512], start=(fc == 0), stop=(fc == 3)),
                                  reads=[skey_, ("aT", ab, fc, tt)], writes=[bk])
                        dst = yacc[:, c, tt * 512:(tt + 1) * 512]
                        if gi_ == 0:
                            evac_copy(dst, bank[:, :], [bk], [("yacc", c, tt)])
                        else:
                            S.add("dve", I("tensor_tensor", out=dst, in0=bank[:, :], in1=dst, op=ALU.add),
                                  reads=[bk], writes=[("yacc", c, tt)])
                if gi_ == 7:
                    for tt in range(2):
                        t = hf_ * 2 + tt
                        c0 = t * 512
                        yv = yacc[:, :, tt * 512:(tt + 1) * 512]
                        rstd = rms_stats(lambda k, yv=yv: yv[:, k, :], 512, lambda k, tt=tt: ("yacc", k, tt), "post")
                        for k in range(KC):
                            S.add("dve", I("scalar_tensor_tensor", out=yv[:, k, :], in0=yv[:, k, :], scalar=gT[:, k, g0 + 5:g0 + 6],
                                           in1=rstd, op0=ALU.mult, op1=ALU.mult),
                                  reads=["rstd", "consts"], writes=[("yacc", k, tt)])
                            S.add("pool", I("tensor_tensor", out=xT[:, k, c0:c0 + 512], in0=xT[:, k, c0:c0 + 512], in1=yv[:, k, :], op=ALU.add),
                                  reads=[("yacc", k, tt)], writes=[("x", k, t)])

            mlp_wload(1)
            mlp_wload(0)
            mlp_wload(2)
            mlp_prenorm(0)
            mlp_prenorm(1)
            mlp_up(0)
            for n in range(NG):
                if n + 1 < NG:
                    mlp_up(n + 1)
                mlp_down(n)
                if n + 3 < NG:
                    mlp_wload(n + 3)
            S.barrier()
            A.release(m0)


        ost = [A.f32(D), A.f32(D)]
        for tt in range(SEQ // 128):
            i = tt % 2
            for hh in range(2):
                bank, bk = ps()
                for kk in range(4):
                    k = hh * 4 + kk
                    S.add("pe", I("transpose", out=bank[:, kk * 128:(kk + 1) * 128], in_=xT[:, k, tt * 128:(tt + 1) * 128], identity=ident),
                          reads=[("x", k, tt // 4), "consts"], writes=[bk])
                evac_copy(ost[i][:, hh * 512:(hh + 1) * 512], bank[:, :], [bk], [("ost", i, hh)])
            S.add("sp", I("dma_start", out=out_d[tt * 128:(tt + 1) * 128, :], in_=ost[i]),
                  reads=[("ost", i, 0), ("ost", i, 1)], slot=f"o{i}")
        S.emit(nc, st, final_wait_slots=["o0", "o1"])
    return nc


def _consts():
    ident = np.eye(128, dtype=np.float32)
    rsw = np.zeros((128, 32), np.float32)
    for i in range(16):
        rsw[i + 16, i] = -1.0
        rsw[i, i + 16] = 1.0
    pos = np.arange(SEQ, dtype=np.float32)
    inv_freq = (np.float32(500000.0) ** (-np.arange(0, 32, 2, dtype=np.float32) / np.float32(32))).astype(np.float32)
    ang = (pos[:, None] * inv_freq[None, :]).astype(np.float32)
    cos = np.cos(ang).astype(np.float32).T
    sin = np.sin(ang).astype(np.float32).T
    cos32 = np.concatenate([cos, cos], axis=0)
    sin32 = np.concatenate([sin, sin], axis=0)
    kk = np.arange(128)[:, None]
    qq = np.arange(128)[None, :]
    tri = np.where(kk <= qq, 0.0, NEG).astype(np.float32)
    mown = np.zeros((2, 128, 256), np.float32)
    mown[0, :, 0:128] = tri
    mown[1, :, 0:128] = NEG
    mown[1, :, 128:256] = tri
    rc = np.broadcast_to((1.0 / np.arange(1, 17, dtype=np.float32))[None, :], (128, 16)).astype(np.float32).copy()
    return dict(c_ident=ident, c_rswap=rsw, c_cos=np.ascontiguousarray(cos32), c_sin=np.ascontiguousarray(sin32), c_mown=mown, c_rc=rc)


_PROG_CACHE = {}


def _get_prog(layers):
    key = tuple(layers)
    if key not in _PROG_CACHE:
        _PROG_CACHE[key] = build_program(list(layers))
    return _PROG_CACHE[key]


def _run(layers, x, mem, shared):
    nc = _get_prog(layers)
    in_maps = []
    for c in range(8):
        m = dict(shared)
        m["x"] = np.ascontiguousarray(x[c])
        m["mem"] = np.ascontiguousarray(mem[c])
        in_maps.append(m)
    res = run_bass_kernel_spmd(nc, in_maps, core_ids=list(range(8)))
    return np.stack([np.asarray(r["out"]) for r in res.results], axis=0)


LAUNCH_GROUPS = [[0, 1, 2, 3]]


def kernel(x, mem, norm_gains, mem_norm, pool_w_in, pool_w_group, pool_scale,
           moba_w_qkv, moba_w_o, xa_w_q, xa_w_kv, xa_w_o, mlp_w1, mlp_w2):
    f = lambda a: np.ascontiguousarray(np.asarray(a, dtype=np.float32))
    x = f(x)
    mem = f(mem)
    gains = np.concatenate([f(norm_gains).reshape(24, D), f(mem_norm).reshape(4, D), f(pool_scale).reshape(2, D)], axis=0)
    shared = dict(gains=np.ascontiguousarray(gains), pool_w_in=f(pool_w_in), pool_w_group=f(pool_w_group),
                  moba_w_qkv=f(moba_w_qkv), moba_w_o=f(moba_w_o), xa_w_q=f(xa_w_q), xa_w_kv=f(xa_w_kv),
                  xa_w_o=f(xa_w_o), mlp_w1=f(mlp_w1), mlp_w2=f(mlp_w2))
    shared.update(_consts())
    cur = x
    for grp in LAUNCH_GROUPS:
        cur = _run(grp, cur, mem, shared)
    return cur.astype(np.float32)
```

```python
import numpy as np
from contextlib import ExitStack
import concourse.bass as bass
import concourse.mybir as mybir
from concourse.bass_utils import run_bass_kernel_spmd

F32 = mybir.dt.float32
BF16 = mybir.dt.bfloat16
ALU = mybir.AluOpType
AF = mybir.ActivationFunctionType
AX = mybir.AxisListType

ENGS = ("pe", "act", "dve", "pool", "sp")


def I(method, **kw):
    return (method, kw)

D = 1024
KC = 8
SEQ = 2048
NMEM = 256
DEPTH = 4
DFF = 4096
NEG = -30000.0
WINDOWS = (2, 4, 8, 16)
EPS = 1e-6
DBG = set()


class Op:
    __slots__ = ("eng", "fn", "deps", "slot", "seq", "epoch", "ndep", "gid")

    def __init__(self, eng, fn, slot, epoch, gid):
        self.eng = eng
        self.fn = fn
        self.deps = []
        self.slot = slot
        self.seq = None
        self.epoch = epoch
        self.ndep = 0
        self.gid = gid


class Sched:
    def __init__(self):
        self.ops = {e: [] for e in ENGS}
        self.lw = {}
        self.rd = {}
        self.epoch = 0
        self.gid = 0
        self.last = {e: None for e in ENGS}
        self.barrier_deps = []
        self.dma_ops = []
        self.lastslot = {}

    def add(self, eng, fn, reads=(), writes=(), slot=None, nobarrier=False):
        o = Op(eng, fn, slot, self.epoch, self.gid)
        self.gid += 1
        deps = {}
        for k in reads:
            w = self.lw.get(k)
            if w is not None:
                deps[w.gid] = w
        for k in writes:
            w = self.lw.get(k)
            if w is not None:
                deps[w.gid] = w
            for r in self.rd.get(k, {}).values():
                deps[r.gid] = r
        if not nobarrier:
            for b in self.barrier_deps:
                deps[b.gid] = b
        if eng == "pe" and slot is None:
            deps = {g: d for g, d in deps.items() if not (d.eng == "pe" and d.slot is None)}
        o.deps = list(deps.values())
        for d in o.deps:
            d.ndep += 1
        rk = eng if slot is None else ("dma", o.gid)
        for k in reads:
            self.rd.setdefault(k, {})[rk] = o
        for k in writes:
            self.lw[k] = o
            self.rd[k] = {}
        self.ops[eng].append(o)
        if slot is not None:
            self.dma_ops.append(o)
            self.lastslot[slot] = o
        else:
            self.last[eng] = o
        return o

    def barrier(self, keep=()):
        deps = {}
        for e in ENGS:
            if self.last[e] is not None:
                deps[self.last[e].gid] = self.last[e]
        for o in self.lastslot.values():
            deps[o.gid] = o
        self.barrier_deps = list(deps.values())
        keep = set(keep) | {"WA", "WB"}
        self.lw = {k: v for k, v in self.lw.items() if k in keep}
        self.rd = {k: v for k, v in self.rd.items() if k in keep}

    def emit(self, nc, stack, final_wait_slots=()):
        n_epochs = self.epoch + 1
        sems = {}
        for e in ("pe", "act", "dve", "pool"):
            for ep in range(n_epochs):
                sems[(e, ep)] = stack.enter_context(nc.semaphore(f"s_{e}_{ep}"))
        slot_names = []
        for o in self.dma_ops:
            if o.slot not in slot_names:
                slot_names.append(o.slot)
        for s in slot_names:
            sems[("dma", s)] = stack.enter_context(nc.semaphore(f"d_{s}"))
        cnt = {}
        for e in ENGS:
            for o in self.ops[e]:
                if o.slot is not None:
                    k = ("dma", o.slot)
                    cnt[k] = cnt.get(k, 0) + 16
                    o.seq = cnt[k]
                elif o.ndep > 0:
                    k = (e, o.epoch)
                    cnt[k] = cnt.get(k, 0) + 1
                    o.seq = cnt[k]
        self.sem_counts = cnt

        def semkey(d):
            return ("dma", d.slot) if d.slot is not None else (d.eng, d.epoch)

        block = stack.enter_context(nc.Block())
        engobj = {"pe": "tensor", "act": "scalar", "dve": "vector", "pool": "gpsimd", "sp": "sync"}

        def run(e, eng):
            waited = {}
            for o in self.ops[e]:
                for d in o.deps:
                    if d.slot is None and d.eng == e and e == "pe":
                        continue
                    k = semkey(d)
                    if waited.get(k, 0) >= d.seq:
                        continue
                    eng.wait_ge(sems[k], d.seq)
                    waited[k] = d.seq
                ins = getattr(eng, o.fn[0])(**o.fn[1])
                if o.slot is not None:
                    ins.then_inc(sems[("dma", o.slot)], 16)
                elif o.ndep > 0:
                    ins.then_inc(sems[(e, o.epoch)], 1)
            if e == "sp":
                for s in final_wait_slots:
                    k = ("dma", s)
                    eng.wait_ge(sems[k], cnt[k])

        for e in ENGS:
            if not self.ops[e] and e != "sp":
                continue
            deco = getattr(block, engobj[e])

            def _f(eng, e=e):
                run(e, eng)
            deco(_f)


class Arena:
    def __init__(self, tensor, nwords):
        self.t = tensor
        self.n = nwords
        self.off = 0

    def f32(self, *shape):
        n = int(np.prod(shape))
        n_al = (n + 7) // 8 * 8
        assert self.off + n_al <= self.n, f"arena overflow {self.off}+{n_al}>{self.n}"
        a = self.t[:, self.off:self.off + n]
        self.off += n_al
        return self._shape(a, shape)

    def bf16(self, *shape):
        n = int(np.prod(shape))
        assert n % 2 == 0
        nw = n // 2
        n_al = (nw + 7) // 8 * 8
        assert self.off + n_al <= self.n, f"arena overflow {self.off}+{n_al}>{self.n}"
        a = self.t[:, self.off:self.off + nw].bitcast(BF16)
        self.off += n_al
        return self._shape(a, shape)

    @staticmethod
    def _shape(a, shape):
        if len(shape) == 1:
            return a
        if len(shape) == 2:
            return a.rearrange("p (a b) -> p a b", a=shape[0])
        if len(shape) == 3:
            return a.rearrange("p (a b c) -> p a b c", a=shape[0], b=shape[1])
        raise ValueError

    def mark(self):
        return self.off

    def release(self, m):
        self.off = m


def build_program(layers, first=True, last=True):
    nc = bass.Bass("TRN2", target_bir_lowering=False)

    def din(name, shape):
        return nc.dram_tensor(name, list(shape), F32, kind="ExternalInput").ap()

    x_d = din("x", (SEQ, D))
    mem_d = din("mem", (NMEM, D))
    gains_d = din("gains", (30, D))
    pool_w_in = din("pool_w_in", (2, D, D))
    pool_w_group = din("pool_w_group", (2, 4, 256, 256))
    moba_w_qkv = din("moba_w_qkv", (2, D, 3 * D))
    moba_w_o = din("moba_w_o", (2, D, D))
    xa_w_q = din("xa_w_q", (DEPTH, D, D))
    xa_w_kv = din("xa_w_kv", (DEPTH, D, 2 * D))
    xa_w_o = din("xa_w_o", (DEPTH, D, D))
    mlp_w1 = din("mlp_w1", (DEPTH, D, DFF))
    mlp_w2 = din("mlp_w2", (DEPTH, DFF, D))
    ident_d = din("c_ident", (128, 128))
    rsw_d = din("c_rswap", (128, 32))
    cos_d = din("c_cos", (32, SEQ))
    sin_d = din("c_sin", (32, SEQ))
    mown_d = din("c_mown", (2, 128, 256))
    rc_d = din("c_rc", (128, 16))
    out_d = nc.dram_tensor("out", [SEQ, D], F32, kind="ExternalOutput").ap()

    S = Sched()
    st = ExitStack()
    with st:
        NW = 212000 // 4
        arena_t = st.enter_context(nc.sbuf_tensor("arena", [128, NW], F32))
        A = Arena(arena_t, NW)
        banks = [st.enter_context(nc.psum_tensor(f"bank{i}", [128, 512], F32)) for i in range(8)]
        ps_rr = {"all": 0, "lo": 0}

        def ps(pool="all"):
            if pool == "all":
                i = ps_rr["all"] % 8
                ps_rr["all"] += 1
            else:
                i = ps_rr["lo"] % 4
                ps_rr["lo"] += 1
            return banks[i], ("ps", i)

        xT = A.f32(KC, SEQ)
        ident = A.f32(128)
        ident_bf = A.bf16(128)
        ones_bf = A.bf16(128)
        rsw = A.f32(32)
        epsb = A.f32(8)
        rc16 = A.f32(16)
        mown = A.bf16(2, 256)
        gT = A.f32(KC, 32)
        WA = A.bf16(KC, D)
        WB = A.bf16(KC, D)
        WAf = WA.rearrange("p a b -> p (a b)")
        WBf = WB.rearrange("p a b -> p (a b)")
        base_mark = A.mark()

        def xkeys(t512, ks=range(KC)):
            return [("x", k, t512) for k in ks]

        rot = {"sq": 0, "cp": 0}

        def wload(slot_ap, slotkey, src_ap, nk, ncols, nsplit=4, nb=True):
            src = src_ap.rearrange("(k p) n -> p k n", p=128)
            step = max(1, nk // nsplit)
            for k0 in range(0, nk, step):
                S.add("pool", I("dma_start", out=slot_ap[:, k0:k0 + step, :], in_=src[:, k0:k0 + step, :]),
                      writes=[slotkey], slot="w_" + slotkey, nobarrier=nb)

        def rms_stats(src_fn, W, src_reads, tag):
            sq = stats_bufs["sq"]
            bank, bk = ps()
            for k in range(KC):
                i = rot["sq"] % 2
                rot["sq"] += 1
                S.add("act", I("activation", out=sq[i][:, 0:W], in_=src_fn(k), func=AF.Square),
                      reads=[src_reads(k)], writes=[("sq", i)])
                S.add("pe", I("matmul", out=bank[:, 0:W], lhsT=ones_bf, rhs=sq[i][:, 0:W], start=(k == 0), stop=(k == KC - 1)),
                      reads=[("sq", i), "consts"], writes=[bk])
            ln = stats_bufs["ln"]
            rstd = stats_bufs["rstd"]
            S.add("act", I("activation", out=ln[:, 0:W], in_=bank[:, 0:W], func=AF.Ln, scale=1.0 / D, bias=epsb[:, 0:1]),
                  reads=[bk, "consts"], writes=["ln"])
            S.add("act", I("activation", out=rstd[:, 0:W], in_=ln[:, 0:W], func=AF.Exp, scale=-0.5),
                  reads=["ln"], writes=["rstd"])
            return rstd

        def prenorm(xcols, t512, W, gidx, dst, dstkey):
            c0 = xcols
            rstd = rms_stats(lambda k: xT[:, k, c0:c0 + W], W, lambda k: ("x", k, t512), "pre")
            for k in range(KC):
                S.add("dve", I("scalar_tensor_tensor", out=dst[:, k, 0:W], in0=xT[:, k, c0:c0 + W], scalar=gT[:, k, gidx:gidx + 1],
                                                                    in1=rstd[:, 0:W], op0=ALU.mult, op1=ALU.mult),
                      reads=[("x", k, t512), "rstd", "consts"], writes=[(dstkey, k)])

        def postnorm_residual(y, ykey, xcols, t512, W, gidx):
            c0 = xcols
            rstd = rms_stats(lambda k: y[:, k, 0:W], W, lambda k: (ykey, k), "post")
            for k in range(KC):
                S.add("dve", I("scalar_tensor_tensor", out=y[:, k, 0:W], in0=y[:, k, 0:W], scalar=gT[:, k, gidx:gidx + 1],
                                                                    in1=rstd[:, 0:W], op0=ALU.mult, op1=ALU.mult),
                      reads=["rstd", "consts"], writes=[(ykey, k)])
                S.add("pool", I("tensor_tensor", out=xT[:, k, c0:c0 + W], in0=xT[:, k, c0:c0 + W], in1=y[:, k, 0:W], op=ALU.add),
                      reads=[(ykey, k)], writes=[("x", k, t512)])

        def evac_copy(dst, src, reads, writes, scale=None):
            i = rot["cp"] % 2
            rot["cp"] += 1
            if scale is not None or i == 0:
                if scale is None:
                    S.add("act", I("activation", out=dst, in_=src, func=AF.Copy), reads=reads, writes=writes)
                else:
                    S.add("act", I("activation", out=dst, in_=src, func=AF.Copy, scale=scale), reads=reads + ["consts"], writes=writes)
            else:
                S.add("dve", I("tensor_copy", out=dst, in_=src), reads=reads, writes=writes)

        def proj(dst_fn, dstkey_fn, W_slot, wkey, src, srckey, W, nout=KC, col0=0, evac=None):
            for c in range(nout):
                bank, bk = ps()
                for k in range(KC):
                    S.add("pe", I("matmul", out=bank[:, 0:W], lhsT=W_slot[:, k, col0 + c * 128: col0 + (c + 1) * 128],
                                                                         rhs=src[:, k, 0:W], start=(k == 0), stop=(k == KC - 1)),
                          reads=[wkey, (srckey, k)], writes=[bk])
                if evac is None:
                    evac_copy(dst_fn(c), bank[:, 0:W], [bk], [dstkey_fn(c)])
                else:
                    evac(c, bank, bk)

        S.add("sp", I("dma_start", out=ident, in_=ident_d), writes=["c_id"], slot="c_id")
        S.add("sp", I("dma_start", out=rsw, in_=rsw_d), writes=["c_rs"], slot="c_rs")
        S.add("sp", I("dma_start", out=rc16, in_=rc_d), writes=["c_rc"], slot="c_rc")
        S.add("pool", I("dma_start", out=mown, in_=mown_d.rearrange("a p n -> p a n")), writes=["c_mo"], slot="c1")
        S.add("dve", I("tensor_copy", out=ident_bf, in_=ident), reads=["c_id"], writes=["c_idbf"])
        S.add("dve", I("memset", ap=ones_bf, constant=1.0), writes=["c_ones"])
        S.add("dve", I("memset", ap=epsb, constant=EPS), writes=["c_eps"])
        graw = A.f32(D)
        S.add("sp", I("dma_start", out=graw[0:30, :], in_=gains_d), writes=["graw"], slot="c_gr")
        for k in range(KC):
            bank, bk = ps()
            S.add("pe", I("transpose", out=bank[:, 0:30], in_=graw[0:30, k * 128:(k + 1) * 128], identity=ident[0:30, 0:30]),
                  reads=["graw", "c_id"], writes=[bk])
            S.add("dve", I("tensor_copy", out=gT[:, k, 0:30], in_=bank[:, 0:30]), reads=[bk], writes=["c_g"])
        xst = [A.f32(D), A.f32(D)]
        for tt in range(SEQ // 128):
            i = tt % 2
            S.add("sp", I("dma_start", out=xst[i], in_=x_d[tt * 128:(tt + 1) * 128, :]), writes=[("xst", i)], slot=f"xs{i}")
            for hh in range(2):
                bank, bk = ps()
                for kk in range(4):
                    k = hh * 4 + kk
                    S.add("pe", I("transpose", out=bank[:, kk * 128:(kk + 1) * 128], in_=xst[i][:, k * 128:(k + 1) * 128], identity=ident),
                          reads=[("xst", i), "c_id"], writes=[bk])
                dst = xT[:, hh * 4:(hh + 1) * 4, tt * 128:(tt + 1) * 128]
                src = bank[:, 0:512].rearrange("p (a b) -> p a b", a=4)
                evac_copy(dst, src, [bk], [("x", k, tt // 4) for k in range(hh * 4, hh * 4 + 4)])
        S.barrier()
        A.release(base_mark)

        pref = set()

        def mixer_load(l2, which):
            if (l2, "mix", which) in pref:
                return
            pref.add((l2, "mix", which))
            j2 = l2 // 2
            if l2 % 2 == 0:
                if which == "WA":
                    wload(WA, "WA", pool_w_in[j2], KC, D)
            else:
                if which == "WA":
                    wload(WA, "WA", moba_w_qkv[j2, :, D:2 * D], KC, D)
                else:
                    wload(WB, "WB", moba_w_qkv[j2, :, 2 * D:3 * D], KC, D)

        def xa_load(l2, which):
            if (l2, "xa", which) in pref:
                return
            pref.add((l2, "xa", which))
            if which == "WA":
                wload(WA, "WA", xa_w_kv[l2, :, 0:D], KC, D)
            else:
                wload(WB, "WB", xa_w_kv[l2, :, D:2 * D], KC, D)

        for lpos, li in enumerate(layers):
            nxt_li = layers[lpos + 1] if lpos + 1 < len(layers) else None
            S.epoch += 1
            j = li // 2
            g0 = li * 6
            if li % 2 == 0:
                m0 = A.mark()
                stats_bufs = {"sq": [A.bf16(512), A.bf16(512)], "ln": A.f32(512), "rstd": A.f32(512)}
                hnb = [A.bf16(KC, 512), A.bf16(KC, 512)]
                wg = A.bf16(4, 2, 256)
                U = A.f32(KC, 528)
                pp = [[A.f32(528), A.f32(528)] for _ in range(2)]
                pooled = A.bf16(KC, 512)
                fix = A.f32(16)
                y = A.f32(KC, 512)
                mixer_load(li, "WA")
                xa_load(li, "WB")
                for g in range(4):
                    S.add("pool", I("dma_start", out=wg[:, g, :, :], in_=pool_w_group[j, g].rearrange("(c p) n -> p c n", p=128)),
                          writes=["wg"], slot="w_wg", nobarrier=False)
                S.add("dve", I("memset", ap=U[:, :, 0:16], constant=0.0), writes=[("U", c) for c in range(KC)])
                prenorm(0, 0, 512, g0 + 0, hnb[0], ("hn", 0))
                for t in range(4):
                    if t + 1 < 4:
                        prenorm((t + 1) * 512, t + 1, 512, g0 + 0, hnb[(t + 1) % 2], ("hn", (t + 1) % 2))
                    hn, hk = hnb[t % 2], ("hn", t % 2)
                    if t > 0:
                        S.add("dve", I("tensor_copy", out=U[:, :, 0:16], in_=U[:, :, 512:528]),
                              reads=[("U", c) for c in range(KC)], writes=[("U", c) for c in range(KC)])
                    proj(lambda c: U[:, c, 16:528], lambda c: ("U", c), WA, "WA", hn, hk, 512)
                    if t == 3:
                        xa_load(li, "WA")
                    for c in range(KC):
                        w = WINDOWS[c // 2]
                        nl = {2: 1, 4: 2, 8: 3, 16: 4}[w]
                        bufs = pp[c % 2]
                        cur, curkey = U[:, c, :], ("U", c)
                        sh = 1
                        lo = 0
                        for lv in range(nl):
                            nxt, nxtkey = bufs[lv % 2], ("pp", c % 2, lv % 2)
                            lo2 = lo + sh
                            S.add("pool", I("tensor_tensor", out=nxt[:, lo2:528], in0=cur[:, lo2:528],
                                                                                                   in1=cur[:, lo2 - sh:528 - sh], op=ALU.add),
                                  reads=[curkey], writes=[nxtkey])
                            cur, curkey = nxt, nxtkey
                            lo = lo2
                            sh *= 2
                        S.add("dve", I("scalar_tensor_tensor", out=pooled[:, c, :], in0=cur[:, 16:528], scalar=1.0 / w,
                                                                                          in1=U[:, c, 16:528], op0=ALU.mult, op1=ALU.subtract),
                              reads=[curkey, ("U", c)], writes=[("pooled", c)])
                        if t == 0:
                            S.add("dve", I("tensor_tensor", out=fix[:, 0:15], in0=cur[:, 16:31], in1=rc16[:, 0:15], op=ALU.mult),
                                  reads=[curkey, "consts"], writes=["fix"])
                            S.add("dve", I("tensor_tensor", out=pooled[:, c, 0:w - 1], in0=fix[:, 0:w - 1], in1=U[:, c, 16:16 + w - 1], op=ALU.subtract),
                                  reads=["fix", ("U", c)], writes=[("pooled", c)])
                    for g in range(4):
                        for oc in range(2):
                            co = 2 * g + oc
                            bank, bk = ps()
                            for kc in range(2):
                                S.add("pe", I("matmul", out=bank[:, :], lhsT=wg[:, g, kc, oc * 128:(oc + 1) * 128],
                                                                                              rhs=pooled[:, 2 * g + kc, :], start=(kc == 0), stop=(kc == 1)),
                                      reads=["wg", ("pooled", 2 * g + kc)], writes=[bk])
                            evac_copy(y[:, co, :], bank[:, :], [bk], [("y", co)], scale=gT[:, co, 28 + j:29 + j])
                    postnorm_residual(y, "y", t * 512, t, 512, g0 + 1)
                S.barrier()
                A.release(m0)
            else:
                m0 = A.mark()
                kT = A.bf16(KC, SEQ)
                V = A.bf16(16, D)
                kmT = A.bf16(8, 8)
                kms = A.f32(8, 8)
                m1 = A.mark()
                stats_bufs = {"sq": [A.bf16(512), A.bf16(512)], "ln": A.f32(512), "rstd": A.f32(512)}
                hnb = [A.bf16(KC, 512), A.bf16(KC, 512)]
                kf = A.f32(512)
                t1 = A.f32(512)
                t2 = A.f32(512)
                rc_ = A.f32(512)
                rs_ = A.f32(512)
                mixer_load(li, "WA")
                mixer_load(li, "WB")

                def rope_part(dst, f32src, bank2, bk2, W, tag):
                    S.add("pe", I("matmul", out=bank2[0:32, 0:W], lhsT=rsw, rhs=f32src[:, 0:W], start=True, stop=True),
                          reads=[tag + "f", "consts"], writes=[bk2])
                    S.add("dve", I("tensor_tensor", out=t1[0:32, 0:W], in0=f32src[0:32, 0:W], in1=rc_[0:32, 0:W], op=ALU.mult),
                          reads=[tag + "f", "ropec"], writes=["t1"])
                    S.add("dve", I("tensor_tensor", out=t2[0:32, 0:W], in0=bank2[0:32, 0:W], in1=rs_[0:32, 0:W], op=ALU.mult),
                          reads=[bk2, "ropes"], writes=["t2"])

                prenorm(0, 0, 512, g0 + 0, hnb[0], ("hn", 0))
                for t in range(4):
                    if t + 1 < 4:
                        prenorm((t + 1) * 512, t + 1, 512, g0 + 0, hnb[(t + 1) % 2], ("hn", (t + 1) % 2))
                    hn, hk = hnb[t % 2], ("hn", t % 2)
                    S.add("sp", I("dma_start", out=rc_[0:32, :], in_=cos_d[:, t * 512:(t + 1) * 512]), writes=["ropec"], slot="rc")
                    S.add("sp", I("dma_start", out=rs_[0:32, :], in_=sin_d[:, t * 512:(t + 1) * 512]), writes=["ropes"], slot="rs")
                    for h in range(0 if "nok" in DBG else 8):
                        bank, bk = ps()
                        for k in range(KC):
                            S.add("pe", I("matmul", out=bank[:, :], lhsT=WA[:, k, h * 128:(h + 1) * 128], rhs=hn[:, k, :],
                                                                                 start=(k == 0), stop=(k == KC - 1)),
                                  reads=["WA", (hk, k)], writes=[bk])
                        S.add("act", I("activation", out=kf, in_=bank[:, :], func=AF.Copy), reads=[bk], writes=["kf"])
                        kdst = kT[:, h, t * 512:(t + 1) * 512]
                        bank2, bk2 = ps()
                        rope_part(None, kf, bank2, bk2, 512, "k")
                        S.add("dve", I("tensor_tensor", out=kf[0:32, :], in0=t1[0:32, :], in1=t2[0:32, :], op=ALU.add),
                              reads=["t1", "t2"], writes=["kf"])
                        S.add("dve", I("tensor_copy", out=kdst, in_=kf), reads=["kf"], writes=[("kT", h)])
                        S.add("dve", I("tensor_reduce", out=kms[:, h, 2 * t:2 * t + 2], in_=kf.rearrange("p (n s) -> p n s", n=2), axis=AX.X, op=ALU.add),
                              reads=["kf"], writes=["kms"])
                    for ts in range(0 if "nov" in DBG else 4):
                        for hf in range(2):
                            bank, bk = ps()
                            for k in range(KC):
                                S.add("pe", I("matmul", out=bank[:, :], lhsT=hn[:, k, ts * 128:(ts + 1) * 128],
                                                                                              rhs=WB[:, k, hf * 512:(hf + 1) * 512], start=(k == 0), stop=(k == KC - 1)),
                                      reads=["WB", (hk, k)], writes=[bk])
                            evac_copy(V[:, t * 4 + ts, hf * 512:(hf + 1) * 512], bank[:, :], [bk], [("V", t * 4 + ts)])
                keep = set([("kT", h) for h in range(8)] + [("V", i) for i in range(16)] + ["kms"])
                S.barrier(keep=keep)
                A.release(m1)
                S.epoch += 1
                NU = 0 if "nop2" in DBG else 8
                GATE_FROM = 99 if "nogate" in DBG else 4
                stats_bufs = {"sq": [A.bf16(256), A.bf16(256)], "ln": A.f32(256), "rstd": A.f32(256)}
                hnb = [A.bf16(KC, 256), A.bf16(KC, 256)]
                qf = A.f32(256)
                t1 = A.f32(256)
                t2 = A.f32(256)
                rc_ = A.f32(256)
                rs_ = A.f32(256)
                qt_ = A.bf16(KC, 256)
                ao = A.bf16(KC, 256)
                y = A.f32(KC, 256)
                pT = [A.bf16(512) for _ in range(3)]
                rec = A.f32(256)
                maskb = A.bf16(2, 8, 8)
                Gs = A.f32(64)
                cmpb = A.f32(8 * 49)
                cntb = A.f32(56)
                wload(WA, "WA", moba_w_qkv[j, :, 0:D], KC, D)
                wload(WB, "WB", moba_w_o[j], KC, D)
                S.add("dve", I("memset", ap=maskb, constant=0.0), writes=["maskb"])
                SC = 128 ** -0.5
                prr = 0
                for u in range(NU):
                    t512 = u // 2
                    c0 = u * 256
                    if u == 0:
                        prenorm(0, 0, 256, g0 + 0, hnb[0], ("hn", 0))
                    if u + 1 < NU:
                        prenorm((u + 1) * 256, (u + 1) // 2, 256, g0 + 0, hnb[(u + 1) % 2], ("hn", (u + 1) % 2))
                    hn, hk = hnb[u % 2], ("hn", u % 2)
                    S.add("sp", I("dma_start", out=rc_[0:32, :], in_=cos_d[:, c0:c0 + 256]), writes=["ropec"], slot="rc")
                    S.add("sp", I("dma_start", out=rs_[0:32, :], in_=sin_d[:, c0:c0 + 256]), writes=["ropes"], slot="rs")
                    for h in range(8):
                        bank, bk = ps("lo")
                        for k in range(KC):
                            S.add("pe", I("matmul", out=bank[:, 0:256], lhsT=WA[:, k, h * 128:(h + 1) * 128], rhs=hn[:, k, :],
                                                                                 start=(k == 0), stop=(k == KC - 1)),
                                  reads=["WA", (hk, k)], writes=[bk])
                        S.add("act", I("activation", out=qf, in_=bank[:, 0:256], func=AF.Copy), reads=[bk], writes=["qf"])
                        bank2, bk2 = ps("lo")
                        rope_part(None, qf, bank2, bk2, 256, "q")
                        S.add("dve", I("tensor_tensor", out=qf[0:32, :], in0=t1[0:32, :], in1=t2[0:32, :], op=ALU.add),
                              reads=["t1", "t2"], writes=["qf"])
                        S.add("dve", I("tensor_copy", out=qt_[:, h, :], in_=qf), reads=["qf"], writes=[("q", h)])
                        if u == NU - 1 and h == 7:
                            xa_load(li, "WA")
                        if u >= GATE_FROM:
                            for qi in range(2):
                                S.add("pe", I("matmul", out=banks[6 + qi][:, h * 8:(h + 1) * 8], lhsT=qf[:, qi * 128:(qi + 1) * 128],
                                              rhs=kms[:, h, :], start=True, stop=True),
                                      reads=["qf", "kms"], writes=[("ps", 6 + qi)])
                    if u >= GATE_FROM:
                        for qi in range(2):
                            bank, bk = banks[6 + qi], ("ps", 6 + qi)
                            S.add("dve", I("tensor_copy", out=Gs, in_=bank[:, 0:64]), reads=[bk], writes=["Gs"])
                            G3 = Gs.rearrange("p (h n) -> p h n", n=8)
                            cm = cmpb[:, 0:8 * u * u].rearrange("p (h n m) -> p h n m", n=u, m=u)
                            in_m = G3[:, :, 0:u].unsqueeze(2).broadcast_to([128, 8, u, u])
                            in_n = G3[:, :, 0:u].unsqueeze(3).broadcast_to([128, 8, u, u])
                            cn = cntb[:, 0:8 * u].rearrange("p (h n) -> p h n", n=u)
                            S.add("dve", I("tensor_tensor", out=cm, in0=in_m, in1=in_n, op=ALU.is_gt),
                                  reads=["Gs"], writes=["cmp"])
                            S.add("dve", I("tensor_reduce", out=cn, in_=cm, axis=AX.X, op=ALU.add), reads=["cmp"], writes=["cnt"])
                            S.add("dve", I("tensor_single_scalar", out=cn, in_=cn, scalar=3.0, op=ALU.is_ge), reads=["cnt"], writes=["cnt"])
                            S.add("dve", I("tensor_single_scalar", out=maskb[:, qi, :, 0:u], in_=cn, scalar=NEG, op=ALU.mult),
                                  reads=["cnt"], writes=["maskb"])
                    for h in range(8):
                        tiles = [("own", 0), ("own", 1)] + [(n, jj) for n in range(u) for jj in range(2)]
                        pairs = [tiles[i:i + 2] for i in range(0, len(tiles), 2)]
                        bo, bok = banks[4 + 2 * (h % 2)], ("ps", 4 + 2 * (h % 2))
                        br, brk = banks[5 + 2 * (h % 2)], ("ps", 5 + 2 * (h % 2))
                        npairs = len(pairs)

                        def keytile(tl):
                            return (u * 2 + tl[1]) if tl[0] == "own" else (tl[0] * 2 + tl[1])

                        def emit_qk(pi):
                            nonlocal prr
                            bank, bk = ps("lo")
                            pbuf = prr % 3
                            prr += 1
                            for idx, tl in enumerate(pairs[pi]):
                                kt = keytile(tl)
                                reg = bank[:, idx * 256:(idx + 1) * 256]
                                masked = (tl[0] == "own") or (u >= GATE_FROM)
                                S.add("pe", I("matmul", out=reg, lhsT=kT[:, h, kt * 128:(kt + 1) * 128], rhs=qt_[:, h, :],
                                                                                              start=True, stop=(not masked)),
                                      reads=[("kT", h), ("q", h)], writes=[bk])
                                if tl[0] == "own":
                                    S.add("pe", I("matmul", out=reg, lhsT=ident_bf, rhs=mown[:, tl[1], :], start=False, stop=True),
                                          reads=["consts"], writes=[bk])
                                elif u >= GATE_FROM:
                                    for qi in range(2):
                                        S.add("pe", I("matmul", out=reg[:, qi * 128:(qi + 1) * 128],
                                                                                                lhsT=maskb[:, qi, h, tl[0]:tl[0] + 1].broadcast_to([128, 128]),
                                                                                                rhs=ident_bf, start=False, stop=(qi == 1)),
                                              reads=["maskb", "consts"], writes=[bk])
                            S.add("act", I("activation", out=pT[pbuf], in_=bank[:, :], func=AF.Exp, scale=SC),
                                  reads=[bk], writes=[("pT", pbuf)])
                            return pbuf

                        def emit_pv(pi, pbuf):
                            for idx, tl in enumerate(pairs[pi]):
                                kt = keytile(tl)
                                first = (pi == 0 and idx == 0)
                                lastm = (pi == npairs - 1 and idx == 1)
                                S.add("pe", I("matmul", out=
                                    bo[:, 0:256], lhsT=V[:, kt, h * 128:(h + 1) * 128], rhs=pT[pbuf][:, idx * 256:(idx + 1) * 256], start=first, stop=lastm),
                                    reads=[("V", kt), ("pT", pbuf)], writes=[bok])
                                S.add("pe", I("matmul", out=
                                    br[:, 0:256], lhsT=ones_bf, rhs=pT[pbuf][:, idx * 256:(idx + 1) * 256], start=first, stop=lastm),
                                    reads=[("pT", pbuf), "consts"], writes=[brk])

                        pb = emit_qk(0)
                        for pi in range(npairs):
                            pbn = emit_qk(pi + 1) if pi + 1 < npairs else None
                            emit_pv(pi, pb)
                            pb = pbn
                        S.add("dve", I("reciprocal", out=rec, in_=br[:, 0:256]), reads=[brk], writes=["rec"])
                        S.add("dve", I("tensor_tensor", out=ao[:, h, :], in0=bo[:, 0:256], in1=rec, op=ALU.mult),
                              reads=[bok, "rec"], writes=[("ao", h)])
                    for c in range(KC):
                        bank, bk = ps("lo")
                        for k in range(KC):
                            S.add("pe", I("matmul", out=bank[:, 0:256], lhsT=WB[:, k, c * 128:(c + 1) * 128], rhs=ao[:, k, :],
                                                                                 start=(k == 0), stop=(k == KC - 1)),
                                  reads=["WB", ("ao", k)], writes=[bk])
                        evac_copy(y[:, c, :], bank[:, 0:256], [bk], [("y", c)])
                    postnorm_residual(y, "y", c0, t512, 256, g0 + 1)
                S.barrier()
                A.release(m0)

            if "noxm" in DBG:
                continue
            S.epoch += 1
            m0 = A.mark()
            stats_bufs = {"sq": [A.bf16(512), A.bf16(512)], "ln": A.f32(512), "rstd": A.f32(512)}
            mst = A.f32(2, D)
            memT = A.f32(KC, NMEM)
            memn = A.bf16(KC, NMEM)
            kTm = A.bf16(KC, NMEM)
            Vm = A.bf16(2, D)
            hnb = [A.bf16(KC, 512), A.bf16(KC, 512)]
            qx = A.bf16(KC, 512)
            ao = A.bf16(KC, 512)
            y = A.f32(KC, 512)
            pT = [A.bf16(512) for _ in range(4)]
            rec = A.f32(512)
            WC = A.bf16(KC, D)
            xa_load(li, "WA")
            xa_load(li, "WB")
            wload(WC, "WC", xa_w_q[li], KC, D, nb=False)
            S.add("sp", I("dma_start", out=mst, in_=mem_d.rearrange("(a p) d -> p a d", p=128)), writes=["mst"], slot="mst")
            for a in range(2):
                for hh in range(2):
                    bank, bk = ps()
                    for kk in range(4):
                        k = hh * 4 + kk
                        S.add("pe", I("transpose", out=bank[:, kk * 128:(kk + 1) * 128], in_=mst[:, a, k * 128:(k + 1) * 128], identity=ident),
                              reads=["mst", "consts"], writes=[bk])
                    evac_copy(memT[:, hh * 4:(hh + 1) * 4, a * 128:(a + 1) * 128], bank[:, 0:512].rearrange("p (a b) -> p a b", a=4), [bk],
                              [("memT", k) for k in range(hh * 4, hh * 4 + 4)])
            rstd = rms_stats(lambda k: memT[:, k, :], NMEM, lambda k: ("memT", k), "mem")
            for k in range(KC):
                S.add("dve", I("scalar_tensor_tensor", out=memn[:, k, :], in0=memT[:, k, :], scalar=gT[:, k, 24 + li:25 + li],
                                                                    in1=rstd[:, 0:NMEM], op0=ALU.mult, op1=ALU.mult),
                      reads=[("memT", k), "rstd", "consts"], writes=[("memn", k)])
            proj(lambda c: kTm[:, c, :], lambda c: ("kTm", c), WA, "WA", memn, "memn", NMEM)
            wload(WAf[:, 0:4096].rearrange("p (k n) -> p k n", k=8), "WA", mlp_w1[li, :, 512:1024], 8, 512, nsplit=2)
            wload(WAf[:, 4096:8192].rearrange("p (k n) -> p k n", k=4), "WA", mlp_w2[li, 512:1024, :], 4, D, nsplit=2)
            pref.add((li, "mlp", 1))
            for a in range(2):
                for hf in range(2):
                    bank, bk = ps()
                    for k in range(KC):
                        S.add("pe", I("matmul", out=bank[:, :], lhsT=memn[:, k, a * 128:(a + 1) * 128],
                                                                                    rhs=WB[:, k, hf * 512:(hf + 1) * 512], start=(k == 0), stop=(k == KC - 1)),
                              reads=["WB", ("memn", k)], writes=[bk])
                    evac_copy(Vm[:, a, hf * 512:(hf + 1) * 512], bank[:, :], [bk], [("Vm", a)])
            wload(WB, "WB", xa_w_o[li], KC, D)
            SCX = 256 ** -0.5
            prr = 0
            prenorm(0, 0, 512, g0 + 2, hnb[0], ("hn", 0))
            for t in range(4):
                if t + 1 < 4:
                    prenorm((t + 1) * 512, t + 1, 512, g0 + 2, hnb[(t + 1) % 2], ("hn", (t + 1) % 2))
                hn, hk = hnb[t % 2], ("hn", t % 2)
                proj(lambda c: qx[:, c, :], lambda c: ("qx", c), WC, "WC", hn, hk, 512)
                for h in range(4):
                    pbs = []
                    for a in range(2):
                        bank, bk = ps()
                        for cc in range(2):
                            S.add("pe", I("matmul", out=bank[:, :], lhsT=kTm[:, 2 * h + cc, a * 128:(a + 1) * 128],
                                                                                        rhs=qx[:, 2 * h + cc, :], start=(cc == 0), stop=(cc == 1)),
                                  reads=[("kTm", 2 * h + cc), ("qx", 2 * h + cc)], writes=[bk])
                        pbuf = prr % 4
                        prr += 1
                        S.add("act", I("activation", out=pT[pbuf], in_=bank[:, :], func=AF.Exp, scale=SCX),
                              reads=[bk], writes=[("pT", pbuf)])
                        pbs.append(pbuf)
                    bankr, bkr = ps()
                    for a in range(2):
                        S.add("pe", I("matmul", out=bankr[:, :], lhsT=ones_bf, rhs=pT[pbs[a]], start=(a == 0), stop=(a == 1)),
                              reads=[("pT", pbs[a]), "consts"], writes=[bkr])
                    S.add("dve", I("reciprocal", out=rec, in_=bankr[:, :]), reads=[bkr], writes=["rec"])
                    for cc in range(2):
                        banko, bko = ps()
                        for a in range(2):
                            S.add("pe", I("matmul", out=banko[:, :], lhsT=Vm[:, a, (2 * h + cc) * 128:(2 * h + cc + 1) * 128],
                                                                                                    rhs=pT[pbs[a]], start=(a == 0), stop=(a == 1)),
                                  reads=[("Vm", a), ("pT", pbs[a])], writes=[bko])
                        S.add("dve", I("tensor_tensor", out=ao[:, 2 * h + cc, :], in0=banko[:, :], in1=rec, op=ALU.mult),
                              reads=[bko, "rec"], writes=[("ao", 2 * h + cc)])
                proj(lambda c: y[:, c, :], lambda c: ("y", c), WB, "WB", ao, "ao", 512)
                postnorm_residual(y, "y", t * 512, t, 512, g0 + 3)
            S.barrier()
            A.release(m0)

            S.epoch += 1
            m0 = A.mark()
            stats_bufs = {"sq": [A.bf16(512), A.bf16(512)], "ln": A.f32(512), "rstd": A.f32(512)}
            hnhb = [A.bf16(KC, 1024), A.bf16(KC, 1024)]
            yacc = A.f32(KC, 1024)
            aT = [A.bf16(4, 1024), A.bf16(4, 1024)]
            rl = [A.f32(512), A.f32(512)]
            WC = A.bf16(KC, D)
            WCf = WC.rearrange("p a b -> p (a b)")
            slots3 = [(WAf, "WA", True), (WBf, "WB", True), (WCf, "WC", False)]
            rlr = 0
            NG = 16

            def mlp_views(n):
                slotf_, skey_, nb_ = slots3[(n + 2) % 3]
                w1g_ = slotf_[:, 0:4096].rearrange("p (k n) -> p k n", k=8)
                w2g_ = slotf_[:, 4096:8192].rearrange("p (k n) -> p k n", k=4)
                return w1g_, w2g_, skey_, nb_

            def mlp_wload(n):
                gi_ = n % 8
                w1g_, w2g_, skey_, nb_ = mlp_views(n)
                wload(w1g_, skey_, mlp_w1[li, :, gi_ * 512:(gi_ + 1) * 512], 8, 512, nsplit=2, nb=nb_)
                wload(w2g_, skey_, mlp_w2[li, gi_ * 512:(gi_ + 1) * 512, :], 4, D, nsplit=2, nb=nb_)

            def mlp_prenorm(hf_):
                for tt in range(2):
                    t = hf_ * 2 + tt
                    c0 = t * 512
                    rstd = rms_stats(lambda k, c0=c0: xT[:, k, c0:c0 + 512], 512, lambda k, t=t: ("x", k, t), "pre")
                    for k in range(KC):
                        S.add("dve", I("scalar_tensor_tensor", out=hnhb[hf_][:, k, tt * 512:(tt + 1) * 512], in0=xT[:, k, c0:c0 + 512],
                                       scalar=gT[:, k, g0 + 4:g0 + 5], in1=rstd, op0=ALU.mult, op1=ALU.mult),
                              reads=[("x", k, t), "rstd", "consts"], writes=[("hnh", hf_, k, tt)])

            def mlp_up(n):
                nonlocal rlr
                hf_, gi_ = divmod(n, 8)
                ab = n % 2
                w1g_, w2g_, skey_, nb_ = mlp_views(n)
                hnh_ = hnhb[hf_]
                for tt in range(2):
                    for fc in range(4):
                        bank, bk = ps()
                        for k in range(KC):
                            S.add("pe", I("matmul", out=bank[:, :], lhsT=w1g_[:, k, fc * 128:(fc + 1) * 128],
                                          rhs=hnh_[:, k, tt * 512:(tt + 1) * 512], start=(k == 0), stop=(k == KC - 1)),
                                  reads=[skey_, ("hnh", hf_, k, tt)], writes=[bk])
                        ri = rlr % 2
                        rlr += 1
                        S.add("act", I("activation", out=rl[ri], in_=bank[:, :], func=AF.Relu), reads=[bk], writes=[("rl", ri)])
                        S.add("dve", I("tensor_tensor", out=aT[ab][:, fc, tt * 512:(tt + 1) * 512], in0=rl[ri], in1=rl[ri], op=ALU.mult),
                              reads=[("rl", ri)], writes=[("aT", ab, fc, tt)])

            def mlp_down(n):
                hf_, gi_ = divmod(n, 8)
                ab = n % 2
                w1g_, w2g_, skey_, nb_ = mlp_views(n)
                for tt in range(2):
                    for c in range(KC):
                        bank, bk = ps()
                        for fc in range(4):
                            S.add("pe", I("matmul", out=bank[:, :], lhsT=w2g_[:, fc, c * 128:(c + 1) * 128],
                                          rhs=aT[ab][:, fc, tt * 512:(tt + 1) * 512], start=(fc == 0), stop=(fc == 3)),
                                  reads=[skey_, ("aT", ab, fc, tt)], writes=[bk])
                        dst = yacc[:, c, tt * 512:(tt + 1) * 512]
                        if gi_ == 0:
                            evac_copy(dst, bank[:, :], [bk], [("yacc", c, tt)])
                        else:
                            S.add("dve", I("tensor_tensor", out=dst, in0=bank[:, :], in1=dst, op=ALU.add),
                                  reads=[bk], writes=[("yacc", c, tt)])
                if gi_ == 7:
                    for tt in range(2):
                        t = hf_ * 2 + tt
                        c0 = t * 512
                        yv = yacc[:, :, tt * 512:(tt + 1) * 512]
                        rstd = rms_stats(lambda k, yv=yv: yv[:, k, :], 512, lambda k, tt=tt: ("yacc", k, tt), "post")
                        for k in range(KC):
                            S.add("dve", I("scalar_tensor_tensor", out=yv[:, k, :], in0=yv[:, k, :], scalar=gT[:, k, g0 + 5:g0 + 6],
                                           in1=rstd, op0=ALU.mult, op1=ALU.mult),
                                  reads=["rstd", "consts"], writes=[("yacc", k, tt)])
                            S.add("pool", I("tensor_tensor", out=xT[:, k, c0:c0 + 512], in0=xT[:, k, c0:c0 + 512], in1=yv[:, k, :], op=ALU.add),
                                  reads=[("yacc", k, tt)], writes=[("x", k, t)])

            if (li, "mlp", 1) not in pref:
                mlp_wload(1)
            mlp_wload(0)
            mlp_wload(2)
            mlp_prenorm(0)
            mlp_prenorm(1)
            mlp_up(0)
            for n in range(NG):
                if n + 1 < NG:
                    mlp_up(n + 1)
                mlp_down(n)
                if n + 3 < NG:
                    mlp_wload(n + 3)
                if nxt_li is not None and n == 13:
                    mixer_load(nxt_li, "WA")
                if nxt_li is not None and n == 14:
                    if nxt_li % 2 == 0:
                        xa_load(nxt_li, "WB")
                    else:
                        mixer_load(nxt_li, "WB")
            S.barrier()
            A.release(m0)


        ost = [A.f32(D), A.f32(D)]
        for tt in range(SEQ // 128):
            i = tt % 2
            for hh in range(2):
                bank, bk = ps()
                for kk in range(4):
                    k = hh * 4 + kk
                    S.add("pe", I("transpose", out=bank[:, kk * 128:(kk + 1) * 128], in_=xT[:, k, tt * 128:(tt + 1) * 128], identity=ident),
                          reads=[("x", k, tt // 4), "consts"], writes=[bk])
                evac_copy(ost[i][:, hh * 512:(hh + 1) * 512], bank[:, :], [bk], [("ost", i, hh)])
            S.add("sp", I("dma_start", out=out_d[tt * 128:(tt + 1) * 128, :], in_=ost[i]),
                  reads=[("ost", i, 0), ("ost", i, 1)], slot=f"o{i}")
        S.emit(nc, st, final_wait_slots=["o0", "o1"])
    return nc


def _consts():
    ident = np.eye(128, dtype=np.float32)
    rsw = np.zeros((128, 32), np.float32)
    for i in range(16):
        rsw[i + 16, i] = -1.0
        rsw[i, i + 16] = 1.0
    pos = np.arange(SEQ, dtype=np.float32)
    inv_freq = (np.float32(500000.0) ** (-np.arange(0, 32, 2, dtype=np.float32) / np.float32(32))).astype(np.float32)
    ang = (pos[:, None] * inv_freq[None, :]).astype(np.float32)
    cos = np.cos(ang).astype(np.float32).T
    sin = np.sin(ang).astype(np.float32).T
    cos32 = np.concatenate([cos, cos], axis=0)
    sin32 = np.concatenate([sin, sin], axis=0)
    kk = np.arange(128)[:, None]
    qq = np.arange(128)[None, :]
    tri = np.where(kk <= qq, 0.0, NEG).astype(np.float32)
    mown = np.zeros((2, 128, 256), np.float32)
    mown[0, :, 0:128] = tri
    mown[1, :, 0:128] = NEG
    mown[1, :, 128:256] = tri
    rc = np.broadcast_to((1.0 / np.arange(1, 17, dtype=np.float32))[None, :], (128, 16)).astype(np.float32).copy()
    return dict(c_ident=ident, c_rswap=rsw, c_cos=np.ascontiguousarray(cos32), c_sin=np.ascontiguousarray(sin32), c_mown=mown, c_rc=rc)


_PROG_CACHE = {}


def _get_prog(layers):
    key = tuple(layers)
    if key not in _PROG_CACHE:
        _PROG_CACHE[key] = build_program(list(layers))
    return _PROG_CACHE[key]


def _run(layers, x, mem, shared):
    nc = _get_prog(layers)
    in_maps = []
    for c in range(8):
        m = dict(shared)
        m["x"] = np.ascontiguousarray(x[c])
        m["mem"] = np.ascontiguousarray(mem[c])
        in_maps.append(m)
    res = run_bass_kernel_spmd(nc, in_maps, core_ids=list(range(8)))
    return np.stack([np.asarray(r["out"]) for r in res.results], axis=0)


LAUNCH_GROUPS = [[0, 1, 2, 3]]


def kernel(x, mem, norm_gains, mem_norm, pool_w_in, pool_w_group, pool_scale,
           moba_w_qkv, moba_w_o, xa_w_q, xa_w_kv, xa_w_o, mlp_w1, mlp_w2):
    f = lambda a: np.ascontiguousarray(np.asarray(a, dtype=np.float32))
    x = f(x)
    mem = f(mem)
    gains = np.concatenate([f(norm_gains).reshape(24, D), f(mem_norm).reshape(4, D), f(pool_scale).reshape(2, D)], axis=0)
    shared = dict(gains=np.ascontiguousarray(gains), pool_w_in=f(pool_w_in), pool_w_group=f(pool_w_group),
                  moba_w_qkv=f(moba_w_qkv), moba_w_o=f(moba_w_o), xa_w_q=f(xa_w_q), xa_w_kv=f(xa_w_kv),
                  xa_w_o=f(xa_w_o), mlp_w1=f(mlp_w1), mlp_w2=f(mlp_w2))
    shared.update(_consts())
    cur = x
    for grp in LAUNCH_GROUPS:
        cur = _run(grp, cur, mem, shared)
    return cur.astype(np.float32)
```

```python
import numpy as np
from contextlib import ExitStack
import concourse.bass as bass
import concourse.mybir as mybir
from concourse.bass_utils import run_bass_kernel_spmd

F32 = mybir.dt.float32
BF16 = mybir.dt.bfloat16
ALU = mybir.AluOpType
AF = mybir.ActivationFunctionType
AX = mybir.AxisListType

ENGS = ("pe", "act", "dve", "pool", "sp")


def I(method, **kw):
    return (method, kw)

D = 1024
KC = 8
SEQ = 2048
NMEM = 256
DEPTH = 4
DFF = 4096
NEG = -30000.0
WINDOWS = (2, 4, 8, 16)
EPS = 1e-6
DBG = set()


class Op:
    __slots__ = ("eng", "fn", "deps", "slot", "seq", "epoch", "ndep", "gid")

    def __init__(self, eng, fn, slot, epoch, gid):
        self.eng = eng
        self.fn = fn
        self.deps = []
        self.slot = slot
        self.seq = None
        self.epoch = epoch
        self.ndep = 0
        self.gid = gid


class Sched:
    def __init__(self):
        self.ops = {e: [] for e in ENGS}
        self.lw = {}
        self.rd = {}
        self.epoch = 0
        self.gid = 0
        self.last = {e: None for e in ENGS}
        self.barrier_deps = []
        self.dma_ops = []
        self.lastslot = {}

    def add(self, eng, fn, reads=(), writes=(), slot=None, nobarrier=False):
        o = Op(eng, fn, slot, self.epoch, self.gid)
        self.gid += 1
        deps = {}
        for k in reads:
            w = self.lw.get(k)
            if w is not None:
                deps[w.gid] = w
        for k in writes:
            w = self.lw.get(k)
            if w is not None:
                deps[w.gid] = w
            for r in self.rd.get(k, {}).values():
                deps[r.gid] = r
        if not nobarrier:
            for b in self.barrier_deps:
                deps[b.gid] = b
        if eng == "pe" and slot is None:
            deps = {g: d for g, d in deps.items() if not (d.eng == "pe" and d.slot is None)}
        o.deps = list(deps.values())
        for d in o.deps:
            d.ndep += 1
        rk = eng if slot is None else ("dma", o.gid)
        for k in reads:
            self.rd.setdefault(k, {})[rk] = o
        for k in writes:
            self.lw[k] = o
            self.rd[k] = {}
        self.ops[eng].append(o)
        if slot is not None:
            self.dma_ops.append(o)
            self.lastslot[slot] = o
        else:
            self.last[eng] = o
        return o

    def barrier(self, keep=()):
        deps = {}
        for e in ENGS:
            if self.last[e] is not None:
                deps[self.last[e].gid] = self.last[e]
        for o in self.lastslot.values():
            deps[o.gid] = o
        self.barrier_deps = list(deps.values())
        keep = set(keep) | {"WA", "WB"}
        self.lw = {k: v for k, v in self.lw.items() if k in keep}
        self.rd = {k: v for k, v in self.rd.items() if k in keep}

    def emit(self, nc, stack, final_wait_slots=()):
        n_epochs = self.epoch + 1
        sems = {}
        for e in ("pe", "act", "dve", "pool"):
            for ep in range(n_epochs):
                sems[(e, ep)] = stack.enter_context(nc.semaphore(f"s_{e}_{ep}"))
        slot_names = []
        for o in self.dma_ops:
            if o.slot not in slot_names:
                slot_names.append(o.slot)
        for s in slot_names:
            sems[("dma", s)] = stack.enter_context(nc.semaphore(f"d_{s}"))
        cnt = {}
        for e in ENGS:
            for o in self.ops[e]:
                if o.slot is not None:
                    k = ("dma", o.slot)
                    cnt[k] = cnt.get(k, 0) + 16
                    o.seq = cnt[k]
                elif o.ndep > 0:
                    k = (e, o.epoch)
                    cnt[k] = cnt.get(k, 0) + 1
                    o.seq = cnt[k]
        self.sem_counts = cnt

        def semkey(d):
            return ("dma", d.slot) if d.slot is not None else (d.eng, d.epoch)

        block = stack.enter_context(nc.Block())
        engobj = {"pe": "tensor", "act": "scalar", "dve": "vector", "pool": "gpsimd", "sp": "sync"}

        def run(e, eng):
            waited = {}
            for o in self.ops[e]:
                for d in o.deps:
                    if d.slot is None and d.eng == e and e == "pe":
                        continue
                    k = semkey(d)
                    if waited.get(k, 0) >= d.seq:
                        continue
                    eng.wait_ge(sems[k], d.seq)
                    waited[k] = d.seq
                ins = getattr(eng, o.fn[0])(**o.fn[1])
                if o.slot is not None:
                    ins.then_inc(sems[("dma", o.slot)], 16)
                elif o.ndep > 0:
                    ins.then_inc(sems[(e, o.epoch)], 1)
            if e == "sp":
                for s in final_wait_slots:
                    k = ("dma", s)
                    eng.wait_ge(sems[k], cnt[k])

        for e in ENGS:
            if not self.ops[e] and e != "sp":
                continue
            deco = getattr(block, engobj[e])

            def _f(eng, e=e):
                run(e, eng)
            deco(_f)


class Arena:
    def __init__(self, tensor, nwords):
        self.t = tensor
        self.n = nwords
        self.off = 0

    def f32(self, *shape):
        n = int(np.prod(shape))
        n_al = (n + 7) // 8 * 8
        assert self.off + n_al <= self.n, f"arena overflow {self.off}+{n_al}>{self.n}"
        a = self.t[:, self.off:self.off + n]
        self.off += n_al
        return self._shape(a, shape)

    def bf16(self, *shape):
        n = int(np.prod(shape))
        assert n % 2 == 0
        nw = n // 2
        n_al = (nw + 7) // 8 * 8
        assert self.off + n_al <= self.n, f"arena overflow {self.off}+{n_al}>{self.n}"
        a = self.t[:, self.off:self.off + nw].bitcast(BF16)
        self.off += n_al
        return self._shape(a, shape)

    @staticmethod
    def _shape(a, shape):
        if len(shape) == 1:
            return a
        if len(shape) == 2:
            return a.rearrange("p (a b) -> p a b", a=shape[0])
        if len(shape) == 3:
            return a.rearrange("p (a b c) -> p a b c", a=shape[0], b=shape[1])
        raise ValueError

    def mark(self):
        return self.off

    def release(self, m):
        self.off = m


def build_program(layers, first=True, last=True):
    nc = bass.Bass("TRN2", target_bir_lowering=False)

    def din(name, shape):
        return nc.dram_tensor(name, list(shape), F32, kind="ExternalInput").ap()

    x_d = din("x", (SEQ, D))
    mem_d = din("mem", (NMEM, D))
    gains_d = din("gains", (30, D))
    pool_w_in = din("pool_w_in", (2, D, D))
    pool_w_group = din("pool_w_group", (2, 4, 256, 256))
    moba_w_qkv = din("moba_w_qkv", (2, D, 3 * D))
    moba_w_o = din("moba_w_o", (2, D, D))
    xa_w_q = din("xa_w_q", (DEPTH, D, D))
    xa_w_kv = din("xa_w_kv", (DEPTH, D, 2 * D))
    xa_w_o = din("xa_w_o", (DEPTH, D, D))
    mlp_w1 = din("mlp_w1", (DEPTH, D, DFF))
    mlp_w2 = din("mlp_w2", (DEPTH, DFF, D))
    ident_d = din("c_ident", (128, 128))
    rsw_d = din("c_rswap", (128, 32))
    cos_d = din("c_cos", (32, SEQ))
    sin_d = din("c_sin", (32, SEQ))
    mown_d = din("c_mown", (2, 128, 256))
    rc_d = din("c_rc", (128, 16))
    out_d = nc.dram_tensor("out", [SEQ, D], F32, kind="ExternalOutput").ap()

    S = Sched()
    st = ExitStack()
    with st:
        NW = 212000 // 4
        arena_t = st.enter_context(nc.sbuf_tensor("arena", [128, NW], F32))
        A = Arena(arena_t, NW)
        banks = [st.enter_context(nc.psum_tensor(f"bank{i}", [128, 512], F32)) for i in range(8)]
        ps_rr = {"all": 0, "lo": 0}

        def ps(pool="all"):
            if pool == "all":
                i = ps_rr["all"] % 8
                ps_rr["all"] += 1
            else:
                i = ps_rr["lo"] % 4
                ps_rr["lo"] += 1
            return banks[i], ("ps", i)

        xT = A.f32(KC, SEQ)
        ident = A.f32(128)
        ident_bf = A.bf16(128)
        ones_bf = A.bf16(128)
        rsw = A.f32(32)
        epsb = A.f32(8)
        rc16 = A.f32(16)
        mown = A.bf16(2, 256)
        gT = A.f32(KC, 32)
        WA = A.bf16(KC, D)
        WB = A.bf16(KC, D)
        WAf = WA.rearrange("p a b -> p (a b)")
        WBf = WB.rearrange("p a b -> p (a b)")
        base_mark = A.mark()

        def xkeys(t512, ks=range(KC)):
            return [("x", k, t512) for k in ks]

        rot = {"sq": 0, "cp": 0}

        def wload(slot_ap, slotkey, src_ap, nk, ncols, nsplit=4, nb=True):
            src = src_ap.rearrange("(k p) n -> p k n", p=128)
            step = max(1, nk // nsplit)
            for k0 in range(0, nk, step):
                S.add("pool", I("dma_start", out=slot_ap[:, k0:k0 + step, :], in_=src[:, k0:k0 + step, :]),
                      writes=[slotkey], slot="w_" + slotkey, nobarrier=nb)

        def rms_stats(src_fn, W, src_reads, tag):
            sq = stats_bufs["sq"]
            bank, bk = ps()
            for k in range(KC):
                i = rot["sq"] % 2
                rot["sq"] += 1
                S.add("act", I("activation", out=sq[i][:, 0:W], in_=src_fn(k), func=AF.Square),
                      reads=[src_reads(k)], writes=[("sq", i)])
                S.add("pe", I("matmul", out=bank[:, 0:W], lhsT=ones_bf, rhs=sq[i][:, 0:W], start=(k == 0), stop=(k == KC - 1)),
                      reads=[("sq", i), "consts"], writes=[bk])
            ln = stats_bufs["ln"]
            rstd = stats_bufs["rstd"]
            S.add("act", I("activation", out=ln[:, 0:W], in_=bank[:, 0:W], func=AF.Ln, scale=1.0 / D, bias=epsb[:, 0:1]),
                  reads=[bk, "consts"], writes=["ln"])
            S.add("act", I("activation", out=rstd[:, 0:W], in_=ln[:, 0:W], func=AF.Exp, scale=-0.5),
                  reads=["ln"], writes=["rstd"])
            return rstd

        def prenorm(xcols, t512, W, gidx, dst, dstkey):
            c0 = xcols
            rstd = rms_stats(lambda k: xT[:, k, c0:c0 + W], W, lambda k: ("x", k, t512), "pre")
            for k in range(KC):
                S.add("dve", I("scalar_tensor_tensor", out=dst[:, k, 0:W], in0=xT[:, k, c0:c0 + W], scalar=gT[:, k, gidx:gidx + 1],
                                                                    in1=rstd[:, 0:W], op0=ALU.mult, op1=ALU.mult),
                      reads=[("x", k, t512), "rstd", "consts"], writes=[(dstkey, k)])

        def postnorm_residual(y, ykey, xcols, t512, W, gidx):
            c0 = xcols
            rstd = rms_stats(lambda k: y[:, k, 0:W], W, lambda k: (ykey, k), "post")
            for k in range(KC):
                S.add("dve", I("scalar_tensor_tensor", out=y[:, k, 0:W], in0=y[:, k, 0:W], scalar=gT[:, k, gidx:gidx + 1],
                                                                    in1=rstd[:, 0:W], op0=ALU.mult, op1=ALU.mult),
                      reads=["rstd", "consts"], writes=[(ykey, k)])
            for k in range(KC):
                S.add("dve", I("tensor_tensor", out=xT[:, k, c0:c0 + W], in0=xT[:, k, c0:c0 + W], in1=y[:, k, 0:W], op=ALU.add),
                      reads=[(ykey, k)], writes=[("x", k, t512)])

        def evac_copy(dst, src, reads, writes, scale=None):
            i = rot["cp"] % 2
            rot["cp"] += 1
            if scale is not None or i == 0:
                if scale is None:
                    S.add("act", I("activation", out=dst, in_=src, func=AF.Copy), reads=reads, writes=writes)
                else:
                    S.add("act", I("activation", out=dst, in_=src, func=AF.Copy, scale=scale), reads=reads + ["consts"], writes=writes)
            else:
                S.add("dve", I("tensor_copy", out=dst, in_=src), reads=reads, writes=writes)

        def proj(dst_fn, dstkey_fn, W_slot, wkey, src, srckey, W, nout=KC, col0=0, evac=None):
            for c in range(nout):
                bank, bk = ps()
                for k in range(KC):
                    S.add("pe", I("matmul", out=bank[:, 0:W], lhsT=W_slot[:, k, col0 + c * 128: col0 + (c + 1) * 128],
                                                                         rhs=src[:, k, 0:W], start=(k == 0), stop=(k == KC - 1)),
                          reads=[wkey, (srckey, k)], writes=[bk])
                if evac is None:
                    evac_copy(dst_fn(c), bank[:, 0:W], [bk], [dstkey_fn(c)])
                else:
                    evac(c, bank, bk)

        S.add("sp", I("dma_start", out=ident, in_=ident_d), writes=["c_id"], slot="c_id")
        S.add("sp", I("dma_start", out=rsw, in_=rsw_d), writes=["c_rs"], slot="c_rs")
        S.add("sp", I("dma_start", out=rc16, in_=rc_d), writes=["c_rc"], slot="c_rc")
        S.add("pool", I("dma_start", out=mown, in_=mown_d.rearrange("a p n -> p a n")), writes=["c_mo"], slot="c1")
        S.add("dve", I("tensor_copy", out=ident_bf, in_=ident), reads=["c_id"], writes=["c_idbf"])
        S.add("dve", I("memset", ap=ones_bf, constant=1.0), writes=["c_ones"])
        S.add("dve", I("memset", ap=epsb, constant=EPS), writes=["c_eps"])
        graw = A.f32(D)
        S.add("sp", I("dma_start", out=graw[0:30, :], in_=gains_d), writes=["graw"], slot="c_gr")
        for k in range(KC):
            bank, bk = ps()
            S.add("pe", I("transpose", out=bank[:, 0:30], in_=graw[0:30, k * 128:(k + 1) * 128], identity=ident[0:30, 0:30]),
                  reads=["graw", "c_id"], writes=[bk])
            S.add("dve", I("tensor_copy", out=gT[:, k, 0:30], in_=bank[:, 0:30]), reads=[bk], writes=["c_g"])
        xst = [A.f32(D), A.f32(D)]
        for tt in range(SEQ // 128):
            i = tt % 2
            S.add("sp", I("dma_start", out=xst[i], in_=x_d[tt * 128:(tt + 1) * 128, :]), writes=[("xst", i)], slot=f"xs{i}")
            for hh in range(2):
                bank, bk = ps()
                for kk in range(4):
                    k = hh * 4 + kk
                    S.add("pe", I("transpose", out=bank[:, kk * 128:(kk + 1) * 128], in_=xst[i][:, k * 128:(k + 1) * 128], identity=ident),
                          reads=[("xst", i), "c_id"], writes=[bk])
                dst = xT[:, hh * 4:(hh + 1) * 4, tt * 128:(tt + 1) * 128]
                src = bank[:, 0:512].rearrange("p (a b) -> p a b", a=4)
                evac_copy(dst, src, [bk], [("x", k, tt // 4) for k in range(hh * 4, hh * 4 + 4)])
        S.barrier()
        A.release(base_mark)

        pref = set()

        def mixer_load(l2, which):
            if (l2, "mix", which) in pref:
                return
            pref.add((l2, "mix", which))
            j2 = l2 // 2
            if l2 % 2 == 0:
                if which == "WA":
                    wload(WA, "WA", pool_w_in[j2], KC, D)
            else:
                if which == "WA":
                    wload(WA, "WA", moba_w_qkv[j2, :, D:2 * D], KC, D)
                else:
                    wload(WB, "WB", moba_w_qkv[j2, :, 2 * D:3 * D], KC, D)

        def xa_load(l2, which):
            if (l2, "xa", which) in pref:
                return
            pref.add((l2, "xa", which))
            if which == "WA":
                wload(WA, "WA", xa_w_kv[l2, :, 0:D], KC, D)
            else:
                wload(WB, "WB", xa_w_kv[l2, :, D:2 * D], KC, D)

        for lpos, li in enumerate(layers):
            nxt_li = layers[lpos + 1] if lpos + 1 < len(layers) else None
            S.epoch += 1
            j = li // 2
            g0 = li * 6
            if li % 2 == 0:
                m0 = A.mark()
                stats_bufs = {"sq": [A.bf16(512), A.bf16(512)], "ln": A.f32(512), "rstd": A.f32(512)}
                hnb = [A.bf16(KC, 512), A.bf16(KC, 512)]
                wg = A.bf16(4, 2, 256)
                U = A.f32(KC, 528)
                pp = [[A.f32(528), A.f32(528)] for _ in range(2)]
                pooled = A.bf16(KC, 512)
                fix = A.f32(16)
                y = A.f32(KC, 512)
                mixer_load(li, "WA")
                xa_load(li, "WB")
                for g in range(4):
                    S.add("pool", I("dma_start", out=wg[:, g, :, :], in_=pool_w_group[j, g].rearrange("(c p) n -> p c n", p=128)),
                          writes=["wg"], slot="w_wg", nobarrier=False)
                S.add("dve", I("memset", ap=U[:, :, 0:16], constant=0.0), writes=[("U", c) for c in range(KC)])
                prenorm(0, 0, 512, g0 + 0, hnb[0], ("hn", 0))
                for t in range(4):
                    if t + 1 < 4:
                        prenorm((t + 1) * 512, t + 1, 512, g0 + 0, hnb[(t + 1) % 2], ("hn", (t + 1) % 2))
                    hn, hk = hnb[t % 2], ("hn", t % 2)
                    if t > 0:
                        S.add("dve", I("tensor_copy", out=U[:, :, 0:16], in_=U[:, :, 512:528]),
                              reads=[("U", c) for c in range(KC)], writes=[("U", c) for c in range(KC)])
                    proj(lambda c: U[:, c, 16:528], lambda c: ("U", c), WA, "WA", hn, hk, 512)
                    if t == 3:
                        xa_load(li, "WA")
                    for c in range(KC):
                        w = WINDOWS[c // 2]
                        nl = {2: 1, 4: 2, 8: 3, 16: 4}[w]
                        bufs = pp[c % 2]
                        cur, curkey = U[:, c, :], ("U", c)
                        sh = 1
                        lo = 0
                        for lv in range(nl):
                            nxt, nxtkey = bufs[lv % 2], ("pp", c % 2, lv % 2)
                            lo2 = lo + sh
                            S.add("pool", I("tensor_tensor", out=nxt[:, lo2:528], in0=cur[:, lo2:528],
                                                                                                   in1=cur[:, lo2 - sh:528 - sh], op=ALU.add),
                                  reads=[curkey], writes=[nxtkey])
                            cur, curkey = nxt, nxtkey
                            lo = lo2
                            sh *= 2
                        S.add("dve", I("scalar_tensor_tensor", out=pooled[:, c, :], in0=cur[:, 16:528], scalar=1.0 / w,
                                                                                          in1=U[:, c, 16:528], op0=ALU.mult, op1=ALU.subtract),
                              reads=[curkey, ("U", c)], writes=[("pooled", c)])
                        if t == 0:
                            S.add("dve", I("tensor_tensor", out=fix[:, 0:15], in0=cur[:, 16:31], in1=rc16[:, 0:15], op=ALU.mult),
                                  reads=[curkey, "consts"], writes=["fix"])
                            S.add("dve", I("tensor_tensor", out=pooled[:, c, 0:w - 1], in0=fix[:, 0:w - 1], in1=U[:, c, 16:16 + w - 1], op=ALU.subtract),
                                  reads=["fix", ("U", c)], writes=[("pooled", c)])
                    for g in range(4):
                        for oc in range(2):
                            co = 2 * g + oc
                            bank, bk = ps()
                            for kc in range(2):
                                S.add("pe", I("matmul", out=bank[:, :], lhsT=wg[:, g, kc, oc * 128:(oc + 1) * 128],
                                                                                              rhs=pooled[:, 2 * g + kc, :], start=(kc == 0), stop=(kc == 1)),
                                      reads=["wg", ("pooled", 2 * g + kc)], writes=[bk])
                            evac_copy(y[:, co, :], bank[:, :], [bk], [("y", co)], scale=gT[:, co, 28 + j:29 + j])
                    postnorm_residual(y, "y", t * 512, t, 512, g0 + 1)
                S.barrier()
                A.release(m0)
            else:
                m0 = A.mark()
                kT = A.bf16(KC, SEQ)
                V = A.bf16(16, D)
                kmT = A.bf16(8, 8)
                kms = A.f32(8, 8)
                m1 = A.mark()
                stats_bufs = {"sq": [A.bf16(512), A.bf16(512)], "ln": A.f32(512), "rstd": A.f32(512)}
                hnb = [A.bf16(KC, 512), A.bf16(KC, 512)]
                kf = A.f32(512)
                t1 = A.f32(512)
                t2 = A.f32(512)
                rc_ = A.f32(512)
                rs_ = A.f32(512)
                mixer_load(li, "WA")
                mixer_load(li, "WB")

                def rope_part(dst, f32src, bank2, bk2, W, tag):
                    S.add("pe", I("matmul", out=bank2[0:32, 0:W], lhsT=rsw, rhs=f32src[:, 0:W], start=True, stop=True),
                          reads=[tag + "f", "consts"], writes=[bk2])
                    S.add("dve", I("tensor_tensor", out=t1[0:32, 0:W], in0=f32src[0:32, 0:W], in1=rc_[0:32, 0:W], op=ALU.mult),
                          reads=[tag + "f", "ropec"], writes=["t1"])
                    S.add("dve", I("tensor_tensor", out=t2[0:32, 0:W], in0=bank2[0:32, 0:W], in1=rs_[0:32, 0:W], op=ALU.mult),
                          reads=[bk2, "ropes"], writes=["t2"])

                prenorm(0, 0, 512, g0 + 0, hnb[0], ("hn", 0))
                for t in range(4):
                    if t + 1 < 4:
                        prenorm((t + 1) * 512, t + 1, 512, g0 + 0, hnb[(t + 1) % 2], ("hn", (t + 1) % 2))
                    hn, hk = hnb[t % 2], ("hn", t % 2)
                    S.add("sp", I("dma_start", out=rc_[0:32, :], in_=cos_d[:, t * 512:(t + 1) * 512]), writes=["ropec"], slot="rc")
                    S.add("sp", I("dma_start", out=rs_[0:32, :], in_=sin_d[:, t * 512:(t + 1) * 512]), writes=["ropes"], slot="rs")
                    for h in range(0 if "nok" in DBG else 8):
                        bank, bk = ps()
                        for k in range(KC):
                            S.add("pe", I("matmul", out=bank[:, :], lhsT=WA[:, k, h * 128:(h + 1) * 128], rhs=hn[:, k, :],
                                                                                 start=(k == 0), stop=(k == KC - 1)),
                                  reads=["WA", (hk, k)], writes=[bk])
                        S.add("act", I("activation", out=kf, in_=bank[:, :], func=AF.Copy), reads=[bk], writes=["kf"])
                        kdst = kT[:, h, t * 512:(t + 1) * 512]
                        bank2, bk2 = ps()
                        rope_part(None, kf, bank2, bk2, 512, "k")
                        S.add("dve", I("tensor_tensor", out=kf[0:32, :], in0=t1[0:32, :], in1=t2[0:32, :], op=ALU.add),
                              reads=["t1", "t2"], writes=["kf"])
                        S.add("dve", I("tensor_copy", out=kdst, in_=kf), reads=["kf"], writes=[("kT", h)])
                        S.add("dve", I("tensor_reduce", out=kms[:, h, 2 * t:2 * t + 2], in_=kf.rearrange("p (n s) -> p n s", n=2), axis=AX.X, op=ALU.add),
                              reads=["kf"], writes=["kms"])
                    for ts in range(0 if "nov" in DBG else 4):
                        for hf in range(2):
                            bank, bk = ps()
                            for k in range(KC):
                                S.add("pe", I("matmul", out=bank[:, :], lhsT=hn[:, k, ts * 128:(ts + 1) * 128],
                                                                                              rhs=WB[:, k, hf * 512:(hf + 1) * 512], start=(k == 0), stop=(k == KC - 1)),
                                      reads=["WB", (hk, k)], writes=[bk])
                            evac_copy(V[:, t * 4 + ts, hf * 512:(hf + 1) * 512], bank[:, :], [bk], [("V", t * 4 + ts)])
                keep = set([("kT", h) for h in range(8)] + [("V", i) for i in range(16)] + ["kms"])
                S.barrier(keep=keep)
                A.release(m1)
                S.epoch += 1
                NU = 0 if "nop2" in DBG else 8
                GATE_FROM = 99 if "nogate" in DBG else 4
                stats_bufs = {"sq": [A.bf16(256), A.bf16(256)], "ln": A.f32(256), "rstd": A.f32(256)}
                hnb = [A.bf16(KC, 256), A.bf16(KC, 256)]
                qf = A.f32(256)
                t1 = A.f32(256)
                t2 = A.f32(256)
                rc_ = A.f32(256)
                rs_ = A.f32(256)
                qt_ = A.bf16(KC, 256)
                ao = A.bf16(KC, 256)
                y = A.f32(KC, 256)
                pT = [A.bf16(512) for _ in range(3)]
                rec = A.f32(256)
                maskb = A.bf16(2, 8, 8)
                Gs = A.f32(64)
                cmpb = A.f32(8 * 49)
                cntb = A.f32(56)
                wload(WA, "WA", moba_w_qkv[j, :, 0:D], KC, D)
                wload(WB, "WB", moba_w_o[j], KC, D)
                S.add("dve", I("memset", ap=maskb, constant=0.0), writes=["maskb"])
                SC = 128 ** -0.5
                prr = 0
                for u in range(NU):
                    t512 = u // 2
                    c0 = u * 256
                    if u == 0:
                        prenorm(0, 0, 256, g0 + 0, hnb[0], ("hn", 0))
                    if u + 1 < NU:
                        prenorm((u + 1) * 256, (u + 1) // 2, 256, g0 + 0, hnb[(u + 1) % 2], ("hn", (u + 1) % 2))
                    hn, hk = hnb[u % 2], ("hn", u % 2)
                    S.add("sp", I("dma_start", out=rc_[0:32, :], in_=cos_d[:, c0:c0 + 256]), writes=["ropec"], slot="rc")
                    S.add("sp", I("dma_start", out=rs_[0:32, :], in_=sin_d[:, c0:c0 + 256]), writes=["ropes"], slot="rs")
                    for h in range(8):
                        bank, bk = ps("lo")
                        for k in range(KC):
                            S.add("pe", I("matmul", out=bank[:, 0:256], lhsT=WA[:, k, h * 128:(h + 1) * 128], rhs=hn[:, k, :],
                                                                                 start=(k == 0), stop=(k == KC - 1)),
                                  reads=["WA", (hk, k)], writes=[bk])
                        S.add("act", I("activation", out=qf, in_=bank[:, 0:256], func=AF.Copy), reads=[bk], writes=["qf"])
                        bank2, bk2 = ps("lo")
                        rope_part(None, qf, bank2, bk2, 256, "q")
                        S.add("dve", I("tensor_tensor", out=qf[0:32, :], in0=t1[0:32, :], in1=t2[0:32, :], op=ALU.add),
                              reads=["t1", "t2"], writes=["qf"])
                        S.add("dve", I("tensor_copy", out=qt_[:, h, :], in_=qf), reads=["qf"], writes=[("q", h)])
                        if u == NU - 1 and h == 7:
                            xa_load(li, "WA")
                        if u >= GATE_FROM:
                            for qi in range(2):
                                S.add("pe", I("matmul", out=banks[6 + qi][:, h * 8:(h + 1) * 8], lhsT=qf[:, qi * 128:(qi + 1) * 128],
                                              rhs=kms[:, h, :], start=True, stop=True),
                                      reads=["qf", "kms"], writes=[("ps", 6 + qi)])
                    if u >= GATE_FROM:
                        for qi in range(2):
                            bank, bk = banks[6 + qi], ("ps", 6 + qi)
                            S.add("dve", I("tensor_copy", out=Gs, in_=bank[:, 0:64]), reads=[bk], writes=["Gs"])
                            G3 = Gs.rearrange("p (h n) -> p h n", n=8)
                            cm = cmpb[:, 0:8 * u * u].rearrange("p (h n m) -> p h n m", n=u, m=u)
                            in_m = G3[:, :, 0:u].unsqueeze(2).broadcast_to([128, 8, u, u])
                            in_n = G3[:, :, 0:u].unsqueeze(3).broadcast_to([128, 8, u, u])
                            cn = cntb[:, 0:8 * u].rearrange("p (h n) -> p h n", n=u)
                            S.add("dve", I("tensor_tensor", out=cm, in0=in_m, in1=in_n, op=ALU.is_gt),
                                  reads=["Gs"], writes=["cmp"])
                            S.add("dve", I("tensor_reduce", out=cn, in_=cm, axis=AX.X, op=ALU.add), reads=["cmp"], writes=["cnt"])
                            S.add("dve", I("tensor_single_scalar", out=cn, in_=cn, scalar=3.0, op=ALU.is_ge), reads=["cnt"], writes=["cnt"])
                            S.add("dve", I("tensor_single_scalar", out=maskb[:, qi, :, 0:u], in_=cn, scalar=NEG, op=ALU.mult),
                                  reads=["cnt"], writes=["maskb"])
                    for h in range(8):
                        tiles = [("own", 0), ("own", 1)] + [(n, jj) for n in range(u) for jj in range(2)]
                        pairs = [tiles[i:i + 2] for i in range(0, len(tiles), 2)]
                        bo, bok = banks[4 + 2 * (h % 2)], ("ps", 4 + 2 * (h % 2))
                        br, brk = banks[5 + 2 * (h % 2)], ("ps", 5 + 2 * (h % 2))
                        npairs = len(pairs)

                        def keytile(tl):
                            return (u * 2 + tl[1]) if tl[0] == "own" else (tl[0] * 2 + tl[1])

                        def emit_qk(pi):
                            nonlocal prr
                            bank, bk = ps("lo")
                            pbuf = prr % 3
                            prr += 1
                            for idx, tl in enumerate(pairs[pi]):
                                kt = keytile(tl)
                                reg = bank[:, idx * 256:(idx + 1) * 256]
                                masked = (tl[0] == "own") or (u >= GATE_FROM)
                                S.add("pe", I("matmul", out=reg, lhsT=kT[:, h, kt * 128:(kt + 1) * 128], rhs=qt_[:, h, :],
                                                                                              start=True, stop=(not masked)),
                                      reads=[("kT", h), ("q", h)], writes=[bk])
                                if tl[0] == "own":
                                    S.add("pe", I("matmul", out=reg, lhsT=ident_bf, rhs=mown[:, tl[1], :], start=False, stop=True),
                                          reads=["consts"], writes=[bk])
                                elif u >= GATE_FROM:
                                    for qi in range(2):
                                        S.add("pe", I("matmul", out=reg[:, qi * 128:(qi + 1) * 128],
                                                                                                lhsT=maskb[:, qi, h, tl[0]:tl[0] + 1].broadcast_to([128, 128]),
                                                                                                rhs=ident_bf, start=False, stop=(qi == 1)),
                                              reads=["maskb", "consts"], writes=[bk])
                            S.add("act", I("activation", out=pT[pbuf], in_=bank[:, :], func=AF.Exp, scale=SC),
                                  reads=[bk], writes=[("pT", pbuf)])
                            return pbuf

                        def emit_pv(pi, pbuf):
                            for idx, tl in enumerate(pairs[pi]):
                                kt = keytile(tl)
                                first = (pi == 0 and idx == 0)
                                lastm = (pi == npairs - 1 and idx == 1)
                                S.add("pe", I("matmul", out=
                                    bo[:, 0:256], lhsT=V[:, kt, h * 128:(h + 1) * 128], rhs=pT[pbuf][:, idx * 256:(idx + 1) * 256], start=first, stop=lastm),
                                    reads=[("V", kt), ("pT", pbuf)], writes=[bok])
                                S.add("pe", I("matmul", out=
                                    br[:, 0:256], lhsT=ones_bf, rhs=pT[pbuf][:, idx * 256:(idx + 1) * 256], start=first, stop=lastm),
                                    reads=[("pT", pbuf), "consts"], writes=[brk])

                        pb = emit_qk(0)
                        for pi in range(npairs):
                            pbn = emit_qk(pi + 1) if pi + 1 < npairs else None
                            emit_pv(pi, pb)
                            pb = pbn
                        S.add("dve", I("reciprocal", out=rec, in_=br[:, 0:256]), reads=[brk], writes=["rec"])
                        S.add("dve", I("tensor_tensor", out=ao[:, h, :], in0=bo[:, 0:256], in1=rec, op=ALU.mult),
                              reads=[bok, "rec"], writes=[("ao", h)])
                    for c in range(KC):
                        bank, bk = ps("lo")
                        for k in range(KC):
                            S.add("pe", I("matmul", out=bank[:, 0:256], lhsT=WB[:, k, c * 128:(c + 1) * 128], rhs=ao[:, k, :],
                                                                                 start=(k == 0), stop=(k == KC - 1)),
                                  reads=["WB", ("ao", k)], writes=[bk])
                        evac_copy(y[:, c, :], bank[:, 0:256], [bk], [("y", c)])
                    postnorm_residual(y, "y", c0, t512, 256, g0 + 1)
                S.barrier()
                A.release(m0)

            if "noxm" in DBG:
                continue
            S.epoch += 1
            m0 = A.mark()
            stats_bufs = {"sq": [A.bf16(512), A.bf16(512)], "ln": A.f32(512), "rstd": A.f32(512)}
            mst = A.f32(2, D)
            memT = A.f32(KC, NMEM)
            memn = A.bf16(KC, NMEM)
            kTm = A.bf16(KC, NMEM)
            Vm = A.bf16(2, D)
            hnb = [A.bf16(KC, 512), A.bf16(KC, 512)]
            qx = A.bf16(KC, 512)
            ao = A.bf16(KC, 512)
            y = A.f32(KC, 512)
            pT = [A.bf16(512) for _ in range(4)]
            rec = A.f32(512)
            WC = A.bf16(KC, D)
            xa_load(li, "WA")
            xa_load(li, "WB")
            wload(WC, "WC", xa_w_q[li], KC, D, nb=False)
            S.add("sp", I("dma_start", out=mst, in_=mem_d.rearrange("(a p) d -> p a d", p=128)), writes=["mst"], slot="mst")
            for a in range(2):
                for hh in range(2):
                    bank, bk = ps()
                    for kk in range(4):
                        k = hh * 4 + kk
                        S.add("pe", I("transpose", out=bank[:, kk * 128:(kk + 1) * 128], in_=mst[:, a, k * 128:(k + 1) * 128], identity=ident),
                              reads=["mst", "consts"], writes=[bk])
                    evac_copy(memT[:, hh * 4:(hh + 1) * 4, a * 128:(a + 1) * 128], bank[:, 0:512].rearrange("p (a b) -> p a b", a=4), [bk],
                              [("memT", k) for k in range(hh * 4, hh * 4 + 4)])
            rstd = rms_stats(lambda k: memT[:, k, :], NMEM, lambda k: ("memT", k), "mem")
            for k in range(KC):
                S.add("dve", I("scalar_tensor_tensor", out=memn[:, k, :], in0=memT[:, k, :], scalar=gT[:, k, 24 + li:25 + li],
                                                                    in1=rstd[:, 0:NMEM], op0=ALU.mult, op1=ALU.mult),
                      reads=[("memT", k), "rstd", "consts"], writes=[("memn", k)])
            proj(lambda c: kTm[:, c, :], lambda c: ("kTm", c), WA, "WA", memn, "memn", NMEM)
            wload(WAf[:, 0:4096].rearrange("p (k n) -> p k n", k=8), "WA", mlp_w1[li, :, 512:1024], 8, 512, nsplit=2)
            wload(WAf[:, 4096:8192].rearrange("p (k n) -> p k n", k=4), "WA", mlp_w2[li, 512:1024, :], 4, D, nsplit=2)
            pref.add((li, "mlp", 1))
            for a in range(2):
                for hf in range(2):
                    bank, bk = ps()
                    for k in range(KC):
                        S.add("pe", I("matmul", out=bank[:, :], lhsT=memn[:, k, a * 128:(a + 1) * 128],
                                                                                    rhs=WB[:, k, hf * 512:(hf + 1) * 512], start=(k == 0), stop=(k == KC - 1)),
                              reads=["WB", ("memn", k)], writes=[bk])
                    evac_copy(Vm[:, a, hf * 512:(hf + 1) * 512], bank[:, :], [bk], [("Vm", a)])
            wload(WB, "WB", xa_w_o[li], KC, D)
            SCX = 256 ** -0.5
            prr = 0
            prenorm(0, 0, 512, g0 + 2, hnb[0], ("hn", 0))
            for t in range(4):
                if t + 1 < 4:
                    prenorm((t + 1) * 512, t + 1, 512, g0 + 2, hnb[(t + 1) % 2], ("hn", (t + 1) % 2))
                hn, hk = hnb[t % 2], ("hn", t % 2)
                proj(lambda c: qx[:, c, :], lambda c: ("qx", c), WC, "WC", hn, hk, 512)
                for h in range(4):
                    pbs = []
                    for a in range(2):
                        bank, bk = ps()
                        for cc in range(2):
                            S.add("pe", I("matmul", out=bank[:, :], lhsT=kTm[:, 2 * h + cc, a * 128:(a + 1) * 128],
                                                                                        rhs=qx[:, 2 * h + cc, :], start=(cc == 0), stop=(cc == 1)),
                                  reads=[("kTm", 2 * h + cc), ("qx", 2 * h + cc)], writes=[bk])
                        pbuf = prr % 4
                        prr += 1
                        S.add("act", I("activation", out=pT[pbuf], in_=bank[:, :], func=AF.Exp, scale=SCX),
                              reads=[bk], writes=[("pT", pbuf)])
                        pbs.append(pbuf)
                    bankr, bkr = ps()
                    for a in range(2):
                        S.add("pe", I("matmul", out=bankr[:, :], lhsT=ones_bf, rhs=pT[pbs[a]], start=(a == 0), stop=(a == 1)),
                              reads=[("pT", pbs[a]), "consts"], writes=[bkr])
                    S.add("dve", I("reciprocal", out=rec, in_=bankr[:, :]), reads=[bkr], writes=["rec"])
                    for cc in range(2):
                        banko, bko = ps()
                        for a in range(2):
                            S.add("pe", I("matmul", out=banko[:, :], lhsT=Vm[:, a, (2 * h + cc) * 128:(2 * h + cc + 1) * 128],
                                                                                                    rhs=pT[pbs[a]], start=(a == 0), stop=(a == 1)),
                                  reads=[("Vm", a), ("pT", pbs[a])], writes=[bko])
                        S.add("dve", I("tensor_tensor", out=ao[:, 2 * h + cc, :], in0=banko[:, :], in1=rec, op=ALU.mult),
                              reads=[bko, "rec"], writes=[("ao", 2 * h + cc)])
                proj(lambda c: y[:, c, :], lambda c: ("y", c), WB, "WB", ao, "ao", 512)
                postnorm_residual(y, "y", t * 512, t, 512, g0 + 3)
            S.barrier()
            A.release(m0)

            S.epoch += 1
            m0 = A.mark()
            stats_bufs = {"sq": [A.bf16(512), A.bf16(512)], "ln": A.f32(512), "rstd": A.f32(512)}
            hnhb = [A.bf16(KC, 1024), A.bf16(KC, 1024)]
            yacc = A.f32(KC, 1024)
            aT = [A.bf16(4, 1024), A.bf16(4, 1024)]
            rl = [A.f32(512), A.f32(512)]
            WC = A.bf16(KC, D)
            WCf = WC.rearrange("p a b -> p (a b)")
            slots3 = [(WAf, "WA", True), (WBf, "WB", True), (WCf, "WC", False)]
            rlr = 0
            NG = 16

            def mlp_views(n):
                slotf_, skey_, nb_ = slots3[(n + 2) % 3]
                w1g_ = slotf_[:, 0:4096].rearrange("p (k n) -> p k n", k=8)
                w2g_ = slotf_[:, 4096:8192].rearrange("p (k n) -> p k n", k=4)
                return w1g_, w2g_, skey_, nb_

            def mlp_wload(n):
                gi_ = n % 8
                w1g_, w2g_, skey_, nb_ = mlp_views(n)
                wload(w1g_, skey_, mlp_w1[li, :, gi_ * 512:(gi_ + 1) * 512], 8, 512, nsplit=2, nb=nb_)
                wload(w2g_, skey_, mlp_w2[li, gi_ * 512:(gi_ + 1) * 512, :], 4, D, nsplit=2, nb=nb_)

            def mlp_prenorm(hf_):
                for tt in range(2):
                    t = hf_ * 2 + tt
                    c0 = t * 512
                    rstd = rms_stats(lambda k, c0=c0: xT[:, k, c0:c0 + 512], 512, lambda k, t=t: ("x", k, t), "pre")
                    for k in range(KC):
                        S.add("dve", I("scalar_tensor_tensor", out=hnhb[hf_][:, k, tt * 512:(tt + 1) * 512], in0=xT[:, k, c0:c0 + 512],
                                       scalar=gT[:, k, g0 + 4:g0 + 5], in1=rstd, op0=ALU.mult, op1=ALU.mult),
                              reads=[("x", k, t), "rstd", "consts"], writes=[("hnh", hf_, k, tt)])

            def mlp_up(n):
                nonlocal rlr
                hf_, gi_ = divmod(n, 8)
                ab = n % 2
                w1g_, w2g_, skey_, nb_ = mlp_views(n)
                hnh_ = hnhb[hf_]
                for tt in range(2):
                    for fc in range(4):
                        bank, bk = ps()
                        for k in range(KC):
                            S.add("pe", I("matmul", out=bank[:, :], lhsT=w1g_[:, k, fc * 128:(fc + 1) * 128],
                                          rhs=hnh_[:, k, tt * 512:(tt + 1) * 512], start=(k == 0), stop=(k == KC - 1)),
                                  reads=[skey_, ("hnh", hf_, k, tt)], writes=[bk])
                        ri = rlr % 2
                        rlr += 1
                        S.add("act", I("activation", out=rl[ri], in_=bank[:, :], func=AF.Relu), reads=[bk], writes=[("rl", ri)])
                        S.add("dve", I("tensor_tensor", out=aT[ab][:, fc, tt * 512:(tt + 1) * 512], in0=rl[ri], in1=rl[ri], op=ALU.mult),
                              reads=[("rl", ri)], writes=[("aT", ab, fc, tt)])

            def mlp_down(n):
                hf_, gi_ = divmod(n, 8)
                ab = n % 2
                w1g_, w2g_, skey_, nb_ = mlp_views(n)
                for tt in range(2):
                    for c in range(KC):
                        bank, bk = ps()
                        for fc in range(4):
                            S.add("pe", I("matmul", out=bank[:, :], lhsT=w2g_[:, fc, c * 128:(c + 1) * 128],
                                          rhs=aT[ab][:, fc, tt * 512:(tt + 1) * 512], start=(fc == 0), stop=(fc == 3)),
                                  reads=[skey_, ("aT", ab, fc, tt)], writes=[bk])
                        dst = yacc[:, c, tt * 512:(tt + 1) * 512]
                        if gi_ == 0:
                            evac_copy(dst, bank[:, :], [bk], [("yacc", c, tt)])
                        else:
                            S.add("dve", I("tensor_tensor", out=dst, in0=bank[:, :], in1=dst, op=ALU.add),
                                  reads=[bk], writes=[("yacc", c, tt)])
                if gi_ == 7:
                    for tt in range(2):
                        t = hf_ * 2 + tt
                        c0 = t * 512
                        yv = yacc[:, :, tt * 512:(tt + 1) * 512]
                        rstd = rms_stats(lambda k, yv=yv: yv[:, k, :], 512, lambda k, tt=tt: ("yacc", k, tt), "post")
                        for k in range(KC):
                            S.add("dve", I("scalar_tensor_tensor", out=yv[:, k, :], in0=yv[:, k, :], scalar=gT[:, k, g0 + 5:g0 + 6],
                                           in1=rstd, op0=ALU.mult, op1=ALU.mult),
                                  reads=["rstd", "consts"], writes=[("yacc", k, tt)])
                        for k in range(KC):
                            S.add("dve", I("tensor_tensor", out=xT[:, k, c0:c0 + 512], in0=xT[:, k, c0:c0 + 512], in1=yv[:, k, :], op=ALU.add),
                                  reads=[("yacc", k, tt)], writes=[("x", k, t)])

            if (li, "mlp", 1) not in pref:
                mlp_wload(1)
            mlp_wload(0)
            mlp_wload(2)
            mlp_prenorm(0)
            mlp_prenorm(1)
            mlp_up(0)
            for n in range(NG):
                if n + 1 < NG:
                    mlp_up(n + 1)
                mlp_down(n)
                if n + 3 < NG:
                    mlp_wload(n + 3)
                if nxt_li is not None and n == 13:
                    mixer_load(nxt_li, "WA")
                if nxt_li is not None and n == 14:
                    if nxt_li % 2 == 0:
                        xa_load(nxt_li, "WB")
                    else:
                        mixer_load(nxt_li, "WB")
            S.barrier()
            A.release(m0)


        ost = [A.f32(D), A.f32(D)]
        for tt in range(SEQ // 128):
            i = tt % 2
            for hh in range(2):
                bank, bk = ps()
                for kk in range(4):
                    k = hh * 4 + kk
                    S.add("pe", I("transpose", out=bank[:, kk * 128:(kk + 1) * 128], in_=xT[:, k, tt * 128:(tt + 1) * 128], identity=ident),
                          reads=[("x", k, tt // 4), "consts"], writes=[bk])
                evac_copy(ost[i][:, hh * 512:(hh + 1) * 512], bank[:, :], [bk], [("ost", i, hh)])
            S.add("sp", I("dma_start", out=out_d[tt * 128:(tt + 1) * 128, :], in_=ost[i]),
                  reads=[("ost", i, 0), ("ost", i, 1)], slot=f"o{i}")
        S.emit(nc, st, final_wait_slots=["o0", "o1"])
    return nc


def _consts():
    ident = np.eye(128, dtype=np.float32)
    rsw = np.zeros((128, 32), np.float32)
    for i in range(16):
        rsw[i + 16, i] = -1.0
        rsw[i, i + 16] = 1.0
    pos = np.arange(SEQ, dtype=np.float32)
    inv_freq = (np.float32(500000.0) ** (-np.arange(0, 32, 2, dtype=np.float32) / np.float32(32))).astype(np.float32)
    ang = (pos[:, None] * inv_freq[None, :]).astype(np.float32)
    cos = np.cos(ang).astype(np.float32).T
    sin = np.sin(ang).astype(np.float32).T
    cos32 = np.concatenate([cos, cos], axis=0)
    sin32 = np.concatenate([sin, sin], axis=0)
    kk = np.arange(128)[:, None]
    qq = np.arange(128)[None, :]
    tri = np.where(kk <= qq, 0.0, NEG).astype(np.float32)
    mown = np.zeros((2, 128, 256), np.float32)
    mown[0, :, 0:128] = tri
    mown[1, :, 0:128] = NEG
    mown[1, :, 128:256] = tri
    rc = np.broadcast_to((1.0 / np.arange(1, 17, dtype=np.float32))[None, :], (128, 16)).astype(np.float32).copy()
    return dict(c_ident=ident, c_rswap=rsw, c_cos=np.ascontiguousarray(cos32), c_sin=np.ascontiguousarray(sin32), c_mown=mown, c_rc=rc)


_PROG_CACHE = {}


def _get_prog(layers):
    key = tuple(layers)
    if key not in _PROG_CACHE:
        _PROG_CACHE[key] = build_program(list(layers))
    return _PROG_CACHE[key]


def _run(layers, x, mem, shared):
    nc = _get_prog(layers)
    in_maps = []
    for c in range(8):
        m = dict(shared)
        m["x"] = np.ascontiguousarray(x[c])
        m["mem"] = np.ascontiguousarray(mem[c])
        in_maps.append(m)
    res = run_bass_kernel_spmd(nc, in_maps, core_ids=list(range(8)))
    return np.stack([np.asarray(r["out"]) for r in res.results], axis=0)


LAUNCH_GROUPS = [[0, 1, 2, 3]]


def kernel(x, mem, norm_gains, mem_norm, pool_w_in, pool_w_group, pool_scale,
           moba_w_qkv, moba_w_o, xa_w_q, xa_w_kv, xa_w_o, mlp_w1, mlp_w2):
    f = lambda a: np.ascontiguousarray(np.asarray(a, dtype=np.float32))
    x = f(x)
    mem = f(mem)
    gains = np.concatenate([f(norm_gains).reshape(24, D), f(mem_norm).reshape(4, D), f(pool_scale).reshape(2, D)], axis=0)
    shared = dict(gains=np.ascontiguousarray(gains), pool_w_in=f(pool_w_in), pool_w_group=f(pool_w_group),
                  moba_w_qkv=f(moba_w_qkv), moba_w_o=f(moba_w_o), xa_w_q=f(xa_w_q), xa_w_kv=f(xa_w_kv),
                  xa_w_o=f(xa_w_o), mlp_w1=f(mlp_w1), mlp_w2=f(mlp_w2))
    shared.update(_consts())
    cur = x
    for grp in LAUNCH_GROUPS:
        cur = _run(grp, cur, mem, shared)
    return cur.astype(np.float32)
```
